# Optimizing a Trainium2 kernel written in Bass

```python
import math
import jax, jax.numpy as jnp
from jax import lax
import numpy as np

D_MODEL = 1024
BATCH = 8
SEQ = 8192
DEPTH = 1

FOX_HEADS = 8
FOX_HEAD_DIM = 64
FOX_WIDTH = FOX_HEADS * FOX_HEAD_DIM
DIFF_HEADS = 4
DIFF_QK_DIM = 64
DIFF_V_DIM = 2 * DIFF_QK_DIM
DIFF_WIDTH = DIFF_HEADS * DIFF_V_DIM
BRANCH_WIDTH = 512
N_BRANCHES = 2

BLOCK_Q = 128
ROPE_THETA = 10000.0
NORM_EPS = 1e-6
SUBLN_EPS = 1e-5

SPLIT_SIZES = [
    FOX_WIDTH,
    FOX_WIDTH,
    FOX_WIDTH,
    FOX_HEADS,
    FOX_WIDTH,
    DIFF_HEADS * 2 * DIFF_QK_DIM,
    DIFF_HEADS * 2 * DIFF_QK_DIM,
    DIFF_WIDTH,
    DIFF_WIDTH,
    N_BRANCHES * D_MODEL,
]
N_IN = sum(SPLIT_SIZES)
SPLIT_POINTS = [int(v) for v in np.cumsum(SPLIT_SIZES)[:-1]]

kernel_name = "hybrid_fox_diffattn_gated_merge"


def lambda_init_for(layer_idx):
    return 0.8 - 0.6 * math.exp(-0.3 * (layer_idx - 1))


def rms_norm(x, g, eps):
    xf = x.astype(jnp.float32)
    y = xf * lax.rsqrt(jnp.mean(xf * xf, axis=-1, keepdims=True) + eps)
    return (y * g.astype(jnp.float32)).astype(x.dtype)


def rope_tables(seq, dim):
    pos = jnp.arange(seq, dtype=jnp.float32)
    inv_freq = ROPE_THETA ** (-jnp.arange(0, dim, 2, dtype=jnp.float32) / dim)
    ang = pos[:, None] * inv_freq[None, :]
    return jnp.cos(ang), jnp.sin(ang)


def apply_rope(x, cos, sin):
    x1, x2 = jnp.split(x, 2, axis=-1)
    c = cos.astype(x.dtype)
    s = sin.astype(x.dtype)
    return jnp.concatenate([x1 * c - x2 * s, x2 * c + x1 * s], axis=-1)


def split_heads(t, n_heads):
    b, s, _ = t.shape
    return t.reshape(b, s, n_heads, -1).transpose(0, 2, 1, 3)


def merge_heads(t):
    b, h, s, d = t.shape
    return t.transpose(0, 2, 1, 3).reshape(b, s, h * d)


def fox_attention(q, k, v, log_f):
    b, h, s, d = q.shape
    nb = s // BLOCK_Q
    scale = 1.0 / math.sqrt(d)
    c = jnp.cumsum(log_f, axis=-1)
    qb = q.reshape(b, h, nb, BLOCK_Q, d).transpose(2, 0, 1, 3, 4)
    cb = c.reshape(b, h, nb, BLOCK_Q).transpose(2, 0, 1, 3)
    kpos = jnp.arange(s)

    def step(args):
        i, qi, ci = args
        qpos = i * BLOCK_Q + jnp.arange(BLOCK_Q)
        logits = jnp.einsum('bhqd,bhkd->bhqk', qi, k).astype(jnp.float32) * scale
        logits = logits + ci[..., :, None] - c[..., None, :]
        logits = jnp.where(kpos[None, :] <= qpos[:, None], logits, -jnp.inf)
        p = jax.nn.softmax(logits, axis=-1)
        return jnp.einsum('bhqk,bhkd->bhqd', p.astype(v.dtype), v)

    out = lax.map(step, (jnp.arange(nb), qb, cb))
    return out.transpose(1, 2, 0, 3, 4).reshape(b, h, s, d)


def diff_attention(q, k, v, lam):
    b, h, _, s, d = q.shape
    nb = s // BLOCK_Q
    scale = 1.0 / math.sqrt(d)
    qb = q.reshape(b, h, 2, nb, BLOCK_Q, d).transpose(3, 0, 1, 2, 4, 5)
    kpos = jnp.arange(s)

    def step(args):
        i, qi = args
        qpos = i * BLOCK_Q + jnp.arange(BLOCK_Q)
        logits = jnp.einsum('bhmqd,bhmkd->bhmqk', qi, k).astype(jnp.float32) * scale
        logits = jnp.where(kpos[None, :] <= qpos[:, None], logits, -jnp.inf)
        p = jax.nn.softmax(logits, axis=-1)
        pd = p[:, :, 0] - lam * p[:, :, 1]
        return jnp.einsum('bhqk,bhkd->bhqd', pd.astype(v.dtype), v)

    out = lax.map(step, (jnp.arange(nb), qb))
    return out.transpose(1, 2, 0, 3, 4).reshape(b, h, s, v.shape[-1])


def setup_inputs(seed: int = 0) -> dict:
    key = jax.random.key(seed)
    ks = jax.random.split(key, 13)
    f32 = jnp.float32
    x = jax.random.normal(ks[0], (BATCH, SEQ, D_MODEL), f32)
    g_pre = 1.0 + 0.1 * jax.random.normal(ks[1], (DEPTH, D_MODEL), f32)
    w_in = jax.random.normal(ks[2], (DEPTH, D_MODEL, N_IN), f32) * D_MODEL ** -0.5
    b_forget = jax.random.uniform(ks[3], (DEPTH, FOX_HEADS), f32, minval=1.0, maxval=5.0)
    lambda_q1 = 0.1 * jax.random.normal(ks[4], (DEPTH, DIFF_QK_DIM), f32)
    lambda_k1 = 0.1 * jax.random.normal(ks[5], (DEPTH, DIFF_QK_DIM), f32)
    lambda_q2 = 0.1 * jax.random.normal(ks[6], (DEPTH, DIFF_QK_DIM), f32)
    lambda_k2 = 0.1 * jax.random.normal(ks[7], (DEPTH, DIFF_QK_DIM), f32)
    g_subln = 1.0 + 0.1 * jax.random.normal(ks[8], (DEPTH, DIFF_V_DIM), f32)
    w_branch = jax.random.normal(ks[9], (DEPTH, N_BRANCHES, BRANCH_WIDTH, D_MODEL), f32) * BRANCH_WIDTH ** -0.5
    w_out = jax.random.normal(ks[10], (DEPTH, D_MODEL, D_MODEL), f32) * D_MODEL ** -0.5
    g_post = 1.0 + 0.1 * jax.random.normal(ks[11], (DEPTH, D_MODEL), f32)
    return {"x": x, "g_pre": g_pre, "w_in": w_in, "b_forget": b_forget,
            "lambda_q1": lambda_q1, "lambda_k1": lambda_k1,
            "lambda_q2": lambda_q2, "lambda_k2": lambda_k2,
            "g_subln": g_subln, "w_branch": w_branch, "w_out": w_out,
            "g_post": g_post}


def reference(x, g_pre, w_in, b_forget, lambda_q1, lambda_k1, lambda_q2, lambda_k2,
              g_subln, w_branch, w_out, g_post):
    b, s, _ = x.shape
    cos, sin = rope_tables(s, DIFF_QK_DIM)
    for l in range(DEPTH):
        lam_init = lambda_init_for(l + 1)
        h = rms_norm(x, g_pre[l], NORM_EPS)
        proj = jnp.einsum('bsd,dn->bsn', h, w_in[l])
        qa, ka, va, fa, za, qb, kb, vb, zb, gates = jnp.split(proj, SPLIT_POINTS, axis=-1)

        log_f = jax.nn.log_sigmoid((fa + b_forget[l]).astype(jnp.float32)).transpose(0, 2, 1)
        ya = fox_attention(split_heads(qa, FOX_HEADS), split_heads(ka, FOX_HEADS),
                           split_heads(va, FOX_HEADS), log_f)
        ya = merge_heads(ya) * jax.nn.silu(za)

        qd = apply_rope(qb.reshape(b, s, DIFF_HEADS, 2, DIFF_QK_DIM).transpose(0, 2, 3, 1, 4), cos, sin)
        kd = apply_rope(kb.reshape(b, s, DIFF_HEADS, 2, DIFF_QK_DIM).transpose(0, 2, 3, 1, 4), cos, sin)
        vd = split_heads(vb, DIFF_HEADS)
        lam = (jnp.exp(jnp.sum(lambda_q1[l].astype(jnp.float32) * lambda_k1[l].astype(jnp.float32)))
               - jnp.exp(jnp.sum(lambda_q2[l].astype(jnp.float32) * lambda_k2[l].astype(jnp.float32)))
               + lam_init)
        yb = diff_attention(qd, kd, vd, lam)
        yb = rms_norm(yb, g_subln[l], SUBLN_EPS) * (1.0 - lam_init)
        yb = merge_heads(yb) * jax.nn.silu(zb)

        gate_a, gate_b = jnp.split(jax.nn.sigmoid(gates), N_BRANCHES, axis=-1)
        merged = (gate_a * jnp.einsum('bsw,wd->bsd', ya, w_branch[l, 0])
                  + gate_b * jnp.einsum('bsw,wd->bsd', yb, w_branch[l, 1]))
        y = jnp.einsum('bsd,de->bse', merged, w_out[l])
        x = x + rms_norm(y, g_post[l], NORM_EPS)
    return x
```

```python
import math
from contextlib import ExitStack

import numpy as np
import ml_dtypes
import concourse.bass as bass
import concourse.mybir as mybir
from concourse.bass_utils import run_bass_kernel_spmd

F32 = mybir.dt.float32
BF16 = mybir.dt.bfloat16
ALU = mybir.AluOpType
AF = mybir.ActivationFunctionType

D = 1024
NFM = 48
COL_VA = NFM * 128
COL_VB = COL_VA + 512
COL_FA = COL_VB + 512
NW = COL_FA + 8
WCH = 598
NPRM = 274 + 1024
LAM_INIT = 0.8 - 0.6 * math.exp(-0.3 * 0.0)
NORM_EPS = 1e-6
SUBLN_EPS = 1e-5


class Ev:
    __slots__ = ("ch", "val")

    def __init__(self, ch, val):
        self.ch = ch
        self.val = val


class Chan:
    def __init__(self, sem, step=1):
        self.sem = sem
        self.step = step
        self.val = 0

    def inc(self, ins):
        self.val += self.step
        ins.then_inc(self.sem, self.step)
        return Ev(self, self.val)


class Emitter:
    def __init__(self, nc):
        self.nc = nc
        self.waited = {}

    def wait(self, eng, *evs):
        for ev in evs:
            if ev is None:
                continue
            if isinstance(ev, (list, tuple)):
                self.wait(eng, *ev)
                continue
            key = (id(eng), ev.ch)
            if self.waited.get(key, 0) >= ev.val:
                continue
            eng.wait_ge(ev.ch.sem, ev.val)
            self.waited[key] = ev.val


def build(S):
    NT = S // 128
    NG = S // 512
    NF = 8 * NT
    NCH = max(1, NF // 128)
    CHP = min(128, NF)

    nc = bass.Bass("TRN2", target_bir_lowering=False)
    x = nc.dram_tensor("x", [S, D], F32, kind="ExternalInput").ap()
    w2 = nc.dram_tensor("w2", [128, 8, NW], F32, kind="ExternalInput").ap()
    wbr = nc.dram_tensor("wbr", [128, 8, D], F32, kind="ExternalInput").ap()
    wout = nc.dram_tensor("wout", [128, 8, D], F32, kind="ExternalInput").ap()
    prm_d = nc.dram_tensor("prm", [128, NPRM], F32, kind="ExternalInput").ap()
    cb_d = nc.dram_tensor("cb", [128, 2304], BF16, kind="ExternalInput").ap()
    cf_d = nc.dram_tensor("cf", [128, 384], F32, kind="ExternalInput").ap()
    ropec_d = nc.dram_tensor("ropec", [128, S], F32, kind="ExternalInput").ap()
    ropes_d = nc.dram_tensor("ropes", [128, S], F32, kind="ExternalInput").ap()
    out_d = nc.dram_tensor("out", [S, D], F32, kind="ExternalOutput").ap()

    QF = nc.dram_tensor("QF", [512, S], BF16, kind="Internal").ap()
    KF = nc.dram_tensor("KF", [512, S], BF16, kind="Internal").ap()
    VF = nc.dram_tensor("VF", [S, 512], BF16, kind="Internal").ap()
    VD = nc.dram_tensor("VD", [S, 512], BF16, kind="Internal").ap()
    ZT = nc.dram_tensor("ZT", [1024, S], BF16, kind="Internal").ap()
    GT = nc.dram_tensor("GT", [2048, S], BF16, kind="Internal").ap()
    QD = nc.dram_tensor("QD", [512, S], BF16, kind="Internal").ap()
    KD = nc.dram_tensor("KD", [512, S], BF16, kind="Internal").ap()
    YT = nc.dram_tensor("YT", [1024, S], BF16, kind="Internal").ap()
    AUG = nc.dram_tensor("AUG", [4, 8 * S], BF16, kind="Internal").ap()

    em = Emitter(nc)
    W = em.wait
    PE, ACT, DVE, POOL, SP = nc.tensor, nc.scalar, nc.vector, nc.gpsimd, nc.sync

    with ExitStack() as g:
        g.enter_context(nc.allow_low_precision("bf16 matmul operands, fp32 accumulation"))
        g.enter_context(nc.allow_non_contiguous_dma("small strided V loads"))
        PRM = g.enter_context(nc.sbuf_tensor("PRM", [128, 274], F32))
        CB = g.enter_context(nc.sbuf_tensor("CB", [128, 2304], BF16))
        CF = g.enter_context(nc.sbuf_tensor("CF", [128, 384], F32))
        FA = g.enter_context(nc.sbuf_tensor("FA", [128, 8, NT], F32))
        SM = g.enter_context(nc.sbuf_tensor("SM", [128, 16], F32))
        identb = CB[:, 0:128]
        onesb = CB[:, 128:256]
        NEGLAM = SM[:, 0:1]
        GSUBC = SM[:, 1:2]
        EPS6 = SM[:, 2:3]
        EPS5 = SM[:, 3:4]
        ONE = SM[:, 4:5]

        def mask(j):
            return CB[:, 256 + j * 512:256 + (j + 1) * 512]

        with ExitStack() as es:
            sbt = lambda n, s, d: es.enter_context(nc.sbuf_tensor(n, s, d))
            sem = lambda n: es.enter_context(nc.semaphore(n))
            Wb = sbt("Wb", [128, 8, NW], BF16)
            wst = [sbt(f"wst{i}", [128, WCH], F32) for i in range(2)]
            xt = [sbt(f"xt{i}", [128, D], F32) for i in range(2)]
            hb = [sbt(f"hb{i}", [128, D], BF16) for i in range(2)]
            hT = [sbt(f"hT{i}", [128, 8, 512], BF16) for i in range(2)]
            rc = [sbt(f"rc{i}", [128, 512], F32) for i in range(2)]
            rs = [sbt(f"rs{i}", [128, 512], F32) for i in range(2)]
            NS = 6
            stage = [sbt(f"stg{i}", [128, 512], BF16) for i in range(NS)]
            tmp1 = sbt("tmp1", [128, 512], F32)
            tmp2 = sbt("tmp2", [128, 512], F32)
            junk = sbt("junk", [128, D], BF16)
            ss = sbt("ss", [128, NT], F32)
            lnv = sbt("lnv", [128, NT], F32)
            rstd = sbt("rstd", [128, NT], F32)
            lt = sbt("lt", [128, 64], F32)
            psm = es.enter_context(nc.psum_tensor("psm", [128, 4, 512], F32))
            psT = es.enter_context(nc.psum_tensor("psT", [128, 2, 8, 128], BF16))
            psf = es.enter_context(nc.psum_tensor("psf", [128, 512], F32))
            cPE, cACT, cDVE, cPOOL = (Chan(sem(n)) for n in ("a_pe", "a_act", "a_dve", "a_pool"))
            cLD0 = Chan(sem("a_ld0"), 16)
            cWL = [Chan(sem(f"a_wl{i}"), 16) for i in range(2)]
            cXL = [Chan(sem(f"a_xl{i}"), 16) for i in range(2)]
            cRL = [Chan(sem(f"a_rl{i}"), 16) for i in range(2)]
            cST = [Chan(sem(f"a_st{i}"), 16) for i in range(NS)]
            block = es.enter_context(nc.Block())

            @block.sync
            def _(_sync):
                SP.dma_start(out=PRM[:, :], in_=prm_d[:, 0:274]).then_inc(cLD0.sem, 16)
                SP.dma_start(out=CB[:, :], in_=cb_d[:, :]).then_inc(cLD0.sem, 16)
                ins = SP.dma_start(out=CF[:, :], in_=cf_d[:, :])
                cLD0.val = 32
                ev0 = cLD0.inc(ins)
                W(DVE, ev0)
                DVE.memset(SM[:, 2:3], NORM_EPS)
                DVE.memset(SM[:, 3:4], SUBLN_EPS)
                DVE.memset(SM[:, 4:5], 1.0)
                W(DVE, cDVE.inc(DVE.tensor_tensor(out=lt[:, :], in0=PRM[:, 16:80], in1=PRM[:, 80:144], op=ALU.mult)))
                ins = DVE.tensor_reduce(out=SM[:, 5:6], in_=lt[:, :], axis=mybir.AxisListType.X, op=ALU.add)
                e = cDVE.inc(ins)
                W(DVE, e)
                W(DVE, cDVE.inc(DVE.tensor_tensor(out=lt[:, :], in0=PRM[:, 144:208], in1=PRM[:, 208:272], op=ALU.mult)))
                ins = DVE.tensor_reduce(out=SM[:, 6:7], in_=lt[:, :], axis=mybir.AxisListType.X, op=ALU.add)
                e = cDVE.inc(ins)
                W(ACT, e)
                ins = ACT.activation(out=SM[:, 7:9], in_=SM[:, 5:7], func=AF.Exp)
                e = cACT.inc(ins)
                W(DVE, e)
                ins = DVE.tensor_tensor(out=SM[:, 9:10], in0=SM[:, 8:9], in1=SM[:, 7:8], op=ALU.subtract)
                e = cDVE.inc(ins)
                W(DVE, e)
                DVE.tensor_scalar(out=SM[:, 0:1], in0=SM[:, 9:10], scalar1=-LAM_INIT, scalar2=None, op0=ALU.add)
                ins = DVE.tensor_scalar(out=SM[:, 1:2], in0=PRM[:, 272:273], scalar1=1.0 - LAM_INIT, scalar2=None,
                                        op0=ALU.mult)
                ev_sm = cDVE.inc(ins)
                W(ACT, ev_sm)
                W(POOL, ev0)

                cast_ev = [None, None]
                last_cast = {}
                k = 0
                for c in range(8):
                    for q in range(NW // WCH):
                        sl = k % 2
                        W(SP, cast_ev[sl])
                        ev = cWL[sl].inc(SP.dma_start(out=wst[sl][:, :], in_=w2[:, c, q * WCH:(q + 1) * WCH]))
                        eng, ch = (DVE, cDVE) if k % 2 == 0 else (POOL, cPOOL)
                        W(eng, ev)
                        ins = eng.tensor_scalar(out=Wb[:, c, q * WCH:(q + 1) * WCH], in0=wst[sl][:, :],
                                                scalar1=PRM[:, c:c + 1], scalar2=None, op0=ALU.mult)
                        cast_ev[sl] = ch.inc(ins)
                        last_cast[k % 2] = cast_ev[sl]
                        k += 1
                W(PE, last_cast[0], last_cast[1], ev0)

                xld_ev = {}
                sq_ev = {}
                rstd_ev = {}
                x_free = [None, None]
                tr_ev = {}
                trc_ev = {}
                grp_mm_ev = {}
                rl_ev = {}
                rope_done = {}

                def tile_load(T):
                    sl = T % 2
                    W(SP, x_free[sl])
                    xld_ev[T] = cXL[sl].inc(SP.dma_start(out=xt[sl][:, :], in_=x[T * 128:(T + 1) * 128, :]))
                    W(ACT, xld_ev[T], sq_ev.get(T - 1))
                    ins = ACT.activation(out=junk[:, :], in_=xt[sl][:, :], func=AF.Square, accum_out=ss[:, T:T + 1])
                    sq_ev[T] = cACT.inc(ins)

                def tile_stats(T):
                    W(ACT, sq_ev[T])
                    ins = ACT.activation(out=lnv[:, T:T + 1], in_=ss[:, T:T + 1], func=AF.Ln, scale=1.0 / D,
                                         bias=EPS6)
                    e1 = cACT.inc(ins)
                    W(ACT, e1)
                    ins = ACT.activation(out=rstd[:, T:T + 1], in_=lnv[:, T:T + 1], func=AF.Exp, scale=-0.5)
                    rstd_ev[T] = cACT.inc(ins)

                def tile_h(T):
                    G, i = divmod(T, 4)
                    sl = T % 2
                    W(DVE, rstd_ev[T], xld_ev[T], tr_ev.get(T - 2))
                    ins = DVE.tensor_scalar(out=hb[sl][:, :], in0=xt[sl][:, :], scalar1=rstd[:, T:T + 1],
                                            scalar2=None, op0=ALU.mult)
                    h_ev = cDVE.inc(ins)
                    x_free[sl] = [h_ev, sq_ev[T]]
                    W(PE, h_ev, trc_ev.get(T - 2))
                    for c in range(8):
                        ins = PE.transpose(out=psT[:, sl, c, :], in_=hb[sl][:, c * 128:(c + 1) * 128],
                                           identity=identb)
                    tr_ev[T] = cPE.inc(ins)
                    W(DVE, tr_ev[T], grp_mm_ev.get(G - 2))
                    ins = DVE.tensor_copy(out=hT[G % 2][:, :, i * 128:(i + 1) * 128], in_=psT[:, sl, :, :])
                    trc_ev[T] = cDVE.inc(ins)

                def rope_load(G):
                    sl = G % 2
                    W(SP, rope_done.get(G - 2))
                    cRL[sl].inc(SP.dma_start(out=rc[sl][:, :], in_=ropec_d[:, G * 512:(G + 1) * 512]))
                    rl_ev[G] = cRL[sl].inc(SP.dma_start(out=rs[sl][:, :], in_=ropes_d[:, G * 512:(G + 1) * 512]))

                state = {"pc": 0, "sc": 0}
                bank_free = [None] * 4
                stage_free = [None] * NS
                tmp_free = [None]
                fa_free = [None]

                def mm_fm(G, blk, bank):
                    W(PE, bank_free[bank])
                    for c in range(8):
                        ins = PE.matmul(psm[:, bank, :], lhsT=Wb[:, c, blk * 128:(blk + 1) * 128],
                                        rhs=hT[G % 2][:, c, :], start=(c == 0), stop=(c == 7))
                    return cPE.inc(ins)

                def mm_tm(G, i, col, bank):
                    W(PE, bank_free[bank])
                    for c in range(8):
                        ins = PE.matmul(psm[:, bank, :], lhsT=hT[G % 2][:, c, i * 128:(i + 1) * 128],
                                        rhs=Wb[:, c, col:col + 512], start=(c == 0), stop=(c == 7))
                    return cPE.inc(ins)

                def store(slot, ev, dst):
                    W(SP, ev)
                    stage_free[slot] = cST[slot].inc(SP.dma_start(out=dst, in_=stage[slot][:, :]))

                def nbank():
                    b = state["pc"] % 4
                    state["pc"] += 1
                    return b

                def nslot():
                    s = state["sc"] % NS
                    state["sc"] += 1
                    return s

                def group_mm(G):
                    W(PE, trc_ev[G * 4 + 3])
                    gs = slice(G * 512, (G + 1) * 512)
                    last_pe = None
                    fm = []
                    for b in range(4):
                        fm.append((b, "copy", QF[b * 128:(b + 1) * 128, gs]))
                    for b in range(4):
                        fm.append((4 + b, "copy", KF[b * 128:(b + 1) * 128, gs]))
                    for b in range(8):
                        fm.append((8 + b, "silu", ZT[b * 128:(b + 1) * 128, gs]))
                    for b in range(16):
                        fm.append((16 + b, "sigm", GT[b * 128:(b + 1) * 128, gs]))
                    for (blk, kind, dst) in fm:
                        bank = nbank()
                        mev = mm_fm(G, blk, bank)
                        slot = nslot()
                        if kind == "copy":
                            W(DVE, mev, stage_free[slot])
                            ins = DVE.tensor_copy(out=stage[slot][:, :], in_=psm[:, bank, :])
                            eev = cDVE.inc(ins)
                        else:
                            W(ACT, mev, stage_free[slot])
                            ins = ACT.activation(out=stage[slot][:, :], in_=psm[:, bank, :],
                                                 func=AF.Silu if kind == "silu" else AF.Sigmoid)
                            eev = cACT.inc(ins)
                        bank_free[bank] = eev
                        store(slot, eev, dst)
                    for b in range(8):
                        dst = (QD if b < 4 else KD)[(b % 4) * 128:(b % 4 + 1) * 128, gs]
                        bankA = nbank()
                        mevA = mm_fm(G, 32 + b, bankA)
                        bankB = nbank()
                        mevB = mm_fm(G, 40 + b, bankB)
                        slot = nslot()
                        W(DVE, mevA, rl_ev[G], tmp_free[0])
                        ins = DVE.tensor_tensor(out=tmp1[:, :], in0=psm[:, bankA, :], in1=rc[G % 2][:, :], op=ALU.mult)
                        bank_free[bankA] = cDVE.inc(ins)
                        W(DVE, mevB)
                        ins = DVE.tensor_tensor(out=tmp2[:, :], in0=psm[:, bankB, :], in1=rs[G % 2][:, :], op=ALU.mult)
                        e2 = cDVE.inc(ins)
                        bank_free[bankB] = e2
                        W(POOL, e2, stage_free[slot])
                        ins = POOL.tensor_tensor(out=stage[slot][:, :], in0=tmp1[:, :], in1=tmp2[:, :], op=ALU.add)
                        e3 = cPOOL.inc(ins)
                        tmp_free[0] = e3
                        store(slot, e3, dst)
                    rope_done[G] = e2
                    W(PE, fa_free[0])
                    for i in range(4):
                        T = G * 4 + i
                        for (col, dstT) in ((COL_VA, VF), (COL_VB, VD)):
                            bank = nbank()
                            mev = mm_tm(G, i, col, bank)
                            slot = nslot()
                            W(ACT, mev, stage_free[slot])
                            ins = ACT.activation(out=stage[slot][:, :], in_=psm[:, bank, :], func=AF.Copy)
                            eev = cACT.inc(ins)
                            bank_free[bank] = eev
                            store(slot, eev, dstT[T * 128:(T + 1) * 128, :])
                        for c in range(8):
                            ins = PE.matmul(psf[:, i * 8:(i + 1) * 8], lhsT=hT[G % 2][:, c, i * 128:(i + 1) * 128],
                                            rhs=Wb[:, c, COL_FA:COL_FA + 8], start=(c == 0), stop=(c == 7))
                        last_pe = cPE.inc(ins)
                    grp_mm_ev[G] = last_pe
                    W(DVE, last_pe)
                    for i in range(4):
                        ins = DVE.tensor_copy(out=FA[:, :, G * 4 + i], in_=psf[:, i * 8:(i + 1) * 8])
                    fa_free[0] = cDVE.inc(ins)

                def prep_group(G):
                    rope_load(G)
                    for i in range(4):
                        T = G * 4 + i
                        tile_load(T)
                        tile_stats(T)
                        tile_h(T)

                prep_group(0)
                for G in range(NG):
                    if G + 1 < NG:
                        prep_group(G + 1)
                    group_mm(G)
                W(SP, *stage_free)
                W(SP, fa_free[0])

        with ExitStack() as es:
            sbt = lambda n, s, d: es.enter_context(nc.sbuf_tensor(n, s, d))
            sem = lambda n: es.enter_context(nc.semaphore(n))
            E1 = sbt("E1", [128, NF], F32)
            LS = sbt("LS", [128, NF], F32)
            TOT = sbt("TOT", [128, 8, NT], F32)
            INC = sbt("INC", [128, 8, NT], F32)
            N1 = sbt("N1", [128, NF], F32)
            N8 = sbt("N8", [128, NF], F32)
            R1 = sbt("R1", [128, NF], F32)
            R2 = sbt("R2", [128, NF], F32)
            AUG4 = sbt("AUG4", [128, 4, NF], BF16)
            AUGT = sbt("AUGT", [128, 4 * NCH, 128], BF16)
            ps = es.enter_context(nc.psum_tensor("ps2", [128, 2, 512], F32))
            psA = es.enter_context(nc.psum_tensor("psA", [128, 4 * NCH, 128], BF16))
            cPE, cACT, cDVE = (Chan(sem(n)) for n in ("b_pe", "b_act", "b_dve"))
            cST = Chan(sem("b_st"), 16)
            block = es.enter_context(nc.Block())

            @block.sync
            def _(_sync):
                FAf = FA[:, :, :].rearrange("p h t -> p (h t)")
                for h in range(8):
                    ins = DVE.tensor_scalar(out=FA[:, h, :], in0=FA[:, h, :], scalar1=PRM[:, 8 + h:9 + h],
                                            scalar2=None, op0=ALU.add)
                e = cDVE.inc(ins)
                W(ACT, e)
                e = cACT.inc(ACT.activation(out=E1[:, :], in_=FAf, func=AF.Exp, scale=-1.0))
                W(ACT, e)
                e = cACT.inc(ACT.activation(out=LS[:, :], in_=E1[:, :], func=AF.Ln, bias=ONE, scale=1.0))
                W(PE, e)
                PE.matmul(ps[:, 0, 0:NF], lhsT=CF[:, 0:128], rhs=LS[:, :], start=True, stop=True)
                e = cPE.inc(PE.matmul(ps[:, 1, 0:NF], lhsT=CF[:, 128:256], rhs=LS[:, :], start=True, stop=True))
                W(DVE, e)
                TOTf = TOT[:, :, :].rearrange("p h t -> p (h t)")
                INCf = INC[:, :, :].rearrange("p h t -> p (h t)")
                e = cDVE.inc(DVE.tensor_copy(out=TOTf, in_=ps[:, 1, 0:NF]))
                W(DVE, e)
                for h in range(8):
                    ins = DVE.tensor_tensor_scan(out=INC[:, h, :], data0=CF[:, 128:128 + NT], data1=TOT[:, h, :],
                                                 initial=0.0, op0=ALU.mult, op1=ALU.add)
                e = cDVE.inc(ins)
                W(DVE, e)
                e = cDVE.inc(DVE.tensor_tensor(out=N1[:, :], in0=ps[:, 0, 0:NF], in1=INCf, op=ALU.add))
                W(DVE, e)
                e = cDVE.inc(DVE.tensor_tensor(out=N8[:, :], in0=N1[:, :], in1=TOTf, op=ALU.subtract))
                W(DVE, e)
                e = cDVE.inc(DVE.tensor_scalar(out=N8[:, :], in0=N8[:, :], scalar1=8.0, scalar2=None, op0=ALU.mult))
                W(DVE, e)
                e = cDVE.inc(DVE.tensor_copy(out=AUG4[:, 1, :], in_=N8[:, :]))
                W(DVE, e)
                e = cDVE.inc(DVE.tensor_tensor(out=R1[:, :], in0=N8[:, :], in1=AUG4[:, 1, :], op=ALU.subtract))
                W(DVE, e)
                e = cDVE.inc(DVE.tensor_copy(out=AUG4[:, 2, :], in_=R1[:, :]))
                W(DVE, e)
                e = cDVE.inc(DVE.tensor_tensor(out=R2[:, :], in0=R1[:, :], in1=AUG4[:, 2, :], op=ALU.subtract))
                W(DVE, e)
                DVE.tensor_copy(out=AUG4[:, 3, :], in_=R2[:, :])
                e = cDVE.inc(DVE.tensor_scalar(out=AUG4[:, 0, :], in0=AUG4[:, 1, :], scalar1=-1.0, scalar2=None,
                                               op0=ALU.mult))
                W(PE, e)
                for r in range(4):
                    for i in range(NCH):
                        ins = PE.transpose(out=psA[0:CHP, r * NCH + i, :], in_=AUG4[:, r, i * 128:i * 128 + CHP],
                                           identity=identb)
                e = cPE.inc(ins)
                W(DVE, e)
                e = cDVE.inc(DVE.tensor_copy(out=AUGT[0:CHP, :, :], in_=psA[0:CHP, :, :]))
                W(SP, e)
                for r in range(4):
                    for i in range(NCH):
                        dst = AUG[r:r + 1, i * 128 * 128:i * 128 * 128 + CHP * 128].rearrange(
                            "o (p t) -> (o p) t", t=128)
                        ev_st = cST.inc(SP.dma_start(out=dst, in_=AUGT[0:CHP, r * NCH + i, :]))
                W(SP, ev_st)

        def attention(kind):
            fox = kind == "fox"
            NH = 8 if fox else 4
            BW = 3 if fox else 2
            NPB = 4
            KR = 68 if fox else 64
            with ExitStack() as es:
                sbt = lambda n, s, d: es.enter_context(nc.sbuf_tensor(kind + "_" + n, s, d))
                sem = lambda n: es.enter_context(nc.semaphore(kind + "_" + n))
                QT = [sbt(f"QT{i}", [128, S], BF16) for i in range(2)]
                KT = [sbt(f"KT{i}", [128, S], BF16) for i in range(2)]
                V = [sbt(f"V{i}", [128, NT, 128], BF16) for i in range(2)]
                Z = [sbt(f"Z{i}", [128, S], BF16) for i in range(2)]
                P = [sbt(f"P{i}", [128, BW, 512], BF16) for i in range(NPB)]
                Rt = sbt("Rt", [128, 512], F32)
                Tt = [sbt(f"Tt{i}", [128, 512], F32) for i in range(2)]
                NYS = 2
                Yst = [sbt(f"Yst{i}", [128, 512], BF16) for i in range(NYS)]
                if not fox:
                    Bt = sbt("Bt", [128, 512], F32)
                    Of = sbt("Of", [128, 512], F32)
                    SQ = sbt("SQ", [128, 512], BF16)
                    LNV = sbt("LNV", [128, 512], F32)
                    RS = sbt("RS", [128, 512], F32)
                    Y1 = sbt("Y1", [128, 512], F32)
                ps = es.enter_context(nc.psum_tensor(kind + "_psat", [128, 8, 512], F32))
                cPE, cACT, cDVE, cPOOL = (Chan(sem(n)) for n in ("c_pe", "c_act", "c_dve", "c_pool"))
                cLD = [Chan(sem(f"c_ld{i}"), 16) for i in range(2)]
                cST = [Chan(sem(f"c_st{i}"), 16) for i in range(NYS)]
                block = es.enter_context(nc.Block())

                @block.sync
                def _(_sync):
                    init_ev = None
                    if fox:
                        for b in range(2):
                            DVE.memset(QT[b][64:68, :], 1.0)
                            DVE.memset(KT[b][64:68, :], 1.0)
                            ins = POOL.memset(V[b][:, :, 64:128], 1.0)
                        init_ev = [cDVE.inc(DVE.memset(Rt[:, :], 1.0)), cPOOL.inc(ins)]
                    head_free = [None, None]
                    ld_ev = {}

                    def load_head(h):
                        b = h % 2
                        W(SP, head_free[b], init_ev)
                        c = cLD[b]
                        if fox:
                            c.inc(SP.dma_start(out=QT[b][0:64, :], in_=QF[h * 64:(h + 1) * 64, :]))
                            c.inc(SP.dma_start(out=QT[b][64:65, :], in_=AUG[0:1, h * S:(h + 1) * S]))
                            c.inc(SP.dma_start(out=KT[b][0:64, :], in_=KF[h * 64:(h + 1) * 64, :]))
                            c.inc(SP.dma_start(out=KT[b][65:68, :], in_=AUG[1:4, h * S:(h + 1) * S]))
                            vsrc = VF.rearrange("(n p) c -> p n c", p=128)
                            nsp = 4 if NT >= 4 else 1
                            for q in range(nsp):
                                a0, a1 = q * NT // nsp, (q + 1) * NT // nsp
                                c.inc(SP.dma_start(out=V[b][:, a0:a1, 0:64], in_=vsrc[:, a0:a1, h * 64:(h + 1) * 64]))
                            ld_ev[h] = c.inc(SP.dma_start(out=Z[b][0:64, :], in_=ZT[h * 64:(h + 1) * 64, :]))
                        else:
                            c.inc(SP.dma_start(out=QT[b][:, :], in_=QD[h * 128:(h + 1) * 128, :]))
                            c.inc(SP.dma_start(out=KT[b][:, :], in_=KD[h * 128:(h + 1) * 128, :]))
                            vsrc = VD.rearrange("(n p) c -> p n c", p=128)
                            nsp = 4 if NT >= 4 else 1
                            for q in range(nsp):
                                a0, a1 = q * NT // nsp, (q + 1) * NT // nsp
                                c.inc(SP.dma_start(out=V[b][:, a0:a1, :], in_=vsrc[:, a0:a1, h * 128:(h + 1) * 128]))
                            ld_ev[h] = c.inc(SP.dma_start(out=Z[b][:, :],
                                                          in_=ZT[512 + h * 128:512 + (h + 1) * 128, :]))

                    units = []
                    batches = []
                    for h in range(NH):
                        for g in range(NG):
                            for m in range(1 if fox else 2):
                                u = len(units)
                                nk = 4 * (g + 1)
                                units.append(dict(h=h, g=g, m=m, nk=nk))
                                kts = list(range(nk))
                                for s0 in range(0, nk, BW):
                                    batches.append(dict(u=u, kts=kts[s0:s0 + BW], first=(s0 == 0),
                                                        last=(s0 + BW >= nk)))
                    NB = len(batches)
                    qk_ev = {}
                    exp_ev = {}
                    mask_ev = {}
                    pv_ev = {}
                    acc_free = {}
                    deferred = {}
                    st_free = [None] * NYS
                    head_last_pe = {}
                    head_last_epi = {}
                    misc = {"ys": 0, "of_free": None, "sub_free": None, "tt_free": [None, None], "loaded": -1}

                    def rows(m):
                        return slice(0, KR) if (fox or m == 0) else slice(64, 128)

                    def acc_banks(u):
                        if fox:
                            return (6 + u % 2,)
                        return (4, 5)

                    def emit_qk(n):
                        bt = batches[n]
                        un = units[bt["u"]]
                        h, g, m = un["h"], un["g"], un["m"]
                        b = h % 2
                        W(PE, ld_ev[h], exp_ev.get(n - 2))
                        sbase = (n % 2) * BW
                        r = rows(m)
                        for j, kt in enumerate(bt["kts"]):
                            band = kt >= 4 * g
                            ins = PE.matmul(ps[:, sbase + j, :], lhsT=KT[b][r, kt * 128:(kt + 1) * 128],
                                            rhs=QT[b][r, g * 512:(g + 1) * 512], start=True, stop=not band)
                            if band:
                                ins = PE.matmul(ps[:, sbase + j, :], lhsT=identb, rhs=mask(kt - 4 * g),
                                                start=False, stop=True)
                        qk_ev[n] = cPE.inc(ins)

                    def emit_exp(n):
                        bt = batches[n]
                        nb = len(bt["kts"])
                        sbase = (n % 2) * BW
                        W(ACT, qk_ev[n], pv_ev.get(n - NPB))
                        ins = ACT.activation(out=P[n % NPB][:, 0:nb, :], in_=ps[:, sbase:sbase + nb, :], func=AF.Exp,
                                             scale=0.125)
                        exp_ev[n] = cACT.inc(ins)

                    def emit_mask(n):
                        bt = batches[n]
                        un = units[bt["u"]]
                        g = un["g"]
                        ins = None
                        for j, kt in enumerate(bt["kts"]):
                            if kt >= 4 * g:
                                W(POOL, exp_ev[n])
                                ins = POOL.tensor_tensor(out=P[n % NPB][:, j, :], in0=P[n % NPB][:, j, :],
                                                         in1=mask(kt - 4 * g), op=ALU.mult)
                        if ins is not None:
                            mask_ev[n] = cPOOL.inc(ins)

                    def emit_pv(n):
                        bt = batches[n]
                        u = bt["u"]
                        un = units[u]
                        h, nk = un["h"], un["nk"]
                        b = h % 2
                        banks = acc_banks(u)
                        W(PE, exp_ev[n], mask_ev.get(n))
                        if bt["first"]:
                            W(PE, acc_free.get(banks[0]))
                        for j, kt in enumerate(bt["kts"]):
                            ins = PE.matmul(ps[:, banks[0], :], lhsT=V[b][:, kt, :], rhs=P[n % NPB][:, j, :],
                                            start=(kt == 0), stop=(kt == nk - 1))
                            if not fox:
                                ins = PE.matmul(ps[:, banks[1], :], lhsT=onesb, rhs=P[n % NPB][:, j, :],
                                                start=(kt == 0), stop=(kt == nk - 1))
                        pv_ev[n] = cPE.inc(ins)
                        head_last_pe[h] = pv_ev[n]

                    def ystore(h, g, e, nrow, row0):
                        ys = misc["ys"] % NYS
                        return ys

                    def epilogue_fox(n):
                        u = batches[n]["u"]
                        un = units[u]
                        h, g = un["h"], un["g"]
                        b = h % 2
                        a = acc_banks(u)[0]
                        tt = u % 2
                        W(DVE, pv_ev[n], misc.get("dl"))
                        e1 = cDVE.inc(DVE.reciprocal(out=Rt[64:128, :], in_=ps[64:128, a, :]))
                        W(DVE, e1, misc["tt_free"][tt])
                        e2 = cDVE.inc(DVE.tensor_tensor(out=Tt[tt][0:64, :], in0=ps[0:64, a, :], in1=Rt[64:128, :],
                                                        op=ALU.mult))
                        acc_free[a] = e2
                        misc["dl"] = e2
                        ys = misc["ys"] % NYS
                        misc["ys"] += 1
                        W(POOL, e2, st_free[ys])
                        e3 = cPOOL.inc(POOL.tensor_tensor(out=Yst[ys][0:64, :], in0=Tt[tt][0:64, :],
                                                          in1=Z[b][0:64, g * 512:(g + 1) * 512], op=ALU.mult))
                        misc["tt_free"][tt] = e3
                        head_last_epi[h] = e3
                        W(SP, e3)
                        st_free[ys] = cST[ys].inc(SP.dma_start(out=YT[h * 64:(h + 1) * 64, g * 512:(g + 1) * 512],
                                                               in_=Yst[ys][0:64, :]))

                    def epilogue_diff(n):
                        u = batches[n]["u"]
                        un = units[u]
                        h, g, m = un["h"], un["g"], un["m"]
                        b = h % 2
                        W(DVE, pv_ev[n], misc.get("dl"))
                        e1 = cDVE.inc(DVE.reciprocal(out=Rt[:, :], in_=ps[:, 5, :]))
                        W(DVE, e1)
                        if m == 0:
                            eA = cDVE.inc(DVE.tensor_tensor(out=Tt[g % 2][:, :], in0=ps[:, 4, :], in1=Rt[:, :],
                                                            op=ALU.mult))
                            acc_free[4] = eA
                            misc["dl"] = eA
                            return
                        eB = cDVE.inc(DVE.tensor_tensor(out=Bt[:, :], in0=ps[:, 4, :], in1=Rt[:, :], op=ALU.mult))
                        acc_free[4] = eB
                        W(DVE, eB, misc["of_free"], misc.get("sq_done"))
                        eO = cDVE.inc(DVE.scalar_tensor_tensor(out=Of[:, :], in0=Bt[:, :], scalar=NEGLAM,
                                                               in1=Tt[g % 2][:, :], op0=ALU.mult, op1=ALU.add))
                        misc["dl"] = eO
                        W(POOL, eO, misc.get("sq_read"), misc.get("pl"))
                        eS = cPOOL.inc(POOL.tensor_tensor(out=SQ[:, :], in0=Of[:, :], in1=Of[:, :], op=ALU.mult))
                        misc["sq_done"] = eS
                        misc["pl"] = eS

                        def pe_part():
                            W(PE, eS, misc["sub_free"])
                            ePS = cPE.inc(PE.matmul(ps[:, 6, :], lhsT=onesb, rhs=SQ[:, :], start=True, stop=True))
                            head_last_pe[h] = ePS
                            misc["sq_read"] = ePS

                            def act_part():
                                W(ACT, ePS, misc.get("al"))
                                eL = cACT.inc(ACT.activation(out=LNV[:, :], in_=ps[:, 6, :], func=AF.Ln,
                                                             scale=1.0 / 128.0, bias=EPS5))
                                misc["sub_free"] = eL
                                W(ACT, eL, misc["of_free"])
                                eR = cACT.inc(ACT.activation(out=RS[:, :], in_=LNV[:, :], func=AF.Exp, scale=-0.5))
                                misc["al"] = eR
                                W(DVE, eR, misc.get("dl"), misc.get("y1_read"))
                                eY1 = cDVE.inc(DVE.scalar_tensor_tensor(out=Y1[:, :], in0=Of[:, :], scalar=GSUBC,
                                                                        in1=RS[:, :], op0=ALU.mult, op1=ALU.mult))
                                misc["of_free"] = eY1
                                misc["dl"] = eY1
                                ys = misc["ys"] % NYS
                                misc["ys"] += 1
                                W(POOL, eY1, st_free[ys], misc.get("pl"))
                                eY = cPOOL.inc(POOL.tensor_tensor(out=Yst[ys][:, :], in0=Y1[:, :],
                                                                  in1=Z[b][:, g * 512:(g + 1) * 512], op=ALU.mult))
                                misc["y1_read"] = eY
                                misc["pl"] = eY
                                head_last_epi[h] = eY
                                W(SP, eY)
                                st_free[ys] = cST[ys].inc(SP.dma_start(
                                    out=YT[512 + h * 128:512 + (h + 1) * 128, g * 512:(g + 1) * 512],
                                    in_=Yst[ys][:, :]))

                            deferred.setdefault(n + 3, []).append(act_part)

                        deferred.setdefault(n + 2, []).append(pe_part)

                    def run_deferred(n):
                        while True:
                            ks = sorted([k for k in deferred if k <= n])
                            if not ks:
                                break
                            for fn in deferred.pop(ks[0]):
                                fn()

                    def ensure_loaded(h):
                        while misc["loaded"] < min(h, NH - 1):
                            hh = misc["loaded"] + 1
                            if hh >= 2:
                                head_free[hh % 2] = [head_last_pe.get(hh - 2), head_last_epi.get(hh - 2)]
                            load_head(hh)
                            misc["loaded"] = hh

                    ensure_loaded(1)
                    emit_qk(0)
                    for n in range(NB):
                        run_deferred(n)
                        if n + 1 < NB:
                            hn = units[batches[n + 1]["u"]]["h"]
                            emit_qk(n + 1)
                        emit_exp(n)
                        emit_pv(n)
                        if batches[n]["last"]:
                            (epilogue_fox if fox else epilogue_diff)(n)
                            un = units[batches[n]["u"]]
                            if un["g"] == NG - 1 and (fox or un["m"] == 1):
                                if not fox:
                                    run_deferred(NB + 10)
                                if un["h"] + 2 < NH:
                                    ensure_loaded(un["h"] + 2)
                    run_deferred(NB + 10)
                    W(SP, *st_free)

        attention("fox")
        attention("diff")

        with ExitStack() as es:
            sbt = lambda n, s, d: es.enter_context(nc.sbuf_tensor(n, s, d))
            sem = lambda n: es.enter_context(nc.semaphore(n))
            WB = sbt("WB", [128, 8, D], BF16)
            WO = sbt("WO", [128, 8, D], BF16)
            GP = sbt("GP", [128, D], F32)
            wst = [sbt(f"dwst{i}", [128, D], F32) for i in range(2)]
            Yin = [sbt(f"Yin{i}", [128, 8, 512], BF16) for i in range(2)]
            Gin = [sbt(f"Gin{i}", [128, 16, 512], BF16) for i in range(2)]
            MT = [sbt(f"MT{i}", [128, 8, 512], BF16) for i in range(2)]
            t1 = [sbt(f"dt1{i}", [128, 512], F32) for i in range(2)]
            t2 = [sbt(f"dt2{i}", [128, 512], F32) for i in range(2)]
            xin = [sbt(f"xin{i}", [128, D], F32) for i in range(2)]
            yn = [sbt(f"yn{i}", [128, D], F32) for i in range(2)]
            ob = [sbt(f"ob{i}", [128, D], F32) for i in range(2)]
            junk = sbt("djunk", [128, D], BF16)
            ss = sbt("dss", [128, NT], F32)
            lnv = sbt("dlnv", [128, NT], F32)
            rstd = sbt("drstd", [128, NT], F32)
            ps = es.enter_context(nc.psum_tensor("psd", [128, 8, 512], F32))
            cPE, cACT, cDVE, cPOOL = (Chan(sem(n)) for n in ("d_pe", "d_act", "d_dve", "d_pool"))
            cWL = [Chan(sem(f"d_wl{i}"), 16) for i in range(2)]
            cGL = [Chan(sem(f"d_gl{i}"), 16) for i in range(2)]
            cXL = [Chan(sem(f"d_xl{i}"), 16) for i in range(2)]
            cOS = [Chan(sem(f"d_os{i}"), 16) for i in range(2)]
            cL0 = Chan(sem("d_l0"), 16)
            block = es.enter_context(nc.Block())

            @block.sync
            def _(_sync):
                ev_gp = cL0.inc(SP.dma_start(out=GP[:, :], in_=prm_d[:, 274:274 + D]))
                cast_ev = [None, None]
                lastc = {}
                k = 0
                for (src, dstw) in ((wbr, WB), (wout, WO)):
                    for c in range(8):
                        sl = k % 2
                        W(SP, cast_ev[sl])
                        ev = cWL[sl].inc(SP.dma_start(out=wst[sl][:, :], in_=src[:, c, :]))
                        eng, ch = (DVE, cDVE) if k % 2 == 0 else (POOL, cPOOL)
                        W(eng, ev)
                        cast_ev[sl] = ch.inc(eng.tensor_copy(out=dstw[:, c, :], in_=wst[sl][:, :]))
                        lastc[k % 2] = cast_ev[sl]
                        k += 1
                W(PE, lastc[0], lastc[1])
                W(DVE, ev_gp)

                gl_ev = {}
                g_free = [None, None]
                mt_free = [None, None]
                mt_ready = {}
                x_free = [None, None]
                ob_free = [None, None]
                tfree = [None, None]
                bank_free = {}
                st = {"pc": 0, "tc": 0}

                def load_group(G):
                    sl = G % 2
                    W(SP, g_free[sl])
                    gs = slice(G * 512, (G + 1) * 512)
                    cGL[sl].inc(SP.dma_start(out=Yin[sl][:, :, :], in_=YT[:, gs].rearrange("(c p) t -> p c t", p=128)))
                    gl_ev[G] = cGL[sl].inc(SP.dma_start(out=Gin[sl][:, :, :],
                                                        in_=GT[:, gs].rearrange("(c p) t -> p c t", p=128)))

                def merge_group(G):
                    sl = G % 2
                    W(PE, gl_ev[G])
                    last_dve = None
                    for j in range(8):
                        bA = (st["pc"] % 2) * 2
                        bB = bA + 1
                        st["pc"] += 1
                        W(PE, bank_free.get(bA), bank_free.get(bB))
                        for c in range(4):
                            ins = PE.matmul(ps[:, bA, :], lhsT=WB[:, c, j * 128:(j + 1) * 128], rhs=Yin[sl][:, c, :],
                                            start=(c == 0), stop=(c == 3))
                        for c in range(4):
                            ins = PE.matmul(ps[:, bB, :], lhsT=WB[:, 4 + c, j * 128:(j + 1) * 128],
                                            rhs=Yin[sl][:, 4 + c, :], start=(c == 0), stop=(c == 3))
                        mev = cPE.inc(ins)
                        ti = st["tc"] % 2
                        st["tc"] += 1
                        W(DVE, mev, tfree[ti])
                        ins = DVE.tensor_tensor(out=t1[ti][:, :], in0=ps[:, bA, :], in1=Gin[sl][:, j, :], op=ALU.mult)
                        bank_free[bA] = cDVE.inc(ins)
                        ins = DVE.tensor_tensor(out=t2[ti][:, :], in0=ps[:, bB, :], in1=Gin[sl][:, 8 + j, :],
                                                op=ALU.mult)
                        e2 = cDVE.inc(ins)
                        bank_free[bB] = e2
                        last_dve = e2
                        W(POOL, e2)
                        if j == 0:
                            W(POOL, mt_free[sl])
                        e3 = cPOOL.inc(POOL.tensor_tensor(out=MT[sl][:, j, :], in0=t1[ti][:, :], in1=t2[ti][:, :],
                                                          op=ALU.add))
                        tfree[ti] = e3
                    mt_ready[G] = e3
                    g_free[sl] = [last_dve, mev]

                def out_group(G):
                    sl = G % 2
                    W(PE, mt_ready[G])
                    for i in range(4):
                        T = G * 4 + i
                        xs = T % 2
                        W(SP, x_free[xs])
                        xev = cXL[xs].inc(SP.dma_start(out=xin[xs][:, :], in_=x[T * 128:(T + 1) * 128, :]))
                        b0 = 4 + (T % 2) * 2
                        W(PE, bank_free.get(b0))
                        for half in range(2):
                            for c in range(8):
                                ins = PE.matmul(ps[:, b0 + half, :], lhsT=MT[sl][:, c, i * 128:(i + 1) * 128],
                                                rhs=WO[:, c, half * 512:(half + 1) * 512], start=(c == 0),
                                                stop=(c == 7))
                        mev = cPE.inc(ins)
                        W(ACT, mev)
                        eq = cACT.inc(ACT.activation(out=junk[:, :], in_=ps[:, b0:b0 + 2, :].rearrange(
                            "p a b -> p (a b)"), func=AF.Square, accum_out=ss[:, T:T + 1]))
                        W(ACT, eq)
                        el = cACT.inc(ACT.activation(out=lnv[:, T:T + 1], in_=ss[:, T:T + 1], func=AF.Ln,
                                                     scale=1.0 / D, bias=EPS6))
                        W(ACT, el)
                        er = cACT.inc(ACT.activation(out=rstd[:, T:T + 1], in_=lnv[:, T:T + 1], func=AF.Exp,
                                                     scale=-0.5))
                        W(DVE, er, mev, x_free[xs])
                        ey = cDVE.inc(DVE.scalar_tensor_tensor(
                            out=yn[xs][:, :], in0=ps[:, b0:b0 + 2, :].rearrange("p a b -> p (a b)"),
                            scalar=rstd[:, T:T + 1], in1=GP[:, :], op0=ALU.mult, op1=ALU.mult))
                        bank_free[b0] = [ey, eq]
                        W(POOL, ey, xev, ob_free[xs])
                        eo = cPOOL.inc(POOL.tensor_tensor(out=ob[xs][:, :], in0=yn[xs][:, :], in1=xin[xs][:, :],
                                                          op=ALU.add))
                        x_free[xs] = eo
                        W(SP, eo)
                        ob_free[xs] = cOS[xs].inc(SP.dma_start(out=out_d[T * 128:(T + 1) * 128, :], in_=ob[xs][:, :]))
                        if i == 3:
                            mt_free[sl] = mev

                load_group(0)
                if NG > 1:
                    load_group(1)
                merge_group(0)
                for G in range(NG):
                    if G + 1 < NG:
                        merge_group(G + 1)
                    out_group(G)
                    if G + 2 < NG:
                        load_group(G + 2)
                W(SP, ob_free[0], ob_free[1])
    return nc


_CONST_CACHE = {}


def _consts(S):
    if S in _CONST_CACHE:
        return _CONST_CACHE[S]
    bf = ml_dtypes.bfloat16
    cb = np.zeros((128, 2304), np.float32)
    cb[:, 0:128] = np.eye(128, dtype=np.float32)
    cb[:, 128:256] = 1.0
    k = np.arange(128)[:, None]
    q = np.arange(512)[None, :]
    for j in range(4):
        cb[:, 256 + j * 512:256 + (j + 1) * 512] = np.where((128 * j + k) <= q, 0.0, -240000.0)
    cb = cb.astype(bf)
    cf = np.zeros((128, 384), np.float32)
    s_ = np.arange(128)[:, None]
    t_ = np.arange(128)[None, :]
    cf[:, 0:128] = (s_ <= t_).astype(np.float32)
    cf[:, 128:256] = 1.0
    cf[:, 256:384] = np.eye(128, dtype=np.float32)
    pos = np.arange(S, dtype=np.float32)
    inv_freq = (np.float32(10000.0) ** (-(np.arange(0, 64, 2, dtype=np.float32) / np.float32(64)))).astype(np.float32)
    ang = (pos[:, None] * inv_freq[None, :]).astype(np.float32)
    cos = np.cos(ang).astype(np.float32).T
    sin = np.sin(ang).astype(np.float32).T
    ropec = np.zeros((128, S), np.float32)
    ropes = np.zeros((128, S), np.float32)
    for p in range(128):
        i = p % 32
        half = (p % 64) // 32
        ropec[p] = cos[i]
        ropes[p] = -sin[i] if half == 0 else sin[i]
    _CONST_CACHE[S] = (cb, cf, ropec, ropes)
    return _CONST_CACHE[S]


def _layout_weights(g_pre, w_in, b_forget, lq1, lk1, lq2, lk2, g_subln, w_branch, w_out, g_post):
    w = np.asarray(w_in[0], np.float32)
    qa, ka, va, fa, za = w[:, 0:512], w[:, 512:1024], w[:, 1024:1536], w[:, 1536:1544], w[:, 1544:2056]
    qb, kb, vb, zb, gates = w[:, 2056:2568], w[:, 2568:3080], w[:, 3080:3592], w[:, 3592:4104], w[:, 4104:6152]
    swap = np.arange(512).reshape(8, 2, 32)[:, ::-1, :].reshape(-1)
    w2 = np.concatenate([qa, ka, za, zb, gates, qb, kb, qb[:, swap], kb[:, swap], va, vb, fa], axis=1)
    assert w2.shape[1] == NW
    w2 = np.ascontiguousarray(w2.reshape(8, 128, NW).transpose(1, 0, 2))
    wbr = np.asarray(w_branch[0], np.float32).reshape(2 * 512, D)
    wbr = np.ascontiguousarray(wbr.reshape(8, 128, D).transpose(1, 0, 2))
    wo = np.ascontiguousarray(np.asarray(w_out[0], np.float32).reshape(8, 128, D).transpose(1, 0, 2))
    prm = np.zeros((128, NPRM), np.float32)
    prm[:, 0:8] = np.asarray(g_pre[0], np.float32).reshape(8, 128).T
    prm[:, 8:16] = np.asarray(b_forget[0], np.float32)[None, :]
    prm[:, 16:80] = np.asarray(lq1[0], np.float32)[None, :]
    prm[:, 80:144] = np.asarray(lk1[0], np.float32)[None, :]
    prm[:, 144:208] = np.asarray(lq2[0], np.float32)[None, :]
    prm[:, 208:272] = np.asarray(lk2[0], np.float32)[None, :]
    prm[:, 272] = np.asarray(g_subln[0], np.float32)
    prm[:, 274:274 + D] = np.asarray(g_post[0], np.float32)[None, :]
    return w2, wbr, wo, prm


_NC_CACHE = {}


def kernel(x, g_pre, w_in, b_forget, lambda_q1, lambda_k1, lambda_q2, lambda_k2, g_subln, w_branch, w_out,
           g_post):
    x = np.asarray(x, np.float32)
    B, S, _ = x.shape
    w2, wbr, wo, prm = _layout_weights(g_pre, w_in, b_forget, lambda_q1, lambda_k1, lambda_q2, lambda_k2,
                                       g_subln, w_branch, w_out, g_post)
    cb, cf, ropec, ropes = _consts(S)
    if S not in _NC_CACHE:
        _NC_CACHE[S] = build(S)
    nc = _NC_CACHE[S]
    in_maps = [dict(x=np.ascontiguousarray(x[b]), w2=w2, wbr=wbr, wout=wo, prm=prm, cb=cb, cf=cf,
                    ropec=ropec, ropes=ropes) for b in range(B)]
    res = run_bass_kernel_spmd(nc, in_maps, core_ids=list(range(B)))
    return np.stack([np.asarray(r["out"], np.float32) for r in res.results], axis=0)
```

```python
import math
from contextlib import ExitStack

import numpy as np
import ml_dtypes
import concourse.bass as bass
import concourse.mybir as mybir
from concourse.bass_utils import run_bass_kernel_spmd

F32 = mybir.dt.float32
BF16 = mybir.dt.bfloat16
ALU = mybir.AluOpType
AF = mybir.ActivationFunctionType

D = 1024
NFM = 48
COL_VA = NFM * 128
COL_VB = COL_VA + 512
COL_FA = COL_VB + 512
NW = COL_FA + 8
WCH = 598
NPRM = 274 + 1024
LAM_INIT = 0.8 - 0.6 * math.exp(-0.3 * 0.0)
NORM_EPS = 1e-6
SUBLN_EPS = 1e-5


class Ev:
    __slots__ = ("ch", "val")

    def __init__(self, ch, val):
        self.ch = ch
        self.val = val


class Chan:
    def __init__(self, sem, step=1):
        self.sem = sem
        self.step = step
        self.val = 0

    def inc(self, ins):
        self.val += self.step
        ins.then_inc(self.sem, self.step)
        return Ev(self, self.val)


class Emitter:
    def __init__(self, nc):
        self.nc = nc
        self.waited = {}

    def wait(self, eng, *evs):
        for ev in evs:
            if ev is None:
                continue
            if isinstance(ev, (list, tuple)):
                self.wait(eng, *ev)
                continue
            key = (id(eng), ev.ch)
            if self.waited.get(key, 0) >= ev.val:
                continue
            eng.wait_ge(ev.ch.sem, ev.val)
            self.waited[key] = ev.val


def build(S):
    NT = S // 128
    NG = S // 512
    NF = 8 * NT
    NCH = max(1, NF // 128)
    CHP = min(128, NF)

    nc = bass.Bass("TRN2", target_bir_lowering=False)
    x = nc.dram_tensor("x", [S, D], F32, kind="ExternalInput").ap()
    w2 = nc.dram_tensor("w2", [128, 8, NW], F32, kind="ExternalInput").ap()
    wbr = nc.dram_tensor("wbr", [128, 8, D], F32, kind="ExternalInput").ap()
    wout = nc.dram_tensor("wout", [128, 8, D], F32, kind="ExternalInput").ap()
    prm_d = nc.dram_tensor("prm", [128, NPRM], F32, kind="ExternalInput").ap()
    cb_d = nc.dram_tensor("cb", [128, 2304], BF16, kind="ExternalInput").ap()
    cf_d = nc.dram_tensor("cf", [128, 384], F32, kind="ExternalInput").ap()
    ropec_d = nc.dram_tensor("ropec", [128, S], F32, kind="ExternalInput").ap()
    ropes_d = nc.dram_tensor("ropes", [128, S], F32, kind="ExternalInput").ap()
    out_d = nc.dram_tensor("out", [S, D], F32, kind="ExternalOutput").ap()

    QF = nc.dram_tensor("QF", [512, S], BF16, kind="Internal").ap()
    KF = nc.dram_tensor("KF", [512, S], BF16, kind="Internal").ap()
    VF = nc.dram_tensor("VF", [S, 512], BF16, kind="Internal").ap()
    VD = nc.dram_tensor("VD", [S, 512], BF16, kind="Internal").ap()
    ZT = nc.dram_tensor("ZT", [1024, S], BF16, kind="Internal").ap()
    GT = nc.dram_tensor("GT", [2048, S], BF16, kind="Internal").ap()
    QD = nc.dram_tensor("QD", [512, S], BF16, kind="Internal").ap()
    KD = nc.dram_tensor("KD", [512, S], BF16, kind="Internal").ap()
    YT = nc.dram_tensor("YT", [1024, S], BF16, kind="Internal").ap()
    AUG = nc.dram_tensor("AUG", [4, 8 * S], BF16, kind="Internal").ap()

    em = Emitter(nc)
    W = em.wait
    PE, ACT, DVE, POOL, SP = nc.tensor, nc.scalar, nc.vector, nc.gpsimd, nc.sync

    with ExitStack() as g:
        g.enter_context(nc.allow_low_precision("bf16 matmul operands, fp32 accumulation"))
        g.enter_context(nc.allow_non_contiguous_dma("small strided V loads"))
        PRM = g.enter_context(nc.sbuf_tensor("PRM", [128, 274], F32))
        CB = g.enter_context(nc.sbuf_tensor("CB", [128, 2304], BF16))
        CF = g.enter_context(nc.sbuf_tensor("CF", [128, 384], F32))
        FA = g.enter_context(nc.sbuf_tensor("FA", [128, 8, NT], F32))
        SM = g.enter_context(nc.sbuf_tensor("SM", [128, 16], F32))
        identb = CB[:, 0:128]
        onesb = CB[:, 128:256]
        NEGLAM = SM[:, 0:1]
        GSUBC = SM[:, 1:2]
        EPS6 = SM[:, 2:3]
        EPS5 = SM[:, 3:4]
        ONE = SM[:, 4:5]

        def mask(j):
            return CB[:, 256 + j * 512:256 + (j + 1) * 512]

        with ExitStack() as es:
            sbt = lambda n, s, d: es.enter_context(nc.sbuf_tensor(n, s, d))
            sem = lambda n: es.enter_context(nc.semaphore(n))
            Wb = sbt("Wb", [128, 8, NW], BF16)
            wst = [sbt(f"wst{i}", [128, WCH], F32) for i in range(4)]
            xt = [sbt(f"xt{i}", [128, D], F32) for i in range(2)]
            hb = [sbt(f"hb{i}", [128, D], BF16) for i in range(2)]
            hT = [sbt(f"hT{i}", [128, 8, 512], BF16) for i in range(2)]
            rc = [sbt(f"rc{i}", [128, 512], F32) for i in range(2)]
            rs = [sbt(f"rs{i}", [128, 512], F32) for i in range(2)]
            NS = 6
            stage = [sbt(f"stg{i}", [128, 512], BF16) for i in range(NS)]
            tmp1 = sbt("tmp1", [128, 512], F32)
            tmp2 = sbt("tmp2", [128, 512], F32)
            junk = sbt("junk", [128, D], BF16)
            ss = sbt("ss", [128, NT], F32)
            lnv = sbt("lnv", [128, NT], F32)
            rstd = sbt("rstd", [128, NT], F32)
            lt = sbt("lt", [128, 64], F32)
            psm = es.enter_context(nc.psum_tensor("psm", [128, 4, 512], F32))
            psT = es.enter_context(nc.psum_tensor("psT", [128, 2, 8, 128], BF16))
            psf = es.enter_context(nc.psum_tensor("psf", [128, 512], F32))
            cPE, cACT, cDVE, cPOOL = (Chan(sem(n)) for n in ("a_pe", "a_act", "a_dve", "a_pool"))
            cLD0 = Chan(sem("a_ld0"), 16)
            cWL = [Chan(sem(f"a_wl{i}"), 16) for i in range(4)]
            cXL = [Chan(sem(f"a_xl{i}"), 16) for i in range(2)]
            cRL = [Chan(sem(f"a_rl{i}"), 16) for i in range(2)]
            cST = [Chan(sem(f"a_st{i}"), 16) for i in range(NS)]
            block = es.enter_context(nc.Block())

            @block.sync
            def _(_sync):
                SP.dma_start(out=PRM[:, :], in_=prm_d[:, 0:274]).then_inc(cLD0.sem, 16)
                SP.dma_start(out=CB[:, :], in_=cb_d[:, :]).then_inc(cLD0.sem, 16)
                ins = SP.dma_start(out=CF[:, :], in_=cf_d[:, :])
                cLD0.val = 32
                ev0 = cLD0.inc(ins)
                W(DVE, ev0)
                DVE.memset(SM[:, 2:3], NORM_EPS)
                DVE.memset(SM[:, 3:4], SUBLN_EPS)
                DVE.memset(SM[:, 4:5], 1.0)
                W(DVE, cDVE.inc(DVE.tensor_tensor(out=lt[:, :], in0=PRM[:, 16:80], in1=PRM[:, 80:144], op=ALU.mult)))
                ins = DVE.tensor_reduce(out=SM[:, 5:6], in_=lt[:, :], axis=mybir.AxisListType.X, op=ALU.add)
                e = cDVE.inc(ins)
                W(DVE, e)
                W(DVE, cDVE.inc(DVE.tensor_tensor(out=lt[:, :], in0=PRM[:, 144:208], in1=PRM[:, 208:272], op=ALU.mult)))
                ins = DVE.tensor_reduce(out=SM[:, 6:7], in_=lt[:, :], axis=mybir.AxisListType.X, op=ALU.add)
                e = cDVE.inc(ins)
                W(ACT, e)
                ins = ACT.activation(out=SM[:, 7:9], in_=SM[:, 5:7], func=AF.Exp)
                e = cACT.inc(ins)
                W(DVE, e)
                ins = DVE.tensor_tensor(out=SM[:, 9:10], in0=SM[:, 8:9], in1=SM[:, 7:8], op=ALU.subtract)
                e = cDVE.inc(ins)
                W(DVE, e)
                DVE.tensor_scalar(out=SM[:, 0:1], in0=SM[:, 9:10], scalar1=-LAM_INIT, scalar2=None, op0=ALU.add)
                ins = DVE.tensor_scalar(out=SM[:, 1:2], in0=PRM[:, 272:273], scalar1=1.0 - LAM_INIT, scalar2=None,
                                        op0=ALU.mult)
                ev_sm = cDVE.inc(ins)
                W(ACT, ev_sm)
                W(POOL, ev0)

                cast_ev = [None] * 4
                last_cast = {}
                k = 0
                for c in range(8):
                    for q in range(NW // WCH):
                        sl = k % 4
                        W(SP, cast_ev[sl])
                        ev = cWL[sl].inc(SP.dma_start(out=wst[sl][:, :], in_=w2[:, c, q * WCH:(q + 1) * WCH]))
                        eng, ch = (DVE, cDVE) if k % 2 == 0 else (POOL, cPOOL)
                        W(eng, ev)
                        ins = eng.tensor_scalar(out=Wb[:, c, q * WCH:(q + 1) * WCH], in0=wst[sl][:, :],
                                                scalar1=PRM[:, c:c + 1], scalar2=None, op0=ALU.mult)
                        cast_ev[sl] = ch.inc(ins)
                        last_cast[k % 2] = cast_ev[sl]
                        k += 1
                W(PE, last_cast[0], last_cast[1], ev0)

                xld_ev = {}
                sq_ev = {}
                rstd_ev = {}
                x_free = [None, None]
                tr_ev = {}
                trc_ev = {}
                grp_mm_ev = {}
                rl_ev = {}
                rope_done = {}

                def tile_load(T):
                    sl = T % 2
                    W(SP, x_free[sl])
                    xld_ev[T] = cXL[sl].inc(SP.dma_start(out=xt[sl][:, :], in_=x[T * 128:(T + 1) * 128, :]))
                    W(ACT, xld_ev[T], sq_ev.get(T - 1))
                    ins = ACT.activation(out=junk[:, :], in_=xt[sl][:, :], func=AF.Square, accum_out=ss[:, T:T + 1])
                    sq_ev[T] = cACT.inc(ins)

                def tile_stats(T):
                    W(ACT, sq_ev[T])
                    ins = ACT.activation(out=lnv[:, T:T + 1], in_=ss[:, T:T + 1], func=AF.Ln, scale=1.0 / D,
                                         bias=EPS6)
                    e1 = cACT.inc(ins)
                    W(ACT, e1)
                    ins = ACT.activation(out=rstd[:, T:T + 1], in_=lnv[:, T:T + 1], func=AF.Exp, scale=-0.5)
                    rstd_ev[T] = cACT.inc(ins)

                def tile_h(T):
                    G, i = divmod(T, 4)
                    sl = T % 2
                    W(DVE, rstd_ev[T], xld_ev[T], tr_ev.get(T - 2))
                    ins = DVE.tensor_scalar(out=hb[sl][:, :], in0=xt[sl][:, :], scalar1=rstd[:, T:T + 1],
                                            scalar2=None, op0=ALU.mult)
                    h_ev = cDVE.inc(ins)
                    x_free[sl] = [h_ev, sq_ev[T]]
                    W(PE, h_ev, trc_ev.get(T - 2))
                    for c in range(8):
                        ins = PE.transpose(out=psT[:, sl, c, :], in_=hb[sl][:, c * 128:(c + 1) * 128],
                                           identity=identb)
                    tr_ev[T] = cPE.inc(ins)
                    W(DVE, tr_ev[T], grp_mm_ev.get(G - 2))
                    ins = DVE.tensor_copy(out=hT[G % 2][:, :, i * 128:(i + 1) * 128], in_=psT[:, sl, :, :])
                    trc_ev[T] = cDVE.inc(ins)

                def rope_load(G):
                    sl = G % 2
                    W(SP, rope_done.get(G - 2))
                    cRL[sl].inc(SP.dma_start(out=rc[sl][:, :], in_=ropec_d[:, G * 512:(G + 1) * 512]))
                    rl_ev[G] = cRL[sl].inc(SP.dma_start(out=rs[sl][:, :], in_=ropes_d[:, G * 512:(G + 1) * 512]))

                state = {"pc": 0, "sc": 0}
                bank_free = [None] * 4
                stage_free = [None] * NS
                tmp_free = [None]
                fa_free = [None]

                def mm_fm(G, blk, bank):
                    W(PE, bank_free[bank])
                    for c in range(8):
                        ins = PE.matmul(psm[:, bank, :], lhsT=Wb[:, c, blk * 128:(blk + 1) * 128],
                                        rhs=hT[G % 2][:, c, :], start=(c == 0), stop=(c == 7))
                    return cPE.inc(ins)

                def mm_tm(G, i, col, bank):
                    W(PE, bank_free[bank])
                    for c in range(8):
                        ins = PE.matmul(psm[:, bank, :], lhsT=hT[G % 2][:, c, i * 128:(i + 1) * 128],
                                        rhs=Wb[:, c, col:col + 512], start=(c == 0), stop=(c == 7))
                    return cPE.inc(ins)

                def store(slot, ev, dst):
                    W(SP, ev)
                    stage_free[slot] = cST[slot].inc(SP.dma_start(out=dst, in_=stage[slot][:, :]))

                def nbank():
                    b = state["pc"] % 4
                    state["pc"] += 1
                    return b

                def nslot():
                    s = state["sc"] % NS
                    state["sc"] += 1
                    return s

                def group_mm(G):
                    W(PE, trc_ev[G * 4 + 3])
                    gs = slice(G * 512, (G + 1) * 512)
                    last_pe = None
                    fm = []
                    for b in range(4):
                        fm.append((b, "copy", QF[b * 128:(b + 1) * 128, gs]))
                    for b in range(4):
                        fm.append((4 + b, "copy", KF[b * 128:(b + 1) * 128, gs]))
                    for b in range(8):
                        fm.append((8 + b, "silu", ZT[b * 128:(b + 1) * 128, gs]))
                    for b in range(16):
                        fm.append((16 + b, "sigm", GT[b * 128:(b + 1) * 128, gs]))
                    for (blk, kind, dst) in fm:
                        bank = nbank()
                        mev = mm_fm(G, blk, bank)
                        slot = nslot()
                        if kind == "copy":
                            W(DVE, mev, stage_free[slot])
                            ins = DVE.tensor_copy(out=stage[slot][:, :], in_=psm[:, bank, :])
                            eev = cDVE.inc(ins)
                        else:
                            W(ACT, mev, stage_free[slot])
                            ins = ACT.activation(out=stage[slot][:, :], in_=psm[:, bank, :],
                                                 func=AF.Silu if kind == "silu" else AF.Sigmoid)
                            eev = cACT.inc(ins)
                        bank_free[bank] = eev
                        store(slot, eev, dst)
                    for b in range(8):
                        dst = (QD if b < 4 else KD)[(b % 4) * 128:(b % 4 + 1) * 128, gs]
                        bankA = nbank()
                        mevA = mm_fm(G, 32 + b, bankA)
                        bankB = nbank()
                        mevB = mm_fm(G, 40 + b, bankB)
                        slot = nslot()
                        W(DVE, mevA, rl_ev[G], tmp_free[0])
                        ins = DVE.tensor_tensor(out=tmp1[:, :], in0=psm[:, bankA, :], in1=rc[G % 2][:, :], op=ALU.mult)
                        bank_free[bankA] = cDVE.inc(ins)
                        W(DVE, mevB)
                        ins = DVE.tensor_tensor(out=tmp2[:, :], in0=psm[:, bankB, :], in1=rs[G % 2][:, :], op=ALU.mult)
                        e2 = cDVE.inc(ins)
                        bank_free[bankB] = e2
                        W(POOL, e2, stage_free[slot])
                        ins = POOL.tensor_tensor(out=stage[slot][:, :], in0=tmp1[:, :], in1=tmp2[:, :], op=ALU.add)
                        e3 = cPOOL.inc(ins)
                        tmp_free[0] = e3
                        store(slot, e3, dst)
                    rope_done[G] = e2
                    W(PE, fa_free[0])
                    for i in range(4):
                        T = G * 4 + i
                        for (col, dstT) in ((COL_VA, VF), (COL_VB, VD)):
                            bank = nbank()
                            mev = mm_tm(G, i, col, bank)
                            slot = nslot()
                            W(ACT, mev, stage_free[slot])
                            ins = ACT.activation(out=stage[slot][:, :], in_=psm[:, bank, :], func=AF.Copy)
                            eev = cACT.inc(ins)
                            bank_free[bank] = eev
                            store(slot, eev, dstT[T * 128:(T + 1) * 128, :])
                        for c in range(8):
                            ins = PE.matmul(psf[:, i * 8:(i + 1) * 8], lhsT=hT[G % 2][:, c, i * 128:(i + 1) * 128],
                                            rhs=Wb[:, c, COL_FA:COL_FA + 8], start=(c == 0), stop=(c == 7))
                        last_pe = cPE.inc(ins)
                    grp_mm_ev[G] = last_pe
                    W(DVE, last_pe)
                    for i in range(4):
                        ins = DVE.tensor_copy(out=FA[:, :, G * 4 + i], in_=psf[:, i * 8:(i + 1) * 8])
                    fa_free[0] = cDVE.inc(ins)

                def prep_group(G):
                    rope_load(G)
                    for i in range(4):
                        T = G * 4 + i
                        tile_load(T)
                        tile_stats(T)
                        tile_h(T)

                prep_group(0)
                for G in range(NG):
                    if G + 1 < NG:
                        prep_group(G + 1)
                    group_mm(G)
                W(SP, *stage_free)
                W(SP, fa_free[0])

        with ExitStack() as es:
            sbt = lambda n, s, d: es.enter_context(nc.sbuf_tensor(n, s, d))
            sem = lambda n: es.enter_context(nc.semaphore(n))
            E1 = sbt("E1", [128, NF], F32)
            LS = sbt("LS", [128, NF], F32)
            TOT = sbt("TOT", [128, 8, NT], F32)
            INC = sbt("INC", [128, 8, NT], F32)
            N1 = sbt("N1", [128, NF], F32)
            N8 = sbt("N8", [128, NF], F32)
            R1 = sbt("R1", [128, NF], F32)
            R2 = sbt("R2", [128, NF], F32)
            AUG4 = sbt("AUG4", [128, 4, NF], BF16)
            AUGT = sbt("AUGT", [128, 4 * NCH, 128], BF16)
            ps = es.enter_context(nc.psum_tensor("ps2", [128, 2, 512], F32))
            psA = es.enter_context(nc.psum_tensor("psA", [128, 4 * NCH, 128], BF16))
            cPE, cACT, cDVE = (Chan(sem(n)) for n in ("b_pe", "b_act", "b_dve"))
            cST = Chan(sem("b_st"), 16)
            block = es.enter_context(nc.Block())

            @block.sync
            def _(_sync):
                FAf = FA[:, :, :].rearrange("p h t -> p (h t)")
                for h in range(8):
                    ins = DVE.tensor_scalar(out=FA[:, h, :], in0=FA[:, h, :], scalar1=PRM[:, 8 + h:9 + h],
                                            scalar2=None, op0=ALU.add)
                e = cDVE.inc(ins)
                W(ACT, e)
                e = cACT.inc(ACT.activation(out=E1[:, :], in_=FAf, func=AF.Exp, scale=-1.0))
                W(ACT, e)
                e = cACT.inc(ACT.activation(out=LS[:, :], in_=E1[:, :], func=AF.Ln, bias=ONE, scale=1.0))
                W(PE, e)
                PE.matmul(ps[:, 0, 0:NF], lhsT=CF[:, 0:128], rhs=LS[:, :], start=True, stop=True)
                e = cPE.inc(PE.matmul(ps[:, 1, 0:NF], lhsT=CF[:, 128:256], rhs=LS[:, :], start=True, stop=True))
                W(DVE, e)
                TOTf = TOT[:, :, :].rearrange("p h t -> p (h t)")
                INCf = INC[:, :, :].rearrange("p h t -> p (h t)")
                e = cDVE.inc(DVE.tensor_copy(out=TOTf, in_=ps[:, 1, 0:NF]))
                W(DVE, e)
                for h in range(8):
                    ins = DVE.tensor_tensor_scan(out=INC[:, h, :], data0=CF[:, 128:128 + NT], data1=TOT[:, h, :],
                                                 initial=0.0, op0=ALU.mult, op1=ALU.add)
                e = cDVE.inc(ins)
                W(DVE, e)
                e = cDVE.inc(DVE.tensor_tensor(out=N1[:, :], in0=ps[:, 0, 0:NF], in1=INCf, op=ALU.add))
                W(DVE, e)
                e = cDVE.inc(DVE.tensor_tensor(out=N8[:, :], in0=N1[:, :], in1=TOTf, op=ALU.subtract))
                W(DVE, e)
                e = cDVE.inc(DVE.tensor_scalar(out=N8[:, :], in0=N8[:, :], scalar1=8.0, scalar2=None, op0=ALU.mult))
                W(DVE, e)
                e = cDVE.inc(DVE.tensor_copy(out=AUG4[:, 1, :], in_=N8[:, :]))
                W(DVE, e)
                e = cDVE.inc(DVE.tensor_tensor(out=R1[:, :], in0=N8[:, :], in1=AUG4[:, 1, :], op=ALU.subtract))
                W(DVE, e)
                e = cDVE.inc(DVE.tensor_copy(out=AUG4[:, 2, :], in_=R1[:, :]))
                W(DVE, e)
                e = cDVE.inc(DVE.tensor_tensor(out=R2[:, :], in0=R1[:, :], in1=AUG4[:, 2, :], op=ALU.subtract))
                W(DVE, e)
                DVE.tensor_copy(out=AUG4[:, 3, :], in_=R2[:, :])
                e = cDVE.inc(DVE.tensor_scalar(out=AUG4[:, 0, :], in0=AUG4[:, 1, :], scalar1=-1.0, scalar2=None,
                                               op0=ALU.mult))
                W(PE, e)
                for r in range(4):
                    for i in range(NCH):
                        ins = PE.transpose(out=psA[0:CHP, r * NCH + i, :], in_=AUG4[:, r, i * 128:i * 128 + CHP],
                                           identity=identb)
                e = cPE.inc(ins)
                W(DVE, e)
                e = cDVE.inc(DVE.tensor_copy(out=AUGT[0:CHP, :, :], in_=psA[0:CHP, :, :]))
                W(SP, e)
                for r in range(4):
                    for i in range(NCH):
                        dst = AUG[r:r + 1, i * 128 * 128:i * 128 * 128 + CHP * 128].rearrange(
                            "o (p t) -> (o p) t", t=128)
                        ev_st = cST.inc(SP.dma_start(out=dst, in_=AUGT[0:CHP, r * NCH + i, :]))
                W(SP, ev_st)

        def attention(kind):
            fox = kind == "fox"
            NH = 8 if fox else 4
            BW = 3 if fox else 2
            NPB = 4
            KR = 68 if fox else 64
            with ExitStack() as es:
                sbt = lambda n, s, d: es.enter_context(nc.sbuf_tensor(kind + "_" + n, s, d))
                sem = lambda n: es.enter_context(nc.semaphore(kind + "_" + n))
                QT = [sbt(f"QT{i}", [128, S], BF16) for i in range(2)]
                KT = [sbt(f"KT{i}", [128, S], BF16) for i in range(2)]
                V = [sbt(f"V{i}", [128, NT, 128], BF16) for i in range(2)]
                Z = [sbt(f"Z{i}", [128, S], BF16) for i in range(2)]
                P = [sbt(f"P{i}", [128, BW, 512], BF16) for i in range(NPB)]
                Rt = sbt("Rt", [128, 512], F32)
                Tt = [sbt(f"Tt{i}", [128, 512], F32) for i in range(2)]
                NYS = 2
                Yst = [sbt(f"Yst{i}", [128, 512], BF16) for i in range(NYS)]
                if not fox:
                    Bt = sbt("Bt", [128, 512], F32)
                    Of = sbt("Of", [128, 512], F32)
                    SQ = sbt("SQ", [128, 512], BF16)
                    LNV = sbt("LNV", [128, 512], F32)
                    RS = sbt("RS", [128, 512], F32)
                    Y1 = sbt("Y1", [128, 512], F32)
                ps = es.enter_context(nc.psum_tensor(kind + "_psat", [128, 8, 512], F32))
                cPE, cACT, cDVE, cPOOL = (Chan(sem(n)) for n in ("c_pe", "c_act", "c_dve", "c_pool"))
                cLD = [Chan(sem(f"c_ld{i}"), 16) for i in range(2)]
                cST = [Chan(sem(f"c_st{i}"), 16) for i in range(NYS)]
                block = es.enter_context(nc.Block())

                @block.sync
                def _(_sync):
                    init_ev = None
                    if fox:
                        for b in range(2):
                            DVE.memset(QT[b][64:68, :], 1.0)
                            DVE.memset(KT[b][64:68, :], 1.0)
                            ins = POOL.memset(V[b][:, :, 64:128], 1.0)
                        init_ev = [cDVE.inc(DVE.memset(Rt[:, :], 1.0)), cPOOL.inc(ins)]
                    head_free = [None, None]
                    ld_ev = {}

                    def load_head(h):
                        b = h % 2
                        W(SP, head_free[b], init_ev)
                        c = cLD[b]
                        if fox:
                            c.inc(SP.dma_start(out=QT[b][0:64, :], in_=QF[h * 64:(h + 1) * 64, :]))
                            c.inc(SP.dma_start(out=QT[b][64:65, :], in_=AUG[0:1, h * S:(h + 1) * S]))
                            c.inc(SP.dma_start(out=KT[b][0:64, :], in_=KF[h * 64:(h + 1) * 64, :]))
                            c.inc(SP.dma_start(out=KT[b][65:68, :], in_=AUG[1:4, h * S:(h + 1) * S]))
                            vsrc = VF.rearrange("(n p) c -> p n c", p=128)
                            nsp = 4 if NT >= 4 else 1
                            for q in range(nsp):
                                a0, a1 = q * NT // nsp, (q + 1) * NT // nsp
                                c.inc(SP.dma_start(out=V[b][:, a0:a1, 0:64], in_=vsrc[:, a0:a1, h * 64:(h + 1) * 64]))
                            ld_ev[h] = c.inc(SP.dma_start(out=Z[b][0:64, :], in_=ZT[h * 64:(h + 1) * 64, :]))
                        else:
                            c.inc(SP.dma_start(out=QT[b][:, :], in_=QD[h * 128:(h + 1) * 128, :]))
                            c.inc(SP.dma_start(out=KT[b][:, :], in_=KD[h * 128:(h + 1) * 128, :]))
                            vsrc = VD.rearrange("(n p) c -> p n c", p=128)
                            nsp = 4 if NT >= 4 else 1
                            for q in range(nsp):
                                a0, a1 = q * NT // nsp, (q + 1) * NT // nsp
                                c.inc(SP.dma_start(out=V[b][:, a0:a1, :], in_=vsrc[:, a0:a1, h * 128:(h + 1) * 128]))
                            ld_ev[h] = c.inc(SP.dma_start(out=Z[b][:, :],
                                                          in_=ZT[512 + h * 128:512 + (h + 1) * 128, :]))

                    units = []
                    batches = []
                    for h in range(NH):
                        for g in range(NG):
                            u = len(units)
                            nk = 4 * (g + 1)
                            units.append(dict(h=h, g=g, nk=nk))
                            if fox:
                                blks = [(kt, 0) for kt in range(nk)]
                            else:
                                blks = [(kt, m) for kt in range(nk) for m in range(2)]
                            for s0 in range(0, len(blks), BW):
                                batches.append(dict(u=u, blks=blks[s0:s0 + BW], first=(s0 == 0),
                                                    last=(s0 + BW >= len(blks))))
                    NB = len(batches)
                    qk_ev = {}
                    exp_ev = {}
                    mask_ev = {}
                    pv_ev = {}
                    acc_free = {}
                    deferred = {}
                    st_free = [None] * NYS
                    head_last_pe = {}
                    head_last_epi = {}
                    misc = {"ys": 0, "of_free": None, "sub_free": None, "tt_free": [None, None], "loaded": -1}

                    def rows(m):
                        return slice(0, KR) if (fox or m == 0) else slice(64, 128)

                    def acc_banks(u):
                        if fox:
                            return (6 + u % 2,)
                        return (4, 5, 6)

                    def emit_qk(n):
                        bt = batches[n]
                        un = units[bt["u"]]
                        h, g = un["h"], un["g"]
                        b = h % 2
                        W(PE, ld_ev[h], exp_ev.get(n - 2))
                        sbase = (n % 2) * BW
                        for j, (kt, m) in enumerate(bt["blks"]):
                            band = kt >= 4 * g
                            r = rows(m)
                            ins = PE.matmul(ps[:, sbase + j, :], lhsT=KT[b][r, kt * 128:(kt + 1) * 128],
                                            rhs=QT[b][r, g * 512:(g + 1) * 512], start=True, stop=not band)
                        for j, (kt, m) in enumerate(bt["blks"]):
                            if kt >= 4 * g:
                                ins = PE.matmul(ps[:, sbase + j, :], lhsT=identb, rhs=mask(kt - 4 * g),
                                                start=False, stop=True)
                        qk_ev[n] = cPE.inc(ins)

                    def emit_exp(n):
                        bt = batches[n]
                        nb = len(bt["blks"])
                        sbase = (n % 2) * BW
                        W(ACT, qk_ev[n], pv_ev.get(n - NPB))
                        ins = ACT.activation(out=P[n % NPB][:, 0:nb, :], in_=ps[:, sbase:sbase + nb, :], func=AF.Exp,
                                             scale=0.125)
                        exp_ev[n] = cACT.inc(ins)

                    def emit_mask(n):
                        bt = batches[n]
                        un = units[bt["u"]]
                        g = un["g"]
                        ins = None
                        for j, kt in enumerate(bt["kts"]):
                            if kt >= 4 * g:
                                W(POOL, exp_ev[n])
                                ins = POOL.tensor_tensor(out=P[n % NPB][:, j, :], in0=P[n % NPB][:, j, :],
                                                         in1=mask(kt - 4 * g), op=ALU.mult)
                        if ins is not None:
                            mask_ev[n] = cPOOL.inc(ins)

                    def emit_pv(n):
                        bt = batches[n]
                        u = bt["u"]
                        un = units[u]
                        h, nk = un["h"], un["nk"]
                        b = h % 2
                        banks = acc_banks(u)
                        W(PE, exp_ev[n], mask_ev.get(n))
                        if bt["first"]:
                            W(PE, acc_free.get(banks[0]))
                        for j, (kt, m) in enumerate(bt["blks"]):
                            ins = PE.matmul(ps[:, banks[m], :], lhsT=V[b][:, kt, :], rhs=P[n % NPB][:, j, :],
                                            start=(kt == 0), stop=(kt == nk - 1))
                        if not fox:
                            for j, (kt, m) in enumerate(bt["blks"]):
                                ins = PE.matmul(ps[64 * m:64 * m + 64, banks[2], :], lhsT=onesb[:, 0:64],
                                                rhs=P[n % NPB][:, j, :], start=(kt == 0), stop=(kt == nk - 1),
                                                tile_position=(0, 64 * m))
                        pv_ev[n] = cPE.inc(ins)
                        head_last_pe[h] = pv_ev[n]

                    def ystore(h, g, e, nrow, row0):
                        ys = misc["ys"] % NYS
                        return ys

                    def epilogue_fox(n):
                        u = batches[n]["u"]
                        un = units[u]
                        h, g = un["h"], un["g"]
                        b = h % 2
                        a = acc_banks(u)[0]
                        tt = u % 2
                        W(DVE, pv_ev[n], misc.get("dl"))
                        e1 = cDVE.inc(DVE.reciprocal(out=Rt[64:128, :], in_=ps[64:128, a, :]))
                        W(DVE, e1, misc["tt_free"][tt])
                        e2 = cDVE.inc(DVE.tensor_tensor(out=Tt[tt][0:64, :], in0=ps[0:64, a, :], in1=Rt[64:128, :],
                                                        op=ALU.mult))
                        acc_free[a] = e2
                        misc["dl"] = e2
                        ys = misc["ys"] % NYS
                        misc["ys"] += 1
                        W(POOL, e2, st_free[ys])
                        e3 = cPOOL.inc(POOL.tensor_tensor(out=Yst[ys][0:64, :], in0=Tt[tt][0:64, :],
                                                          in1=Z[b][0:64, g * 512:(g + 1) * 512], op=ALU.mult))
                        misc["tt_free"][tt] = e3
                        head_last_epi[h] = e3
                        W(SP, e3)
                        st_free[ys] = cST[ys].inc(SP.dma_start(out=YT[h * 64:(h + 1) * 64, g * 512:(g + 1) * 512],
                                                               in_=Yst[ys][0:64, :]))

                    def epilogue_diff(n):
                        u = batches[n]["u"]
                        un = units[u]
                        h, g = un["h"], un["g"]
                        b = h % 2
                        A = Tt[0]
                        W(DVE, pv_ev[n], misc.get("dl"))
                        e1 = cDVE.inc(DVE.reciprocal(out=Rt[:, :], in_=ps[:, 6, :]))
                        W(DVE, e1)
                        DVE.tensor_tensor(out=A[0:64, :], in0=ps[0:64, 4, :], in1=Rt[0:64, :], op=ALU.mult)
                        DVE.tensor_tensor(out=A[64:128, :], in0=ps[64:128, 4, :], in1=Rt[0:64, :], op=ALU.mult)
                        DVE.tensor_tensor(out=Bt[0:64, :], in0=ps[0:64, 5, :], in1=Rt[64:128, :], op=ALU.mult)
                        eB = cDVE.inc(DVE.tensor_tensor(out=Bt[64:128, :], in0=ps[64:128, 5, :],
                                                        in1=Rt[64:128, :], op=ALU.mult))
                        acc_free[4] = eB
                        W(DVE, eB, misc["of_free"], misc.get("sq_done"))
                        eO = cDVE.inc(DVE.scalar_tensor_tensor(out=Of[:, :], in0=Bt[:, :], scalar=NEGLAM,
                                                               in1=A[:, :], op0=ALU.mult, op1=ALU.add))
                        misc["dl"] = eO
                        W(POOL, eO, misc.get("sq_read"), misc.get("pl"))
                        eS = cPOOL.inc(POOL.tensor_tensor(out=SQ[:, :], in0=Of[:, :], in1=Of[:, :], op=ALU.mult))
                        misc["sq_done"] = eS
                        misc["pl"] = eS

                        def pe_part():
                            W(PE, eS, misc["sub_free"])
                            ePS = cPE.inc(PE.matmul(ps[:, 7, :], lhsT=onesb, rhs=SQ[:, :], start=True, stop=True))
                            head_last_pe[h] = ePS
                            misc["sq_read"] = ePS

                            def act_part():
                                W(ACT, ePS, misc.get("al"))
                                eL = cACT.inc(ACT.activation(out=LNV[:, :], in_=ps[:, 7, :], func=AF.Ln,
                                                             scale=1.0 / 128.0, bias=EPS5))
                                misc["sub_free"] = eL
                                W(ACT, eL, misc["of_free"])
                                eR = cACT.inc(ACT.activation(out=RS[:, :], in_=LNV[:, :], func=AF.Exp, scale=-0.5))
                                misc["al"] = eR
                                W(DVE, eR, misc.get("dl"), misc.get("y1_read"))
                                eY1 = cDVE.inc(DVE.scalar_tensor_tensor(out=Y1[:, :], in0=Of[:, :], scalar=GSUBC,
                                                                        in1=RS[:, :], op0=ALU.mult, op1=ALU.mult))
                                misc["of_free"] = eY1
                                misc["dl"] = eY1
                                ys = misc["ys"] % NYS
                                misc["ys"] += 1
                                W(POOL, eY1, st_free[ys], misc.get("pl"))
                                eY = cPOOL.inc(POOL.tensor_tensor(out=Yst[ys][:, :], in0=Y1[:, :],
                                                                  in1=Z[b][:, g * 512:(g + 1) * 512], op=ALU.mult))
                                misc["y1_read"] = eY
                                misc["pl"] = eY
                                head_last_epi[h] = eY
                                W(SP, eY)
                                st_free[ys] = cST[ys].inc(SP.dma_start(
                                    out=YT[512 + h * 128:512 + (h + 1) * 128, g * 512:(g + 1) * 512],
                                    in_=Yst[ys][:, :]))

                            deferred.setdefault(n + 3, []).append(act_part)

                        deferred.setdefault(n + 2, []).append(pe_part)

                    def run_deferred(n):
                        while True:
                            ks = sorted([k for k in deferred if k <= n])
                            if not ks:
                                break
                            for fn in deferred.pop(ks[0]):
                                fn()

                    def ensure_loaded(h):
                        while misc["loaded"] < min(h, NH - 1):
                            hh = misc["loaded"] + 1
                            if hh >= 2:
                                head_free[hh % 2] = [head_last_pe.get(hh - 2), head_last_epi.get(hh - 2)]
                            load_head(hh)
                            misc["loaded"] = hh

                    ensure_loaded(1)
                    emit_qk(0)
                    for n in range(NB):
                        run_deferred(n)
                        if n + 1 < NB:
                            hn = units[batches[n + 1]["u"]]["h"]
                            emit_qk(n + 1)
                        emit_exp(n)
                        emit_pv(n)
                        if batches[n]["last"]:
                            (epilogue_fox if fox else epilogue_diff)(n)
                            un = units[batches[n]["u"]]
                            if un["g"] == NG - 1:
                                if not fox:
                                    run_deferred(NB + 10)
                                if un["h"] + 2 < NH:
                                    ensure_loaded(un["h"] + 2)
                    run_deferred(NB + 10)
                    W(SP, *st_free)

        attention("fox")
        attention("diff")

        with ExitStack() as es:
            sbt = lambda n, s, d: es.enter_context(nc.sbuf_tensor(n, s, d))
            sem = lambda n: es.enter_context(nc.semaphore(n))
            WB = sbt("WB", [128, 8, D], BF16)
            WO = sbt("WO", [128, 8, D], BF16)
            GP = sbt("GP", [128, D], F32)
            wst = [sbt(f"dwst{i}", [128, D], F32) for i in range(2)]
            Yin = [sbt(f"Yin{i}", [128, 8, 512], BF16) for i in range(2)]
            Gin = [sbt(f"Gin{i}", [128, 16, 512], BF16) for i in range(2)]
            MT = [sbt(f"MT{i}", [128, 8, 512], BF16) for i in range(2)]
            t1 = [sbt(f"dt1{i}", [128, 512], F32) for i in range(2)]
            t2 = [sbt(f"dt2{i}", [128, 512], F32) for i in range(2)]
            xin = [sbt(f"xin{i}", [128, D], F32) for i in range(2)]
            yn = [sbt(f"yn{i}", [128, D], F32) for i in range(2)]
            ob = [sbt(f"ob{i}", [128, D], F32) for i in range(2)]
            junk = sbt("djunk", [128, D], BF16)
            ss = sbt("dss", [128, NT], F32)
            lnv = sbt("dlnv", [128, NT], F32)
            rstd = sbt("drstd", [128, NT], F32)
            ps = es.enter_context(nc.psum_tensor("psd", [128, 8, 512], F32))
            cPE, cACT, cDVE, cPOOL = (Chan(sem(n)) for n in ("d_pe", "d_act", "d_dve", "d_pool"))
            cWL = [Chan(sem(f"d_wl{i}"), 16) for i in range(2)]
            cGL = [Chan(sem(f"d_gl{i}"), 16) for i in range(2)]
            cXL = [Chan(sem(f"d_xl{i}"), 16) for i in range(2)]
            cOS = [Chan(sem(f"d_os{i}"), 16) for i in range(2)]
            cL0 = Chan(sem("d_l0"), 16)
            block = es.enter_context(nc.Block())

            @block.sync
            def _(_sync):
                ev_gp = cL0.inc(SP.dma_start(out=GP[:, :], in_=prm_d[:, 274:274 + D]))
                cast_ev = [None, None]
                lastc = {}
                k = 0
                for (src, dstw) in ((wbr, WB), (wout, WO)):
                    for c in range(8):
                        sl = k % 2
                        W(SP, cast_ev[sl])
                        ev = cWL[sl].inc(SP.dma_start(out=wst[sl][:, :], in_=src[:, c, :]))
                        eng, ch = (DVE, cDVE) if k % 2 == 0 else (POOL, cPOOL)
                        W(eng, ev)
                        cast_ev[sl] = ch.inc(eng.tensor_copy(out=dstw[:, c, :], in_=wst[sl][:, :]))
                        lastc[k % 2] = cast_ev[sl]
                        k += 1
                W(PE, lastc[0], lastc[1])
                W(DVE, ev_gp)

                gl_ev = {}
                xld = {}
                g_free = [None, None]
                mt_free = [None, None]
                mt_ready = {}
                x_free = [None, None]
                ob_free = [None, None]
                tfree = [None, None]
                bank_free = {}
                st = {"pc": 0, "tc": 0}

                def load_group(G):
                    sl = G % 2
                    W(SP, g_free[sl])
                    gs = slice(G * 512, (G + 1) * 512)
                    cGL[sl].inc(SP.dma_start(out=Yin[sl][:, :, :], in_=YT[:, gs].rearrange("(c p) t -> p c t", p=128)))
                    gl_ev[G] = cGL[sl].inc(SP.dma_start(out=Gin[sl][:, :, :],
                                                        in_=GT[:, gs].rearrange("(c p) t -> p c t", p=128)))

                def merge_group(G):
                    sl = G % 2
                    W(PE, gl_ev[G])
                    last_dve = None
                    for j in range(8):
                        bA = (st["pc"] % 2) * 2
                        bB = bA + 1
                        st["pc"] += 1
                        W(PE, bank_free.get(bA), bank_free.get(bB))
                        for c in range(4):
                            ins = PE.matmul(ps[:, bA, :], lhsT=WB[:, c, j * 128:(j + 1) * 128], rhs=Yin[sl][:, c, :],
                                            start=(c == 0), stop=(c == 3))
                        for c in range(4):
                            ins = PE.matmul(ps[:, bB, :], lhsT=WB[:, 4 + c, j * 128:(j + 1) * 128],
                                            rhs=Yin[sl][:, 4 + c, :], start=(c == 0), stop=(c == 3))
                        mev = cPE.inc(ins)
                        ti = st["tc"] % 2
                        st["tc"] += 1
                        W(DVE, mev, tfree[ti])
                        ins = DVE.tensor_tensor(out=t1[ti][:, :], in0=ps[:, bA, :], in1=Gin[sl][:, j, :], op=ALU.mult)
                        bank_free[bA] = cDVE.inc(ins)
                        ins = DVE.tensor_tensor(out=t2[ti][:, :], in0=ps[:, bB, :], in1=Gin[sl][:, 8 + j, :],
                                                op=ALU.mult)
                        e2 = cDVE.inc(ins)
                        bank_free[bB] = e2
                        last_dve = e2
                        W(POOL, e2)
                        if j == 0:
                            W(POOL, mt_free[sl])
                        e3 = cPOOL.inc(POOL.tensor_tensor(out=MT[sl][:, j, :], in0=t1[ti][:, :], in1=t2[ti][:, :],
                                                          op=ALU.add))
                        tfree[ti] = e3
                    mt_ready[G] = e3
                    g_free[sl] = [last_dve, mev]

                def out_group(G):
                    sl = G % 2
                    W(PE, mt_ready[G])
                    for i in range(4):
                        T = G * 4 + i
                        xs = T % 2
                        if T == 0:
                            xld[0] = cXL[0].inc(SP.dma_start(out=xin[0][:, :], in_=x[0:128, :]))
                        if T + 1 < NT:
                            xn = (T + 1) % 2
                            W(SP, x_free[xn])
                            xld[T + 1] = cXL[xn].inc(SP.dma_start(out=xin[xn][:, :],
                                                                  in_=x[(T + 1) * 128:(T + 2) * 128, :]))
                        xev = xld[T]
                        b0 = 4 + (T % 2) * 2
                        W(PE, bank_free.get(b0))
                        for half in range(2):
                            for c in range(8):
                                ins = PE.matmul(ps[:, b0 + half, :], lhsT=MT[sl][:, c, i * 128:(i + 1) * 128],
                                                rhs=WO[:, c, half * 512:(half + 1) * 512], start=(c == 0),
                                                stop=(c == 7))
                        mev = cPE.inc(ins)
                        W(ACT, mev)
                        eq = cACT.inc(ACT.activation(out=junk[:, :], in_=ps[:, b0:b0 + 2, :].rearrange(
                            "p a b -> p (a b)"), func=AF.Square, accum_out=ss[:, T:T + 1]))
                        W(ACT, eq)
                        el = cACT.inc(ACT.activation(out=lnv[:, T:T + 1], in_=ss[:, T:T + 1], func=AF.Ln,
                                                     scale=1.0 / D, bias=EPS6))
                        W(ACT, el)
                        er = cACT.inc(ACT.activation(out=rstd[:, T:T + 1], in_=lnv[:, T:T + 1], func=AF.Exp,
                                                     scale=-0.5))
                        W(DVE, er, mev, x_free[xs])
                        ey = cDVE.inc(DVE.scalar_tensor_tensor(
                            out=yn[xs][:, :], in0=ps[:, b0:b0 + 2, :].rearrange("p a b -> p (a b)"),
                            scalar=rstd[:, T:T + 1], in1=GP[:, :], op0=ALU.mult, op1=ALU.mult))
                        bank_free[b0] = [ey, eq]
                        W(POOL, ey, xev, ob_free[xs])
                        eo = cPOOL.inc(POOL.tensor_tensor(out=ob[xs][:, :], in0=yn[xs][:, :], in1=xin[xs][:, :],
                                                          op=ALU.add))
                        x_free[xs] = eo
                        W(SP, eo)
                        ob_free[xs] = cOS[xs].inc(SP.dma_start(out=out_d[T * 128:(T + 1) * 128, :], in_=ob[xs][:, :]))
                        if i == 3:
                            mt_free[sl] = mev

                load_group(0)
                if NG > 1:
                    load_group(1)
                merge_group(0)
                for G in range(NG):
                    if G + 1 < NG:
                        merge_group(G + 1)
                    out_group(G)
                    if G + 2 < NG:
                        load_group(G + 2)
                W(SP, ob_free[0], ob_free[1])
    return nc


_CONST_CACHE = {}


def _consts(S):
    if S in _CONST_CACHE:
        return _CONST_CACHE[S]
    bf = ml_dtypes.bfloat16
    cb = np.zeros((128, 2304), np.float32)
    cb[:, 0:128] = np.eye(128, dtype=np.float32)
    cb[:, 128:256] = 1.0
    k = np.arange(128)[:, None]
    q = np.arange(512)[None, :]
    for j in range(4):
        cb[:, 256 + j * 512:256 + (j + 1) * 512] = np.where((128 * j + k) <= q, 0.0, -240000.0)
    cb = cb.astype(bf)
    cf = np.zeros((128, 384), np.float32)
    s_ = np.arange(128)[:, None]
    t_ = np.arange(128)[None, :]
    cf[:, 0:128] = (s_ <= t_).astype(np.float32)
    cf[:, 128:256] = 1.0
    cf[:, 256:384] = np.eye(128, dtype=np.float32)
    pos = np.arange(S, dtype=np.float32)
    inv_freq = (np.float32(10000.0) ** (-(np.arange(0, 64, 2, dtype=np.float32) / np.float32(64)))).astype(np.float32)
    ang = (pos[:, None] * inv_freq[None, :]).astype(np.float32)
    cos = np.cos(ang).astype(np.float32).T
    sin = np.sin(ang).astype(np.float32).T
    ropec = np.zeros((128, S), np.float32)
    ropes = np.zeros((128, S), np.float32)
    for p in range(128):
        i = p % 32
        half = (p % 64) // 32
        ropec[p] = cos[i]
        ropes[p] = -sin[i] if half == 0 else sin[i]
    _CONST_CACHE[S] = (cb, cf, ropec, ropes)
    return _CONST_CACHE[S]


def _layout_weights(g_pre, w_in, b_forget, lq1, lk1, lq2, lk2, g_subln, w_branch, w_out, g_post):
    w = np.asarray(w_in[0], np.float32)
    qa, ka, va, fa, za = w[:, 0:512], w[:, 512:1024], w[:, 1024:1536], w[:, 1536:1544], w[:, 1544:2056]
    qb, kb, vb, zb, gates = w[:, 2056:2568], w[:, 2568:3080], w[:, 3080:3592], w[:, 3592:4104], w[:, 4104:6152]
    swap = np.arange(512).reshape(8, 2, 32)[:, ::-1, :].reshape(-1)
    w2 = np.concatenate([qa, ka, za, zb, gates, qb, kb, qb[:, swap], kb[:, swap], va, vb, fa], axis=1)
    assert w2.shape[1] == NW
    w2 = np.ascontiguousarray(w2.reshape(8, 128, NW).transpose(1, 0, 2))
    wbr = np.asarray(w_branch[0], np.float32).reshape(2 * 512, D)
    wbr = np.ascontiguousarray(wbr.reshape(8, 128, D).transpose(1, 0, 2))
    wo = np.ascontiguousarray(np.asarray(w_out[0], np.float32).reshape(8, 128, D).transpose(1, 0, 2))
    prm = np.zeros((128, NPRM), np.float32)
    prm[:, 0:8] = np.asarray(g_pre[0], np.float32).reshape(8, 128).T
    prm[:, 8:16] = np.asarray(b_forget[0], np.float32)[None, :]
    prm[:, 16:80] = np.asarray(lq1[0], np.float32)[None, :]
    prm[:, 80:144] = np.asarray(lk1[0], np.float32)[None, :]
    prm[:, 144:208] = np.asarray(lq2[0], np.float32)[None, :]
    prm[:, 208:272] = np.asarray(lk2[0], np.float32)[None, :]
    prm[:, 272] = np.asarray(g_subln[0], np.float32)
    prm[:, 274:274 + D] = np.asarray(g_post[0], np.float32)[None, :]
    return w2, wbr, wo, prm


_NC_CACHE = {}


def kernel(x, g_pre, w_in, b_forget, lambda_q1, lambda_k1, lambda_q2, lambda_k2, g_subln, w_branch, w_out,
           g_post):
    x = np.asarray(x, np.float32)
    B, S, _ = x.shape
    w2, wbr, wo, prm = _layout_weights(g_pre, w_in, b_forget, lambda_q1, lambda_k1, lambda_q2, lambda_k2,
                                       g_subln, w_branch, w_out, g_post)
    cb, cf, ropec, ropes = _consts(S)
    if S not in _NC_CACHE:
        _NC_CACHE[S] = build(S)
    nc = _NC_CACHE[S]
    in_maps = [dict(x=np.ascontiguousarray(x[b]), w2=w2, wbr=wbr, wout=wo, prm=prm, cb=cb, cf=cf,
                    ropec=ropec, ropes=ropes) for b in range(B)]
    res = run_bass_kernel_spmd(nc, in_maps, core_ids=list(range(B)))
    return np.stack([np.asarray(r["out"], np.float32) for r in res.results], axis=0)
```

```python
import math
from contextlib import ExitStack

import numpy as np
import ml_dtypes
import concourse.bass as bass
import concourse.mybir as mybir
from concourse.bass_utils import run_bass_kernel_spmd

F32 = mybir.dt.float32
BF16 = mybir.dt.bfloat16
ALU = mybir.AluOpType
AF = mybir.ActivationFunctionType

D = 1024
NFM = 40
COL_VA = NFM * 128
COL_VB = COL_VA + 512
COL_FA = COL_VB + 512
NW = COL_FA + 8
WCH = 769
NPRM = 274 + 1024
LAM_INIT = 0.8 - 0.6 * math.exp(-0.3 * 0.0)
NORM_EPS = 1e-6
SUBLN_EPS = 1e-5


class Ev:
    __slots__ = ("ch", "val")

    def __init__(self, ch, val):
        self.ch = ch
        self.val = val


class Chan:
    def __init__(self, sem, step=1):
        self.sem = sem
        self.step = step
        self.val = 0

    def inc(self, ins):
        self.val += self.step
        ins.then_inc(self.sem, self.step)
        return Ev(self, self.val)


class Emitter:
    def __init__(self, nc):
        self.nc = nc
        self.waited = {}

    def wait(self, eng, *evs):
        for ev in evs:
            if ev is None:
                continue
            if isinstance(ev, (list, tuple)):
                self.wait(eng, *ev)
                continue
            key = (id(eng), ev.ch)
            if self.waited.get(key, 0) >= ev.val:
                continue
            eng.wait_ge(ev.ch.sem, ev.val)
            self.waited[key] = ev.val


def build(S):
    NT = S // 128
    NG = S // 512
    NF = 8 * NT
    NCH = max(1, NF // 128)
    CHP = min(128, NF)

    nc = bass.Bass("TRN2", target_bir_lowering=False)
    x = nc.dram_tensor("x", [S, D], F32, kind="ExternalInput").ap()
    w2 = nc.dram_tensor("w2", [128, 8, NW], F32, kind="ExternalInput").ap()
    wbr = nc.dram_tensor("wbr", [128, 8, D], F32, kind="ExternalInput").ap()
    wout = nc.dram_tensor("wout", [128, 8, D], F32, kind="ExternalInput").ap()
    prm_d = nc.dram_tensor("prm", [128, NPRM], F32, kind="ExternalInput").ap()
    cb_d = nc.dram_tensor("cb", [128, 2304], BF16, kind="ExternalInput").ap()
    cf_d = nc.dram_tensor("cf", [128, 384], F32, kind="ExternalInput").ap()
    ropec_d = nc.dram_tensor("ropec", [128, S], F32, kind="ExternalInput").ap()
    ropes_d = nc.dram_tensor("ropes", [128, S], F32, kind="ExternalInput").ap()
    out_d = nc.dram_tensor("out", [S, D], F32, kind="ExternalOutput").ap()

    QF = nc.dram_tensor("QF", [512, S], BF16, kind="Internal").ap()
    KF = nc.dram_tensor("KF", [512, S], BF16, kind="Internal").ap()
    VF = nc.dram_tensor("VF", [S, 512], BF16, kind="Internal").ap()
    VD = nc.dram_tensor("VD", [S, 512], BF16, kind="Internal").ap()
    ZT = nc.dram_tensor("ZT", [1024, S], BF16, kind="Internal").ap()
    GT = nc.dram_tensor("GT", [2048, S], BF16, kind="Internal").ap()
    QD = nc.dram_tensor("QD", [512, S], BF16, kind="Internal").ap()
    KD = nc.dram_tensor("KD", [512, S], BF16, kind="Internal").ap()
    YT = nc.dram_tensor("YT", [1024, S], BF16, kind="Internal").ap()
    AUG = nc.dram_tensor("AUG", [4, 8 * S], BF16, kind="Internal").ap()

    em = Emitter(nc)
    W = em.wait
    PE, ACT, DVE, POOL, SP = nc.tensor, nc.scalar, nc.vector, nc.gpsimd, nc.sync

    with ExitStack() as g:
        g.enter_context(nc.allow_low_precision("bf16 matmul operands, fp32 accumulation"))
        g.enter_context(nc.allow_non_contiguous_dma("small strided V loads"))
        PRM = g.enter_context(nc.sbuf_tensor("PRM", [128, 274], F32))
        CB = g.enter_context(nc.sbuf_tensor("CB", [128, 2304], BF16))
        CF = g.enter_context(nc.sbuf_tensor("CF", [128, 384], F32))
        FA = g.enter_context(nc.sbuf_tensor("FA", [128, 8, NT], F32))
        SM = g.enter_context(nc.sbuf_tensor("SM", [128, 16], F32))
        identb = CB[:, 0:128]
        onesb = CB[:, 128:256]
        NEGLAM = SM[:, 0:1]
        GSUBC = SM[:, 1:2]
        EPS6 = SM[:, 2:3]
        EPS5 = SM[:, 3:4]
        ONE = SM[:, 4:5]

        def mask(j):
            return CB[:, 256 + j * 512:256 + (j + 1) * 512]

        with ExitStack() as es:
            sbt = lambda n, s, d: es.enter_context(nc.sbuf_tensor(n, s, d))
            sem = lambda n: es.enter_context(nc.semaphore(n))
            Wb = sbt("Wb", [128, 8, NW], BF16)
            wst = [sbt(f"wst{i}", [128, WCH], F32) for i in range(6)]
            xt = [sbt(f"xt{i}", [128, D], F32) for i in range(2)]
            hb = [sbt(f"hb{i}", [128, D], BF16) for i in range(2)]
            hT = [sbt(f"hT{i}", [128, 8, 512], BF16) for i in range(2)]
            rc = [sbt(f"rc{i}", [128, 512], F32) for i in range(2)]
            rs = [sbt(f"rs{i}", [128, 512], F32) for i in range(2)]
            NS = 6
            stage = [sbt(f"stg{i}", [128, 512], BF16) for i in range(NS)]
            tmp1 = sbt("tmp1", [128, 512], F32)
            tmp2 = sbt("tmp2", [128, 512], F32)
            junk = sbt("junk", [128, D], BF16)
            ss = sbt("ss", [128, NT], F32)
            lnv = sbt("lnv", [128, NT], F32)
            rstd = sbt("rstd", [128, NT], F32)
            lt = sbt("lt", [128, 64], F32)
            psm = es.enter_context(nc.psum_tensor("psm", [128, 4, 512], F32))
            psT = es.enter_context(nc.psum_tensor("psT", [128, 2, 8, 128], BF16))
            psf = es.enter_context(nc.psum_tensor("psf", [128, 512], F32))
            cPE, cACT, cDVE, cPOOL = (Chan(sem(n)) for n in ("a_pe", "a_act", "a_dve", "a_pool"))
            cLD0 = Chan(sem("a_ld0"), 16)
            cWL = [Chan(sem(f"a_wl{i}"), 16) for i in range(6)]
            cXL = [Chan(sem(f"a_xl{i}"), 16) for i in range(2)]
            cRL = [Chan(sem(f"a_rl{i}"), 16) for i in range(2)]
            cST = [Chan(sem(f"a_st{i}"), 16) for i in range(NS)]
            block = es.enter_context(nc.Block())

            @block.sync
            def _(_sync):
                SP.dma_start(out=PRM[:, :], in_=prm_d[:, 0:274]).then_inc(cLD0.sem, 16)
                SP.dma_start(out=CB[:, :], in_=cb_d[:, :]).then_inc(cLD0.sem, 16)
                ins = SP.dma_start(out=CF[:, :], in_=cf_d[:, :])
                cLD0.val = 32
                ev0 = cLD0.inc(ins)
                W(DVE, ev0)
                DVE.memset(SM[:, 2:3], NORM_EPS)
                DVE.memset(SM[:, 3:4], SUBLN_EPS)
                DVE.memset(SM[:, 4:5], 1.0)
                W(DVE, cDVE.inc(DVE.tensor_tensor(out=lt[:, :], in0=PRM[:, 16:80], in1=PRM[:, 80:144], op=ALU.mult)))
                ins = DVE.tensor_reduce(out=SM[:, 5:6], in_=lt[:, :], axis=mybir.AxisListType.X, op=ALU.add)
                e = cDVE.inc(ins)
                W(DVE, e)
                W(DVE, cDVE.inc(DVE.tensor_tensor(out=lt[:, :], in0=PRM[:, 144:208], in1=PRM[:, 208:272], op=ALU.mult)))
                ins = DVE.tensor_reduce(out=SM[:, 6:7], in_=lt[:, :], axis=mybir.AxisListType.X, op=ALU.add)
                e = cDVE.inc(ins)
                W(ACT, e)
                ins = ACT.activation(out=SM[:, 7:9], in_=SM[:, 5:7], func=AF.Exp)
                e = cACT.inc(ins)
                W(DVE, e)
                ins = DVE.tensor_tensor(out=SM[:, 9:10], in0=SM[:, 8:9], in1=SM[:, 7:8], op=ALU.subtract)
                e = cDVE.inc(ins)
                W(DVE, e)
                DVE.tensor_scalar(out=SM[:, 0:1], in0=SM[:, 9:10], scalar1=-LAM_INIT, scalar2=None, op0=ALU.add)
                ins = DVE.tensor_scalar(out=SM[:, 1:2], in0=PRM[:, 272:273], scalar1=1.0 - LAM_INIT, scalar2=None,
                                        op0=ALU.mult)
                ev_sm = cDVE.inc(ins)
                W(ACT, ev_sm)
                W(POOL, ev0)

                cast_ev = [None] * 6
                last_cast = {}
                chunk_ev = {}
                k = 0
                for q in range(NW // WCH):
                    for c in range(8):
                        sl = k % 6
                        W(SP, cast_ev[sl])
                        ev = cWL[sl].inc(SP.dma_start(out=wst[sl][:, :], in_=w2[:, c, q * WCH:(q + 1) * WCH]))
                        eng, ch = (DVE, cDVE) if k % 2 == 0 else (POOL, cPOOL)
                        W(eng, ev)
                        ins = eng.tensor_scalar(out=Wb[:, c, q * WCH:(q + 1) * WCH], in0=wst[sl][:, :],
                                                scalar1=PRM[:, c:c + 1], scalar2=None, op0=ALU.mult)
                        cast_ev[sl] = ch.inc(ins)
                        last_cast[k % 2] = cast_ev[sl]
                        chunk_ev[q] = [last_cast.get(0), last_cast.get(1)]
                        k += 1
                W(PE, ev0)

                def wready(col_hi):
                    W(PE, chunk_ev[(col_hi - 1) // WCH])

                xld_ev = {}
                sq_ev = {}
                rstd_ev = {}
                x_free = [None, None]
                tr_ev = {}
                trc_ev = {}
                grp_mm_ev = {}
                rl_ev = {}
                rope_done = {}

                def tile_load(T):
                    sl = T % 2
                    W(SP, x_free[sl])
                    xld_ev[T] = cXL[sl].inc(SP.dma_start(out=xt[sl][:, :], in_=x[T * 128:(T + 1) * 128, :]))
                    W(ACT, xld_ev[T], sq_ev.get(T - 1))
                    ins = ACT.activation(out=junk[:, :], in_=xt[sl][:, :], func=AF.Square, accum_out=ss[:, T:T + 1])
                    sq_ev[T] = cACT.inc(ins)

                def tile_stats(T):
                    W(ACT, sq_ev[T])
                    ins = ACT.activation(out=lnv[:, T:T + 1], in_=ss[:, T:T + 1], func=AF.Ln, scale=1.0 / D,
                                         bias=EPS6)
                    e1 = cACT.inc(ins)
                    W(ACT, e1)
                    ins = ACT.activation(out=rstd[:, T:T + 1], in_=lnv[:, T:T + 1], func=AF.Exp, scale=-0.5)
                    rstd_ev[T] = cACT.inc(ins)

                def tile_h(T):
                    G, i = divmod(T, 4)
                    sl = T % 2
                    W(DVE, rstd_ev[T], xld_ev[T], tr_ev.get(T - 2))
                    ins = DVE.tensor_scalar(out=hb[sl][:, :], in0=xt[sl][:, :], scalar1=rstd[:, T:T + 1],
                                            scalar2=None, op0=ALU.mult)
                    h_ev = cDVE.inc(ins)
                    x_free[sl] = [h_ev, sq_ev[T]]
                    W(PE, h_ev, trc_ev.get(T - 2))
                    for c in range(8):
                        ins = PE.transpose(out=psT[:, sl, c, :], in_=hb[sl][:, c * 128:(c + 1) * 128],
                                           identity=identb)
                    tr_ev[T] = cPE.inc(ins)
                    W(DVE, tr_ev[T], grp_mm_ev.get(G - 2))
                    ins = DVE.tensor_copy(out=hT[G % 2][:, :, i * 128:(i + 1) * 128], in_=psT[:, sl, :, :])
                    trc_ev[T] = cDVE.inc(ins)

                def rope_load(G):
                    sl = G % 2
                    W(SP, rope_done.get(G - 2))
                    cRL[sl].inc(SP.dma_start(out=rc[sl][:, :], in_=ropec_d[:, G * 512:(G + 1) * 512]))
                    rl_ev[G] = cRL[sl].inc(SP.dma_start(out=rs[sl][:, :], in_=ropes_d[:, G * 512:(G + 1) * 512]))

                state = {"pc": 0, "sc": 0}
                bank_free = [None] * 4
                stage_free = [None] * NS
                tmp_free = [None]
                fa_free = [None]

                def mm_fm(G, blk, bank):
                    wready((blk + 1) * 128)
                    W(PE, bank_free[bank])
                    for c in range(8):
                        ins = PE.matmul(psm[:, bank, :], lhsT=Wb[:, c, blk * 128:(blk + 1) * 128],
                                        rhs=hT[G % 2][:, c, :], start=(c == 0), stop=(c == 7))
                    return cPE.inc(ins)

                def mm_tm(G, i, col, bank):
                    wready(col + 512)
                    W(PE, bank_free[bank])
                    for c in range(8):
                        ins = PE.matmul(psm[:, bank, :], lhsT=hT[G % 2][:, c, i * 128:(i + 1) * 128],
                                        rhs=Wb[:, c, col:col + 512], start=(c == 0), stop=(c == 7))
                    return cPE.inc(ins)

                def store(slot, ev, dst):
                    W(SP, ev)
                    stage_free[slot] = cST[slot].inc(SP.dma_start(out=dst, in_=stage[slot][:, :]))

                def nbank():
                    b = state["pc"] % 4
                    state["pc"] += 1
                    return b

                def nslot():
                    s = state["sc"] % NS
                    state["sc"] += 1
                    return s

                def group_mm(G):
                    W(PE, trc_ev[G * 4 + 3])
                    gs = slice(G * 512, (G + 1) * 512)
                    last_pe = None
                    fm = []
                    for b in range(4):
                        fm.append((b, "copy", QF[b * 128:(b + 1) * 128, gs]))
                    for b in range(4):
                        fm.append((4 + b, "copy", KF[b * 128:(b + 1) * 128, gs]))
                    for b in range(8):
                        fm.append((8 + b, "silu", ZT[b * 128:(b + 1) * 128, gs]))
                    for b in range(16):
                        fm.append((16 + b, "sigm", GT[b * 128:(b + 1) * 128, gs]))
                    for (blk, kind, dst) in fm:
                        bank = nbank()
                        mev = mm_fm(G, blk, bank)
                        slot = nslot()
                        if kind == "copy":
                            W(DVE, mev, stage_free[slot])
                            ins = DVE.tensor_copy(out=stage[slot][:, :], in_=psm[:, bank, :])
                            eev = cDVE.inc(ins)
                        else:
                            W(ACT, mev, stage_free[slot])
                            ins = ACT.activation(out=stage[slot][:, :], in_=psm[:, bank, :],
                                                 func=AF.Silu if kind == "silu" else AF.Sigmoid)
                            eev = cACT.inc(ins)
                        bank_free[bank] = eev
                        store(slot, eev, dst)
                    for b in range(8):
                        dram = QD if b < 4 else KD
                        r0 = (b % 4) * 128
                        bankA = nbank()
                        mevA = mm_fm(G, 32 + b, bankA)
                        slot = nslot()
                        W(DVE, mevA, rl_ev[G], tmp_free[0])
                        DVE.tensor_tensor(out=tmp1[:, :], in0=psm[:, bankA, :], in1=rc[G % 2][:, :], op=ALU.mult)
                        DVE.tensor_tensor(out=tmp2[0:64, :], in0=psm[64:128, bankA, :], in1=rs[G % 2][0:64, :],
                                          op=ALU.mult)
                        ins = DVE.tensor_tensor(out=tmp2[64:128, :], in0=psm[0:64, bankA, :],
                                                in1=rs[G % 2][64:128, :], op=ALU.mult)
                        e2 = cDVE.inc(ins)
                        bank_free[bankA] = e2
                        W(POOL, e2, stage_free[slot])
                        ins = POOL.tensor_tensor(out=stage[slot][:, :], in0=tmp1[:, :], in1=tmp2[:, :], op=ALU.add)
                        e3 = cPOOL.inc(ins)
                        tmp_free[0] = e3
                        W(SP, e3)
                        for (p0, d0) in ((0, 0), (32, 64), (64, 32), (96, 96)):
                            stage_free[slot] = cST[slot].inc(SP.dma_start(
                                out=dram[r0 + d0:r0 + d0 + 32, gs], in_=stage[slot][p0:p0 + 32, :]))
                    rope_done[G] = e2
                    W(PE, fa_free[0])
                    for i in range(4):
                        T = G * 4 + i
                        for (col, dstT) in ((COL_VA, VF), (COL_VB, VD)):
                            bank = nbank()
                            mev = mm_tm(G, i, col, bank)
                            slot = nslot()
                            W(ACT, mev, stage_free[slot])
                            ins = ACT.activation(out=stage[slot][:, :], in_=psm[:, bank, :], func=AF.Copy)
                            eev = cACT.inc(ins)
                            bank_free[bank] = eev
                            store(slot, eev, dstT[T * 128:(T + 1) * 128, :])
                        wready(NW)
                        for c in range(8):
                            ins = PE.matmul(psf[:, i * 8:(i + 1) * 8], lhsT=hT[G % 2][:, c, i * 128:(i + 1) * 128],
                                            rhs=Wb[:, c, COL_FA:COL_FA + 8], start=(c == 0), stop=(c == 7))
                        last_pe = cPE.inc(ins)
                    grp_mm_ev[G] = last_pe
                    W(DVE, last_pe)
                    for i in range(4):
                        ins = DVE.tensor_copy(out=FA[:, :, G * 4 + i], in_=psf[:, i * 8:(i + 1) * 8])
                    fa_free[0] = cDVE.inc(ins)

                def prep_group(G):
                    rope_load(G)
                    for i in range(4):
                        T = G * 4 + i
                        tile_load(T)
                        tile_stats(T)
                        tile_h(T)

                prep_group(0)
                for G in range(NG):
                    if G + 1 < NG:
                        prep_group(G + 1)
                    group_mm(G)
                W(SP, *stage_free)
                W(SP, fa_free[0])

        with ExitStack() as es:
            sbt = lambda n, s, d: es.enter_context(nc.sbuf_tensor(n, s, d))
            sem = lambda n: es.enter_context(nc.semaphore(n))
            E1 = sbt("E1", [128, NF], F32)
            LS = sbt("LS", [128, NF], F32)
            TOT = sbt("TOT", [128, 8, NT], F32)
            INC = sbt("INC", [128, 8, NT], F32)
            N1 = sbt("N1", [128, NF], F32)
            N8 = sbt("N8", [128, NF], F32)
            R1 = sbt("R1", [128, NF], F32)
            R2 = sbt("R2", [128, NF], F32)
            AUG4 = sbt("AUG4", [128, 4, NF], BF16)
            AUGT = sbt("AUGT", [128, 4 * NCH, 128], BF16)
            ps = es.enter_context(nc.psum_tensor("ps2", [128, 2, 512], F32))
            psA = es.enter_context(nc.psum_tensor("psA", [128, 4 * NCH, 128], BF16))
            cPE, cACT, cDVE = (Chan(sem(n)) for n in ("b_pe", "b_act", "b_dve"))
            cST = Chan(sem("b_st"), 16)
            block = es.enter_context(nc.Block())

            @block.sync
            def _(_sync):
                FAf = FA[:, :, :].rearrange("p h t -> p (h t)")
                for h in range(8):
                    ins = DVE.tensor_scalar(out=FA[:, h, :], in0=FA[:, h, :], scalar1=PRM[:, 8 + h:9 + h],
                                            scalar2=None, op0=ALU.add)
                e = cDVE.inc(ins)
                W(ACT, e)
                e = cACT.inc(ACT.activation(out=E1[:, :], in_=FAf, func=AF.Exp, scale=-1.0))
                W(ACT, e)
                e = cACT.inc(ACT.activation(out=LS[:, :], in_=E1[:, :], func=AF.Ln, bias=ONE, scale=1.0))
                W(PE, e)
                PE.matmul(ps[:, 0, 0:NF], lhsT=CF[:, 0:128], rhs=LS[:, :], start=True, stop=True)
                e = cPE.inc(PE.matmul(ps[:, 1, 0:NF], lhsT=CF[:, 128:256], rhs=LS[:, :], start=True, stop=True))
                W(DVE, e)
                TOTf = TOT[:, :, :].rearrange("p h t -> p (h t)")
                INCf = INC[:, :, :].rearrange("p h t -> p (h t)")
                e = cDVE.inc(DVE.tensor_copy(out=TOTf, in_=ps[:, 1, 0:NF]))
                W(DVE, e)
                for h in range(8):
                    ins = DVE.tensor_tensor_scan(out=INC[:, h, :], data0=CF[:, 128:128 + NT], data1=TOT[:, h, :],
                                                 initial=0.0, op0=ALU.mult, op1=ALU.add)
                e = cDVE.inc(ins)
                W(DVE, e)
                e = cDVE.inc(DVE.tensor_tensor(out=N1[:, :], in0=ps[:, 0, 0:NF], in1=INCf, op=ALU.add))
                W(DVE, e)
                e = cDVE.inc(DVE.tensor_tensor(out=N8[:, :], in0=N1[:, :], in1=TOTf, op=ALU.subtract))
                W(DVE, e)
                e = cDVE.inc(DVE.tensor_scalar(out=N8[:, :], in0=N8[:, :], scalar1=8.0, scalar2=None, op0=ALU.mult))
                W(DVE, e)
                e = cDVE.inc(DVE.tensor_copy(out=AUG4[:, 1, :], in_=N8[:, :]))
                W(DVE, e)
                e = cDVE.inc(DVE.tensor_tensor(out=R1[:, :], in0=N8[:, :], in1=AUG4[:, 1, :], op=ALU.subtract))
                W(DVE, e)
                e = cDVE.inc(DVE.tensor_copy(out=AUG4[:, 2, :], in_=R1[:, :]))
                W(DVE, e)
                e = cDVE.inc(DVE.tensor_tensor(out=R2[:, :], in0=R1[:, :], in1=AUG4[:, 2, :], op=ALU.subtract))
                W(DVE, e)
                DVE.tensor_copy(out=AUG4[:, 3, :], in_=R2[:, :])
                e = cDVE.inc(DVE.tensor_scalar(out=AUG4[:, 0, :], in0=AUG4[:, 1, :], scalar1=-1.0, scalar2=None,
                                               op0=ALU.mult))
                W(PE, e)
                for r in range(4):
                    for i in range(NCH):
                        ins = PE.transpose(out=psA[0:CHP, r * NCH + i, :], in_=AUG4[:, r, i * 128:i * 128 + CHP],
                                           identity=identb)
                e = cPE.inc(ins)
                W(DVE, e)
                e = cDVE.inc(DVE.tensor_copy(out=AUGT[0:CHP, :, :], in_=psA[0:CHP, :, :]))
                W(SP, e)
                for r in range(4):
                    for i in range(NCH):
                        dst = AUG[r:r + 1, i * 128 * 128:i * 128 * 128 + CHP * 128].rearrange(
                            "o (p t) -> (o p) t", t=128)
                        ev_st = cST.inc(SP.dma_start(out=dst, in_=AUGT[0:CHP, r * NCH + i, :]))
                W(SP, ev_st)

        def attention(kind):
            fox = kind == "fox"
            NH = 8 if fox else 4
            BW = 3 if fox else 2
            NPB = 4
            KR = 68 if fox else 64
            with ExitStack() as es:
                sbt = lambda n, s, d: es.enter_context(nc.sbuf_tensor(kind + "_" + n, s, d))
                sem = lambda n: es.enter_context(nc.semaphore(kind + "_" + n))
                QT = [sbt(f"QT{i}", [128, S], BF16) for i in range(2)]
                KT = [sbt(f"KT{i}", [128, S], BF16) for i in range(2)]
                V = [sbt(f"V{i}", [128, NT, 128], BF16) for i in range(2)]
                Z = [sbt(f"Z{i}", [128, S], BF16) for i in range(2)]
                P = [sbt(f"P{i}", [128, BW, 512], BF16) for i in range(NPB)]
                Rt = sbt("Rt", [128, 512], F32)
                Tt = [sbt(f"Tt{i}", [128, 512], F32) for i in range(2)]
                NYS = 2
                Yst = [sbt(f"Yst{i}", [128, 512], BF16) for i in range(NYS)]
                if not fox:
                    Bt = sbt("Bt", [128, 512], F32)
                    Of = sbt("Of", [128, 512], F32)
                    SQ = sbt("SQ", [128, 512], BF16)
                    LNV = sbt("LNV", [128, 512], F32)
                    RS = sbt("RS", [128, 512], F32)
                    Y1 = sbt("Y1", [128, 512], F32)
                ps = es.enter_context(nc.psum_tensor(kind + "_psat", [128, 8, 512], F32))
                cPE, cACT, cDVE, cPOOL = (Chan(sem(n)) for n in ("c_pe", "c_act", "c_dve", "c_pool"))
                cLD = [Chan(sem(f"c_ld{i}"), 16) for i in range(2)]
                cST = [Chan(sem(f"c_st{i}"), 16) for i in range(NYS)]
                block = es.enter_context(nc.Block())

                @block.sync
                def _(_sync):
                    init_ev = None
                    if fox:
                        for b in range(2):
                            DVE.memset(QT[b][64:68, :], 1.0)
                            DVE.memset(KT[b][64:68, :], 1.0)
                            ins = POOL.memset(V[b][:, :, 64:128], 1.0)
                        init_ev = [cDVE.inc(DVE.memset(Rt[:, :], 1.0)), cPOOL.inc(ins)]
                    head_free = [None, None]
                    ld_ev = {}

                    def load_head(h):
                        b = h % 2
                        W(SP, head_free[b], init_ev)
                        c = cLD[b]
                        if fox:
                            c.inc(SP.dma_start(out=QT[b][0:64, :], in_=QF[h * 64:(h + 1) * 64, :]))
                            c.inc(SP.dma_start(out=QT[b][64:65, :], in_=AUG[0:1, h * S:(h + 1) * S]))
                            c.inc(SP.dma_start(out=KT[b][0:64, :], in_=KF[h * 64:(h + 1) * 64, :]))
                            c.inc(SP.dma_start(out=KT[b][65:68, :], in_=AUG[1:4, h * S:(h + 1) * S]))
                            vsrc = VF.rearrange("(n p) c -> p n c", p=128)
                            nsp = 4 if NT >= 4 else 1
                            for q in range(nsp):
                                a0, a1 = q * NT // nsp, (q + 1) * NT // nsp
                                c.inc(SP.dma_start(out=V[b][:, a0:a1, 0:64], in_=vsrc[:, a0:a1, h * 64:(h + 1) * 64]))
                            ld_ev[h] = c.inc(SP.dma_start(out=Z[b][0:64, :], in_=ZT[h * 64:(h + 1) * 64, :]))
                        else:
                            c.inc(SP.dma_start(out=QT[b][:, :], in_=QD[h * 128:(h + 1) * 128, :]))
                            c.inc(SP.dma_start(out=KT[b][:, :], in_=KD[h * 128:(h + 1) * 128, :]))
                            vsrc = VD.rearrange("(n p) c -> p n c", p=128)
                            nsp = 4 if NT >= 4 else 1
                            for q in range(nsp):
                                a0, a1 = q * NT // nsp, (q + 1) * NT // nsp
                                c.inc(SP.dma_start(out=V[b][:, a0:a1, :], in_=vsrc[:, a0:a1, h * 128:(h + 1) * 128]))
                            ld_ev[h] = c.inc(SP.dma_start(out=Z[b][:, :],
                                                          in_=ZT[512 + h * 128:512 + (h + 1) * 128, :]))

                    units = []
                    batches = []
                    for h in range(NH):
                        for g in range(NG):
                            u = len(units)
                            nk = 4 * (g + 1)
                            units.append(dict(h=h, g=g, nk=nk))
                            if fox:
                                blks = [(kt, 0) for kt in range(nk)]
                            else:
                                blks = [(kt, m) for kt in range(nk) for m in range(2)]
                            for s0 in range(0, len(blks), BW):
                                batches.append(dict(u=u, blks=blks[s0:s0 + BW], first=(s0 == 0),
                                                    last=(s0 + BW >= len(blks))))
                    NB = len(batches)
                    qk_ev = {}
                    exp_ev = {}
                    mask_ev = {}
                    pv_ev = {}
                    acc_free = {}
                    deferred = {}
                    st_free = [None] * NYS
                    head_last_pe = {}
                    head_last_epi = {}
                    misc = {"ys": 0, "of_free": None, "sub_free": None, "tt_free": [None, None], "loaded": -1}

                    def rows(m):
                        return slice(0, KR) if (fox or m == 0) else slice(64, 128)

                    def acc_banks(u):
                        if fox:
                            return (6 + u % 2,)
                        return (4, 5, 6)

                    def emit_qk(n):
                        bt = batches[n]
                        un = units[bt["u"]]
                        h, g = un["h"], un["g"]
                        b = h % 2
                        W(PE, ld_ev[h], exp_ev.get(n - 2))
                        sbase = (n % 2) * BW
                        for j, (kt, m) in enumerate(bt["blks"]):
                            band = kt >= 4 * g
                            c0 = 128 * (kt - 4 * g) if band else 0
                            r = rows(m)
                            ins = PE.matmul(ps[:, sbase + j, c0:512], lhsT=KT[b][r, kt * 128:(kt + 1) * 128],
                                            rhs=QT[b][r, g * 512 + c0:(g + 1) * 512], start=True, stop=True)
                        for j, (kt, m) in enumerate(bt["blks"]):
                            if kt >= 4 * g:
                                c1 = 128 * (kt - 4 * g + 1)
                                ins = PE.matmul(ps[:, sbase + j, 0:c1], lhsT=identb, rhs=mask(kt - 4 * g)[:, 0:c1],
                                                start=False, stop=True, skip_group_check=True)
                        qk_ev[n] = cPE.inc(ins)

                    def emit_exp(n):
                        bt = batches[n]
                        nb = len(bt["blks"])
                        sbase = (n % 2) * BW
                        W(ACT, qk_ev[n], pv_ev.get(n - NPB))
                        ins = ACT.activation(out=P[n % NPB][:, 0:nb, :], in_=ps[:, sbase:sbase + nb, :], func=AF.Exp,
                                             scale=0.125)
                        exp_ev[n] = cACT.inc(ins)

                    def emit_mask(n):
                        bt = batches[n]
                        un = units[bt["u"]]
                        g = un["g"]
                        ins = None
                        for j, kt in enumerate(bt["kts"]):
                            if kt >= 4 * g:
                                W(POOL, exp_ev[n])
                                ins = POOL.tensor_tensor(out=P[n % NPB][:, j, :], in0=P[n % NPB][:, j, :],
                                                         in1=mask(kt - 4 * g), op=ALU.mult)
                        if ins is not None:
                            mask_ev[n] = cPOOL.inc(ins)

                    def emit_pv(n):
                        bt = batches[n]
                        u = bt["u"]
                        un = units[u]
                        h, nk = un["h"], un["nk"]
                        b = h % 2
                        banks = acc_banks(u)
                        W(PE, exp_ev[n], mask_ev.get(n))
                        if bt["first"]:
                            W(PE, acc_free.get(banks[0]))
                        g = un["g"]
                        for j, (kt, m) in enumerate(bt["blks"]):
                            c0 = 128 * (kt - 4 * g) if kt >= 4 * g else 0
                            ins = PE.matmul(ps[:, banks[m], c0:512], lhsT=V[b][:, kt, :], rhs=P[n % NPB][:, j, c0:512],
                                            start=(kt == 0), stop=(kt == nk - 1))
                        if not fox:
                            for j, (kt, m) in enumerate(bt["blks"]):
                                c0 = 128 * (kt - 4 * g) if kt >= 4 * g else 0
                                ins = PE.matmul(ps[64 * m:64 * m + 64, banks[2], c0:512], lhsT=onesb[:, 0:64],
                                                rhs=P[n % NPB][:, j, c0:512], start=(kt == 0), stop=(kt == nk - 1),
                                                tile_position=(0, 64 * m))
                        pv_ev[n] = cPE.inc(ins)
                        head_last_pe[h] = pv_ev[n]

                    def ystore(h, g, e, nrow, row0):
                        ys = misc["ys"] % NYS
                        return ys

                    def epilogue_fox(n):
                        u = batches[n]["u"]
                        un = units[u]
                        h, g = un["h"], un["g"]
                        b = h % 2
                        a = acc_banks(u)[0]
                        tt = u % 2
                        W(DVE, pv_ev[n], misc.get("dl"))
                        e1 = cDVE.inc(DVE.reciprocal(out=Rt[64:128, :], in_=ps[64:128, a, :]))
                        W(DVE, e1, misc["tt_free"][tt])
                        e2 = cDVE.inc(DVE.tensor_tensor(out=Tt[tt][0:64, :], in0=ps[0:64, a, :], in1=Rt[64:128, :],
                                                        op=ALU.mult))
                        acc_free[a] = e2
                        misc["dl"] = e2
                        ys = misc["ys"] % NYS
                        misc["ys"] += 1
                        W(POOL, e2, st_free[ys])
                        e3 = cPOOL.inc(POOL.tensor_tensor(out=Yst[ys][0:64, :], in0=Tt[tt][0:64, :],
                                                          in1=Z[b][0:64, g * 512:(g + 1) * 512], op=ALU.mult))
                        misc["tt_free"][tt] = e3
                        head_last_epi[h] = e3
                        W(SP, e3)
                        st_free[ys] = cST[ys].inc(SP.dma_start(out=YT[h * 64:(h + 1) * 64, g * 512:(g + 1) * 512],
                                                               in_=Yst[ys][0:64, :]))

                    def epilogue_diff(n):
                        u = batches[n]["u"]
                        un = units[u]
                        h, g = un["h"], un["g"]
                        b = h % 2
                        A = Tt[0]
                        W(DVE, pv_ev[n], misc.get("dl"))
                        e1 = cDVE.inc(DVE.reciprocal(out=Rt[:, :], in_=ps[:, 6, :]))
                        W(DVE, e1)
                        DVE.tensor_tensor(out=A[0:64, :], in0=ps[0:64, 4, :], in1=Rt[0:64, :], op=ALU.mult)
                        DVE.tensor_tensor(out=A[64:128, :], in0=ps[64:128, 4, :], in1=Rt[0:64, :], op=ALU.mult)
                        DVE.tensor_tensor(out=Bt[0:64, :], in0=ps[0:64, 5, :], in1=Rt[64:128, :], op=ALU.mult)
                        eB = cDVE.inc(DVE.tensor_tensor(out=Bt[64:128, :], in0=ps[64:128, 5, :],
                                                        in1=Rt[64:128, :], op=ALU.mult))
                        acc_free[4] = eB
                        W(DVE, eB, misc["of_free"], misc.get("sq_done"))
                        eO = cDVE.inc(DVE.scalar_tensor_tensor(out=Of[:, :], in0=Bt[:, :], scalar=NEGLAM,
                                                               in1=A[:, :], op0=ALU.mult, op1=ALU.add))
                        misc["dl"] = eO
                        W(POOL, eO, misc.get("sq_read"), misc.get("pl"))
                        eS = cPOOL.inc(POOL.tensor_tensor(out=SQ[:, :], in0=Of[:, :], in1=Of[:, :], op=ALU.mult))
                        misc["sq_done"] = eS
                        misc["pl"] = eS

                        def pe_part():
                            W(PE, eS, misc["sub_free"])
                            ePS = cPE.inc(PE.matmul(ps[:, 7, :], lhsT=onesb, rhs=SQ[:, :], start=True, stop=True))
                            head_last_pe[h] = ePS
                            misc["sq_read"] = ePS

                            def act_part():
                                W(ACT, ePS, misc.get("al"))
                                eL = cACT.inc(ACT.activation(out=LNV[:, :], in_=ps[:, 7, :], func=AF.Ln,
                                                             scale=1.0 / 128.0, bias=EPS5))
                                misc["sub_free"] = eL
                                W(ACT, eL, misc["of_free"])
                                eR = cACT.inc(ACT.activation(out=RS[:, :], in_=LNV[:, :], func=AF.Exp, scale=-0.5))
                                misc["al"] = eR
                                W(DVE, eR, misc.get("dl"), misc.get("y1_read"))
                                eY1 = cDVE.inc(DVE.scalar_tensor_tensor(out=Y1[:, :], in0=Of[:, :], scalar=GSUBC,
                                                                        in1=RS[:, :], op0=ALU.mult, op1=ALU.mult))
                                misc["of_free"] = eY1
                                misc["dl"] = eY1
                                ys = misc["ys"] % NYS
                                misc["ys"] += 1
                                W(POOL, eY1, st_free[ys], misc.get("pl"))
                                eY = cPOOL.inc(POOL.tensor_tensor(out=Yst[ys][:, :], in0=Y1[:, :],
                                                                  in1=Z[b][:, g * 512:(g + 1) * 512], op=ALU.mult))
                                misc["y1_read"] = eY
                                misc["pl"] = eY
                                head_last_epi[h] = eY
                                W(SP, eY)
                                st_free[ys] = cST[ys].inc(SP.dma_start(
                                    out=YT[512 + h * 128:512 + (h + 1) * 128, g * 512:(g + 1) * 512],
                                    in_=Yst[ys][:, :]))

                            deferred.setdefault(n + 3, []).append(act_part)

                        deferred.setdefault(n + 2, []).append(pe_part)

                    def run_deferred(n):
                        while True:
                            ks = sorted([k for k in deferred if k <= n])
                            if not ks:
                                break
                            for fn in deferred.pop(ks[0]):
                                fn()

                    def ensure_loaded(h):
                        while misc["loaded"] < min(h, NH - 1):
                            hh = misc["loaded"] + 1
                            if hh >= 2:
                                head_free[hh % 2] = [head_last_pe.get(hh - 2), head_last_epi.get(hh - 2)]
                            load_head(hh)
                            misc["loaded"] = hh

                    ensure_loaded(1)
                    emit_qk(0)
                    for n in range(NB):
                        run_deferred(n)
                        if n + 1 < NB:
                            hn = units[batches[n + 1]["u"]]["h"]
                            emit_qk(n + 1)
                        emit_exp(n)
                        emit_pv(n)
                        if batches[n]["last"]:
                            (epilogue_fox if fox else epilogue_diff)(n)
                            un = units[batches[n]["u"]]
                            if un["g"] == NG - 1:
                                if not fox:
                                    run_deferred(NB + 10)
                                if un["h"] + 2 < NH:
                                    ensure_loaded(un["h"] + 2)
                    run_deferred(NB + 10)
                    W(SP, *st_free)

        attention("fox")
        attention("diff")

        with ExitStack() as es:
            sbt = lambda n, s, d: es.enter_context(nc.sbuf_tensor(n, s, d))
            sem = lambda n: es.enter_context(nc.semaphore(n))
            WB = sbt("WB", [128, 8, D], BF16)
            WO = sbt("WO", [128, 8, D], BF16)
            GP = sbt("GP", [128, D], F32)
            wst = [sbt(f"dwst{i}", [128, D], F32) for i in range(2)]
            Yin = [sbt(f"Yin{i}", [128, 8, 512], BF16) for i in range(2)]
            Gin = [sbt(f"Gin{i}", [128, 16, 512], BF16) for i in range(2)]
            MT = [sbt(f"MT{i}", [128, 8, 512], BF16) for i in range(2)]
            t1 = [sbt(f"dt1{i}", [128, 512], F32) for i in range(2)]
            t2 = [sbt(f"dt2{i}", [128, 512], F32) for i in range(2)]
            xin = [sbt(f"xin{i}", [128, D], F32) for i in range(2)]
            yn = [sbt(f"yn{i}", [128, D], F32) for i in range(2)]
            ob = [sbt(f"ob{i}", [128, D], F32) for i in range(2)]
            junk = sbt("djunk", [128, D], BF16)
            ss = sbt("dss", [128, NT], F32)
            lnv = sbt("dlnv", [128, NT], F32)
            rstd = sbt("drstd", [128, NT], F32)
            ps = es.enter_context(nc.psum_tensor("psd", [128, 8, 512], F32))
            cPE, cACT, cDVE, cPOOL = (Chan(sem(n)) for n in ("d_pe", "d_act", "d_dve", "d_pool"))
            cWL = [Chan(sem(f"d_wl{i}"), 16) for i in range(2)]
            cGL = [Chan(sem(f"d_gl{i}"), 16) for i in range(2)]
            cXL = [Chan(sem(f"d_xl{i}"), 16) for i in range(2)]
            cOS = [Chan(sem(f"d_os{i}"), 16) for i in range(2)]
            cL0 = Chan(sem("d_l0"), 16)
            block = es.enter_context(nc.Block())

            @block.sync
            def _(_sync):
                ev_gp = cL0.inc(SP.dma_start(out=GP[:, :], in_=prm_d[:, 274:274 + D]))
                cast_ev = [None, None]
                lastc = {}
                k = 0
                for (src, dstw) in ((wbr, WB), (wout, WO)):
                    for c in range(8):
                        sl = k % 2
                        W(SP, cast_ev[sl])
                        ev = cWL[sl].inc(SP.dma_start(out=wst[sl][:, :], in_=src[:, c, :]))
                        eng, ch = (DVE, cDVE) if k % 2 == 0 else (POOL, cPOOL)
                        W(eng, ev)
                        cast_ev[sl] = ch.inc(eng.tensor_copy(out=dstw[:, c, :], in_=wst[sl][:, :]))
                        lastc[k % 2] = cast_ev[sl]
                        k += 1
                W(PE, lastc[0], lastc[1])
                W(DVE, ev_gp)

                gl_ev = {}
                xld = {}
                g_free = [None, None]
                mt_free = [None, None]
                mt_ready = {}
                x_free = [None, None]
                ob_free = [None, None]
                tfree = [None, None]
                bank_free = {}
                st = {"pc": 0, "tc": 0}

                def load_group(G):
                    sl = G % 2
                    W(SP, g_free[sl])
                    gs = slice(G * 512, (G + 1) * 512)
                    cGL[sl].inc(SP.dma_start(out=Yin[sl][:, :, :], in_=YT[:, gs].rearrange("(c p) t -> p c t", p=128)))
                    gl_ev[G] = cGL[sl].inc(SP.dma_start(out=Gin[sl][:, :, :],
                                                        in_=GT[:, gs].rearrange("(c p) t -> p c t", p=128)))

                def merge_group(G):
                    sl = G % 2
                    W(PE, gl_ev[G])
                    last_dve = None
                    for j in range(8):
                        bA = (st["pc"] % 2) * 2
                        bB = bA + 1
                        st["pc"] += 1
                        W(PE, bank_free.get(bA), bank_free.get(bB))
                        for c in range(4):
                            ins = PE.matmul(ps[:, bA, :], lhsT=WB[:, c, j * 128:(j + 1) * 128], rhs=Yin[sl][:, c, :],
                                            start=(c == 0), stop=(c == 3))
                        for c in range(4):
                            ins = PE.matmul(ps[:, bB, :], lhsT=WB[:, 4 + c, j * 128:(j + 1) * 128],
                                            rhs=Yin[sl][:, 4 + c, :], start=(c == 0), stop=(c == 3))
                        mev = cPE.inc(ins)
                        ti = st["tc"] % 2
                        st["tc"] += 1
                        W(DVE, mev, tfree[ti])
                        ins = DVE.tensor_tensor(out=t1[ti][:, :], in0=ps[:, bA, :], in1=Gin[sl][:, j, :], op=ALU.mult)
                        bank_free[bA] = cDVE.inc(ins)
                        ins = DVE.tensor_tensor(out=t2[ti][:, :], in0=ps[:, bB, :], in1=Gin[sl][:, 8 + j, :],
                                                op=ALU.mult)
                        e2 = cDVE.inc(ins)
                        bank_free[bB] = e2
                        last_dve = e2
                        W(POOL, e2)
                        if j == 0:
                            W(POOL, mt_free[sl])
                        e3 = cPOOL.inc(POOL.tensor_tensor(out=MT[sl][:, j, :], in0=t1[ti][:, :], in1=t2[ti][:, :],
                                                          op=ALU.add))
                        tfree[ti] = e3
                    mt_ready[G] = e3
                    g_free[sl] = [last_dve, mev]

                def out_group(G):
                    sl = G % 2
                    W(PE, mt_ready[G])
                    for i in range(4):
                        T = G * 4 + i
                        xs = T % 2
                        if T == 0:
                            xld[0] = cXL[0].inc(SP.dma_start(out=xin[0][:, :], in_=x[0:128, :]))
                        if T + 1 < NT:
                            xn = (T + 1) % 2
                            W(SP, x_free[xn])
                            xld[T + 1] = cXL[xn].inc(SP.dma_start(out=xin[xn][:, :],
                                                                  in_=x[(T + 1) * 128:(T + 2) * 128, :]))
                        xev = xld[T]
                        b0 = 4 + (T % 2) * 2
                        W(PE, bank_free.get(b0))
                        for half in range(2):
                            for c in range(8):
                                ins = PE.matmul(ps[:, b0 + half, :], lhsT=MT[sl][:, c, i * 128:(i + 1) * 128],
                                                rhs=WO[:, c, half * 512:(half + 1) * 512], start=(c == 0),
                                                stop=(c == 7))
                        mev = cPE.inc(ins)
                        W(ACT, mev)
                        eq = cACT.inc(ACT.activation(out=junk[:, :], in_=ps[:, b0:b0 + 2, :].rearrange(
                            "p a b -> p (a b)"), func=AF.Square, accum_out=ss[:, T:T + 1]))
                        W(ACT, eq)
                        el = cACT.inc(ACT.activation(out=lnv[:, T:T + 1], in_=ss[:, T:T + 1], func=AF.Ln,
                                                     scale=1.0 / D, bias=EPS6))
                        W(ACT, el)
                        er = cACT.inc(ACT.activation(out=rstd[:, T:T + 1], in_=lnv[:, T:T + 1], func=AF.Exp,
                                                     scale=-0.5))
                        W(DVE, er, mev, x_free[xs])
                        ey = cDVE.inc(DVE.scalar_tensor_tensor(
                            out=yn[xs][:, :], in0=ps[:, b0:b0 + 2, :].rearrange("p a b -> p (a b)"),
                            scalar=rstd[:, T:T + 1], in1=GP[:, :], op0=ALU.mult, op1=ALU.mult))
                        bank_free[b0] = [ey, eq]
                        W(POOL, ey, xev, ob_free[xs])
                        eo = cPOOL.inc(POOL.tensor_tensor(out=ob[xs][:, :], in0=yn[xs][:, :], in1=xin[xs][:, :],
                                                          op=ALU.add))
                        x_free[xs] = eo
                        W(SP, eo)
                        ob_free[xs] = cOS[xs].inc(SP.dma_start(out=out_d[T * 128:(T + 1) * 128, :], in_=ob[xs][:, :]))
                        if i == 3:
                            mt_free[sl] = mev

                load_group(0)
                if NG > 1:
                    load_group(1)
                merge_group(0)
                for G in range(NG):
                    if G + 1 < NG:
                        merge_group(G + 1)
                    if G + 2 < NG:
                        load_group(G + 2)
                    out_group(G)
                W(SP, ob_free[0], ob_free[1])
    return nc


_CONST_CACHE = {}


def _consts(S):
    if S in _CONST_CACHE:
        return _CONST_CACHE[S]
    bf = ml_dtypes.bfloat16
    cb = np.zeros((128, 2304), np.float32)
    cb[:, 0:128] = np.eye(128, dtype=np.float32)
    cb[:, 128:256] = 1.0
    k = np.arange(128)[:, None]
    q = np.arange(512)[None, :]
    for j in range(4):
        cb[:, 256 + j * 512:256 + (j + 1) * 512] = np.where((128 * j + k) <= q, 0.0, -240000.0)
    cb = cb.astype(bf)
    cf = np.zeros((128, 384), np.float32)
    s_ = np.arange(128)[:, None]
    t_ = np.arange(128)[None, :]
    cf[:, 0:128] = (s_ <= t_).astype(np.float32)
    cf[:, 128:256] = 1.0
    cf[:, 256:384] = np.eye(128, dtype=np.float32)
    pos = np.arange(S, dtype=np.float32)
    inv_freq = (np.float32(10000.0) ** (-(np.arange(0, 64, 2, dtype=np.float32) / np.float32(64)))).astype(np.float32)
    ang = (pos[:, None] * inv_freq[None, :]).astype(np.float32)
    cos = np.cos(ang).astype(np.float32).T
    sin = np.sin(ang).astype(np.float32).T
    ropec = np.zeros((128, S), np.float32)
    ropes = np.zeros((128, S), np.float32)
    for p in range(128):
        i = p % 32
        ropec[p] = cos[i]
        ropes[p] = -sin[i] if p < 64 else sin[i]
    _CONST_CACHE[S] = (cb, cf, ropec, ropes)
    return _CONST_CACHE[S]


def _layout_weights(g_pre, w_in, b_forget, lq1, lk1, lq2, lk2, g_subln, w_branch, w_out, g_post):
    w = np.asarray(w_in[0], np.float32)
    qa, ka, va, fa, za = w[:, 0:512], w[:, 512:1024], w[:, 1024:1536], w[:, 1536:1544], w[:, 1544:2056]
    qb, kb, vb, zb, gates = w[:, 2056:2568], w[:, 2568:3080], w[:, 3080:3592], w[:, 3592:4104], w[:, 4104:6152]
    perm = np.arange(512).reshape(4, 2, 2, 32).transpose(0, 2, 1, 3).reshape(-1)
    w2 = np.concatenate([qa, ka, za, zb, gates, qb[:, perm], kb[:, perm], va, vb, fa], axis=1)
    assert w2.shape[1] == NW
    w2 = np.ascontiguousarray(w2.reshape(8, 128, NW).transpose(1, 0, 2))
    wbr = np.asarray(w_branch[0], np.float32).reshape(2 * 512, D)
    wbr = np.ascontiguousarray(wbr.reshape(8, 128, D).transpose(1, 0, 2))
    wo = np.ascontiguousarray(np.asarray(w_out[0], np.float32).reshape(8, 128, D).transpose(1, 0, 2))
    prm = np.zeros((128, NPRM), np.float32)
    prm[:, 0:8] = np.asarray(g_pre[0], np.float32).reshape(8, 128).T
    prm[:, 8:16] = np.asarray(b_forget[0], np.float32)[None, :]
    prm[:, 16:80] = np.asarray(lq1[0], np.float32)[None, :]
    prm[:, 80:144] = np.asarray(lk1[0], np.float32)[None, :]
    prm[:, 144:208] = np.asarray(lq2[0], np.float32)[None, :]
    prm[:, 208:272] = np.asarray(lk2[0], np.float32)[None, :]
    prm[:, 272] = np.asarray(g_subln[0], np.float32)
    prm[:, 274:274 + D] = np.asarray(g_post[0], np.float32)[None, :]
    return w2, wbr, wo, prm


_NC_CACHE = {}


def kernel(x, g_pre, w_in, b_forget, lambda_q1, lambda_k1, lambda_q2, lambda_k2, g_subln, w_branch, w_out,
           g_post):
    x = np.asarray(x, np.float32)
    B, S, _ = x.shape
    w2, wbr, wo, prm = _layout_weights(g_pre, w_in, b_forget, lambda_q1, lambda_k1, lambda_q2, lambda_k2,
                                       g_subln, w_branch, w_out, g_post)
    cb, cf, ropec, ropes = _consts(S)
    if S not in _NC_CACHE:
        _NC_CACHE[S] = build(S)
    nc = _NC_CACHE[S]
    in_maps = [dict(x=np.ascontiguousarray(x[b]), w2=w2, wbr=wbr, wout=wo, prm=prm, cb=cb, cf=cf,
                    ropec=ropec, ropes=ropes) for b in range(B)]
    res = run_bass_kernel_spmd(nc, in_maps, core_ids=list(range(B)))
    return np.stack([np.asarray(r["out"], np.float32) for r in res.results], axis=0)
```

```python
import math
from contextlib import ExitStack

import numpy as np
import ml_dtypes
import concourse.bass as bass
import concourse.mybir as mybir
from concourse.bass_utils import run_bass_kernel_spmd

F32 = mybir.dt.float32
BF16 = mybir.dt.bfloat16
ALU = mybir.AluOpType
AF = mybir.ActivationFunctionType

D = 1024
NFM = 40
COL_VA = NFM * 128
COL_VB = COL_VA + 512
COL_FA = COL_VB + 512
NW = COL_FA + 8
WCH = 769
NPRM = 274 + 1024
LAM_INIT = 0.8 - 0.6 * math.exp(-0.3 * 0.0)
NORM_EPS = 1e-6
SUBLN_EPS = 1e-5


class Ev:
    __slots__ = ("ch", "val")

    def __init__(self, ch, val):
        self.ch = ch
        self.val = val


class Chan:
    def __init__(self, sem, step=1):
        self.sem = sem
        self.step = step
        self.val = 0

    def inc(self, ins):
        self.val += self.step
        ins.then_inc(self.sem, self.step)
        return Ev(self, self.val)


class Emitter:
    def __init__(self, nc):
        self.nc = nc
        self.waited = {}

    def wait(self, eng, *evs):
        for ev in evs:
            if ev is None:
                continue
            if isinstance(ev, (list, tuple)):
                self.wait(eng, *ev)
                continue
            key = (id(eng), ev.ch)
            if self.waited.get(key, 0) >= ev.val:
                continue
            eng.wait_ge(ev.ch.sem, ev.val)
            self.waited[key] = ev.val


def build(S):
    NT = S // 128
    NG = S // 512
    NF = 8 * NT
    NCH = max(1, NF // 128)
    CHP = min(128, NF)

    nc = bass.Bass("TRN2", target_bir_lowering=False)
    x = nc.dram_tensor("x", [S, D], F32, kind="ExternalInput").ap()
    w2 = nc.dram_tensor("w2", [128, 8, NW], F32, kind="ExternalInput").ap()
    wbr = nc.dram_tensor("wbr", [128, 8, D], F32, kind="ExternalInput").ap()
    wout = nc.dram_tensor("wout", [128, 8, D], F32, kind="ExternalInput").ap()
    prm_d = nc.dram_tensor("prm", [128, NPRM], F32, kind="ExternalInput").ap()
    cb_d = nc.dram_tensor("cb", [128, 2304], BF16, kind="ExternalInput").ap()
    cf_d = nc.dram_tensor("cf", [128, 384], F32, kind="ExternalInput").ap()
    ropec_d = nc.dram_tensor("ropec", [128, S], F32, kind="ExternalInput").ap()
    ropes_d = nc.dram_tensor("ropes", [128, S], F32, kind="ExternalInput").ap()
    out_d = nc.dram_tensor("out", [S, D], F32, kind="ExternalOutput").ap()

    QF = nc.dram_tensor("QF", [512, S], BF16, kind="Internal").ap()
    KF = nc.dram_tensor("KF", [512, S], BF16, kind="Internal").ap()
    VF = nc.dram_tensor("VF", [S, 512], BF16, kind="Internal").ap()
    VD = nc.dram_tensor("VD", [S, 512], BF16, kind="Internal").ap()
    ZT = nc.dram_tensor("ZT", [1024, S], BF16, kind="Internal").ap()
    GT = nc.dram_tensor("GT", [2048, S], BF16, kind="Internal").ap()
    QD = nc.dram_tensor("QD", [512, S], BF16, kind="Internal").ap()
    KD = nc.dram_tensor("KD", [512, S], BF16, kind="Internal").ap()
    YT = nc.dram_tensor("YT", [1024, S], BF16, kind="Internal").ap()
    AUG = nc.dram_tensor("AUG", [4, 8 * S], BF16, kind="Internal").ap()

    em = Emitter(nc)
    W = em.wait
    PE, ACT, DVE, POOL, SP = nc.tensor, nc.scalar, nc.vector, nc.gpsimd, nc.sync

    with ExitStack() as g:
        g.enter_context(nc.allow_low_precision("bf16 matmul operands, fp32 accumulation"))
        g.enter_context(nc.allow_non_contiguous_dma("small strided V loads"))
        PRM = g.enter_context(nc.sbuf_tensor("PRM", [128, 274], F32))
        CB = g.enter_context(nc.sbuf_tensor("CB", [128, 2304], BF16))
        CF = g.enter_context(nc.sbuf_tensor("CF", [128, 384], F32))
        FA = g.enter_context(nc.sbuf_tensor("FA", [128, 8, NT], F32))
        SM = g.enter_context(nc.sbuf_tensor("SM", [128, 16], F32))
        identb = CB[:, 0:128]
        onesb = CB[:, 128:256]
        NEGLAM = SM[:, 0:1]
        GSUBC = SM[:, 1:2]
        EPS6 = SM[:, 2:3]
        EPS5 = SM[:, 3:4]
        ONE = SM[:, 4:5]

        def mask(j):
            return CB[:, 256 + j * 512:256 + (j + 1) * 512]

        with ExitStack() as es:
            sbt = lambda n, s, d: es.enter_context(nc.sbuf_tensor(n, s, d))
            sem = lambda n: es.enter_context(nc.semaphore(n))
            Wb = sbt("Wb", [128, 8, NW], BF16)
            wst = [sbt(f"wst{i}", [128, WCH], F32) for i in range(6)]
            xt = [sbt(f"xt{i}", [128, D], F32) for i in range(2)]
            hb = [sbt(f"hb{i}", [128, D], BF16) for i in range(4)]
            hT = [sbt(f"hT{i}", [128, 8, 512], BF16) for i in range(2)]
            rc = [sbt(f"rc{i}", [128, 512], F32) for i in range(2)]
            rs = [sbt(f"rs{i}", [128, 512], F32) for i in range(2)]
            NS = 6
            stage = [sbt(f"stg{i}", [128, 512], BF16) for i in range(NS)]
            tmp1s = [sbt(f"tmp1_{i}", [128, 512], F32) for i in range(2)]
            tmp2s = [sbt(f"tmp2_{i}", [128, 512], F32) for i in range(2)]
            junk = sbt("junk", [128, D], BF16)
            ss = sbt("ss", [128, NT], F32)
            lnv = sbt("lnv", [128, NT], F32)
            rstd = sbt("rstd", [128, NT], F32)
            lt = sbt("lt", [128, 64], F32)
            psm = es.enter_context(nc.psum_tensor("psm", [128, 4, 512], F32))
            psT = es.enter_context(nc.psum_tensor("psT", [128, 2, 8, 128], BF16))
            psf = es.enter_context(nc.psum_tensor("psf", [128, 512], F32))
            cPE, cACT, cDVE, cPOOL = (Chan(sem(n)) for n in ("a_pe", "a_act", "a_dve", "a_pool"))
            cLD0 = Chan(sem("a_ld0"), 16)
            cWL = [Chan(sem(f"a_wl{i}"), 16) for i in range(6)]
            cXL = [Chan(sem(f"a_xl{i}"), 16) for i in range(2)]
            cRL = [Chan(sem(f"a_rl{i}"), 16) for i in range(2)]
            cST = [Chan(sem(f"a_st{i}"), 16) for i in range(NS)]
            block = es.enter_context(nc.Block())

            @block.sync
            def _(_sync):
                SP.dma_start(out=PRM[:, :], in_=prm_d[:, 0:274]).then_inc(cLD0.sem, 16)
                SP.dma_start(out=CB[:, :], in_=cb_d[:, :]).then_inc(cLD0.sem, 16)
                ins = SP.dma_start(out=CF[:, :], in_=cf_d[:, :])
                cLD0.val = 32
                ev0 = cLD0.inc(ins)
                W(DVE, ev0)
                DVE.memset(SM[:, 2:3], NORM_EPS)
                DVE.memset(SM[:, 3:4], SUBLN_EPS)
                DVE.memset(SM[:, 4:5], 1.0)
                W(DVE, cDVE.inc(DVE.tensor_tensor(out=lt[:, :], in0=PRM[:, 16:80], in1=PRM[:, 80:144], op=ALU.mult)))
                ins = DVE.tensor_reduce(out=SM[:, 5:6], in_=lt[:, :], axis=mybir.AxisListType.X, op=ALU.add)
                e = cDVE.inc(ins)
                W(DVE, e)
                W(DVE, cDVE.inc(DVE.tensor_tensor(out=lt[:, :], in0=PRM[:, 144:208], in1=PRM[:, 208:272], op=ALU.mult)))
                ins = DVE.tensor_reduce(out=SM[:, 6:7], in_=lt[:, :], axis=mybir.AxisListType.X, op=ALU.add)
                e = cDVE.inc(ins)
                W(ACT, e)
                ins = ACT.activation(out=SM[:, 7:9], in_=SM[:, 5:7], func=AF.Exp)
                e = cACT.inc(ins)
                W(DVE, e)
                ins = DVE.tensor_tensor(out=SM[:, 9:10], in0=SM[:, 8:9], in1=SM[:, 7:8], op=ALU.subtract)
                e = cDVE.inc(ins)
                W(DVE, e)
                DVE.tensor_scalar(out=SM[:, 0:1], in0=SM[:, 9:10], scalar1=-LAM_INIT, scalar2=None, op0=ALU.add)
                ins = DVE.tensor_scalar(out=SM[:, 1:2], in0=PRM[:, 272:273], scalar1=1.0 - LAM_INIT, scalar2=None,
                                        op0=ALU.mult)
                ev_sm = cDVE.inc(ins)
                W(ACT, ev_sm)
                W(POOL, ev0)

                cast_ev = [None] * 6
                last_cast = {}
                chunk_ev = {}
                k = 0
                for q in range(NW // WCH):
                    for c in range(8):
                        sl = k % 6
                        W(SP, cast_ev[sl])
                        ev = cWL[sl].inc(SP.dma_start(out=wst[sl][:, :], in_=w2[:, c, q * WCH:(q + 1) * WCH]))
                        eng, ch = (DVE, cDVE) if k % 2 == 0 else (POOL, cPOOL)
                        W(eng, ev)
                        ins = eng.tensor_scalar(out=Wb[:, c, q * WCH:(q + 1) * WCH], in0=wst[sl][:, :],
                                                scalar1=PRM[:, c:c + 1], scalar2=None, op0=ALU.mult)
                        cast_ev[sl] = ch.inc(ins)
                        last_cast[k % 2] = cast_ev[sl]
                        chunk_ev[q] = [last_cast.get(0), last_cast.get(1)]
                        k += 1
                W(PE, ev0)

                def wready(col_hi):
                    W(PE, chunk_ev[(col_hi - 1) // WCH])

                xld_ev = {}
                sq_ev = {}
                rstd_ev = {}
                x_free = [None, None]
                tr_ev = {}
                trc_ev = {}
                h_evs = {}
                grp_mm_ev = {}
                rl_ev = {}
                rope_done = {}

                def tile_load(T):
                    sl = T % 2
                    W(SP, x_free[sl])
                    xld_ev[T] = cXL[sl].inc(SP.dma_start(out=xt[sl][:, :], in_=x[T * 128:(T + 1) * 128, :]))
                    W(ACT, xld_ev[T], sq_ev.get(T - 1))
                    ins = ACT.activation(out=junk[:, :], in_=xt[sl][:, :], func=AF.Square, accum_out=ss[:, T:T + 1])
                    sq_ev[T] = cACT.inc(ins)

                def tile_stats(T):
                    W(ACT, sq_ev[T])
                    ins = ACT.activation(out=lnv[:, T:T + 1], in_=ss[:, T:T + 1], func=AF.Ln, scale=1.0 / D,
                                         bias=EPS6)
                    e1 = cACT.inc(ins)
                    W(ACT, e1)
                    ins = ACT.activation(out=rstd[:, T:T + 1], in_=lnv[:, T:T + 1], func=AF.Exp, scale=-0.5)
                    rstd_ev[T] = cACT.inc(ins)

                def tile_hA(T):
                    sl = T % 2
                    hs = T % 4
                    W(DVE, rstd_ev[T], xld_ev[T], tr_ev.get(T - 4))
                    ins = DVE.tensor_scalar(out=hb[hs][:, :], in0=xt[sl][:, :], scalar1=rstd[:, T:T + 1],
                                            scalar2=None, op0=ALU.mult)
                    h_evs[T] = cDVE.inc(ins)
                    x_free[sl] = [h_evs[T], sq_ev[T]]

                def tile_hB(T):
                    G, i = divmod(T, 4)
                    sl = T % 2
                    hs = T % 4
                    W(PE, h_evs[T], trc_ev.get(T - 2))
                    for c in range(8):
                        ins = PE.transpose(out=psT[:, sl, c, :], in_=hb[hs][:, c * 128:(c + 1) * 128],
                                           identity=identb)
                    tr_ev[T] = cPE.inc(ins)
                    W(DVE, tr_ev[T], grp_mm_ev.get(G - 2))
                    ins = DVE.tensor_copy(out=hT[G % 2][:, :, i * 128:(i + 1) * 128], in_=psT[:, sl, :, :])
                    trc_ev[T] = cDVE.inc(ins)

                def rope_load(G):
                    sl = G % 2
                    W(SP, rope_done.get(G - 2))
                    cRL[sl].inc(SP.dma_start(out=rc[sl][:, :], in_=ropec_d[:, G * 512:(G + 1) * 512]))
                    rl_ev[G] = cRL[sl].inc(SP.dma_start(out=rs[sl][:, :], in_=ropes_d[:, G * 512:(G + 1) * 512]))

                state = {"pc": 0, "sc": 0}
                bank_free = [None] * 4
                stage_free = [None] * NS
                tmp_free = [None, None]
                rope_cnt = [0]
                fa_free = [None]

                def mm_fm(G, blk, bank):
                    wready((blk + 1) * 128)
                    W(PE, bank_free[bank])
                    for c in range(8):
                        ins = PE.matmul(psm[:, bank, :], lhsT=Wb[:, c, blk * 128:(blk + 1) * 128],
                                        rhs=hT[G % 2][:, c, :], start=(c == 0), stop=(c == 7))
                    return cPE.inc(ins)

                def mm_tm(G, i, col, bank):
                    wready(col + 512)
                    W(PE, bank_free[bank])
                    for c in range(8):
                        ins = PE.matmul(psm[:, bank, :], lhsT=hT[G % 2][:, c, i * 128:(i + 1) * 128],
                                        rhs=Wb[:, c, col:col + 512], start=(c == 0), stop=(c == 7))
                    return cPE.inc(ins)

                def store(slot, ev, dst):
                    W(SP, ev)
                    stage_free[slot] = cST[slot].inc(SP.dma_start(out=dst, in_=stage[slot][:, :]))

                def nbank():
                    b = state["pc"] % 4
                    state["pc"] += 1
                    return b

                def nslot():
                    s = state["sc"] % NS
                    state["sc"] += 1
                    return s

                def group_mm(G, mid=None):
                    W(PE, trc_ev[G * 4 + 3])
                    gs = slice(G * 512, (G + 1) * 512)
                    last_pe = None
                    fm = []
                    for b in range(4):
                        fm.append((b, "copy", QF[b * 128:(b + 1) * 128, gs]))
                    for b in range(4):
                        fm.append((4 + b, "copy", KF[b * 128:(b + 1) * 128, gs]))
                    for b in range(8):
                        fm.append((8 + b, "silu", ZT[b * 128:(b + 1) * 128, gs]))
                    for b in range(16):
                        fm.append((16 + b, "sigm", GT[b * 128:(b + 1) * 128, gs]))
                    for bi, (blk, kind, dst) in enumerate(fm):
                        if bi == 12 and mid is not None:
                            mid()
                        bank = nbank()
                        mev = mm_fm(G, blk, bank)
                        slot = nslot()
                        if kind == "copy":
                            W(DVE, mev, stage_free[slot])
                            ins = DVE.tensor_copy(out=stage[slot][:, :], in_=psm[:, bank, :])
                            eev = cDVE.inc(ins)
                        else:
                            W(ACT, mev, stage_free[slot])
                            ins = ACT.activation(out=stage[slot][:, :], in_=psm[:, bank, :],
                                                 func=AF.Silu if kind == "silu" else AF.Sigmoid)
                            eev = cACT.inc(ins)
                        bank_free[bank] = eev
                        store(slot, eev, dst)
                    for b in range(8):
                        dram = QD if b < 4 else KD
                        r0 = (b % 4) * 128
                        bankA = nbank()
                        mevA = mm_fm(G, 32 + b, bankA)
                        slot = nslot()
                        ti = rope_cnt[0] % 2
                        rope_cnt[0] += 1
                        tmp1, tmp2 = tmp1s[ti], tmp2s[ti]
                        W(DVE, mevA, rl_ev[G], tmp_free[ti])
                        DVE.tensor_tensor(out=tmp1[:, :], in0=psm[:, bankA, :], in1=rc[G % 2][:, :], op=ALU.mult)
                        DVE.tensor_tensor(out=tmp2[0:64, :], in0=psm[64:128, bankA, :], in1=rs[G % 2][0:64, :],
                                          op=ALU.mult)
                        ins = DVE.tensor_tensor(out=tmp2[64:128, :], in0=psm[0:64, bankA, :],
                                                in1=rs[G % 2][64:128, :], op=ALU.mult)
                        e2 = cDVE.inc(ins)
                        bank_free[bankA] = e2
                        W(POOL, e2, stage_free[slot])
                        ins = POOL.tensor_tensor(out=stage[slot][:, :], in0=tmp1[:, :], in1=tmp2[:, :], op=ALU.add)
                        e3 = cPOOL.inc(ins)
                        tmp_free[ti] = e3
                        W(SP, e3)
                        for (p0, d0) in ((0, 0), (32, 64), (64, 32), (96, 96)):
                            stage_free[slot] = cST[slot].inc(SP.dma_start(
                                out=dram[r0 + d0:r0 + d0 + 32, gs], in_=stage[slot][p0:p0 + 32, :]))
                    rope_done[G] = e2
                    W(PE, fa_free[0])
                    for i in range(4):
                        T = G * 4 + i
                        for (col, dstT) in ((COL_VA, VF), (COL_VB, VD)):
                            bank = nbank()
                            mev = mm_tm(G, i, col, bank)
                            slot = nslot()
                            W(ACT, mev, stage_free[slot])
                            ins = ACT.activation(out=stage[slot][:, :], in_=psm[:, bank, :], func=AF.Copy)
                            eev = cACT.inc(ins)
                            bank_free[bank] = eev
                            store(slot, eev, dstT[T * 128:(T + 1) * 128, :])
                        wready(NW)
                        for c in range(8):
                            ins = PE.matmul(psf[:, i * 8:(i + 1) * 8], lhsT=hT[G % 2][:, c, i * 128:(i + 1) * 128],
                                            rhs=Wb[:, c, COL_FA:COL_FA + 8], start=(c == 0), stop=(c == 7))
                        last_pe = cPE.inc(ins)
                    grp_mm_ev[G] = last_pe
                    W(DVE, last_pe)
                    for i in range(4):
                        ins = DVE.tensor_copy(out=FA[:, :, G * 4 + i], in_=psf[:, i * 8:(i + 1) * 8])
                    fa_free[0] = cDVE.inc(ins)

                def prepA(G):
                    rope_load(G)
                    for i in range(4):
                        T = G * 4 + i
                        tile_load(T)
                        tile_stats(T)
                        tile_hA(T)

                def prepB(G):
                    for i in range(4):
                        tile_hB(G * 4 + i)

                prepA(0)
                prepB(0)
                for G in range(NG):
                    if G + 1 < NG:
                        prepA(G + 1)
                        group_mm(G, mid=lambda G=G: prepB(G + 1))
                    else:
                        group_mm(G)
                W(SP, *stage_free)
                W(SP, fa_free[0])

        with ExitStack() as es:
            sbt = lambda n, s, d: es.enter_context(nc.sbuf_tensor(n, s, d))
            sem = lambda n: es.enter_context(nc.semaphore(n))
            E1 = sbt("E1", [128, NF], F32)
            LS = sbt("LS", [128, NF], F32)
            TOT = sbt("TOT", [128, 8, NT], F32)
            INC = sbt("INC", [128, 8, NT], F32)
            N1 = sbt("N1", [128, NF], F32)
            N8 = sbt("N8", [128, NF], F32)
            R1 = sbt("R1", [128, NF], F32)
            R2 = sbt("R2", [128, NF], F32)
            AUG4 = sbt("AUG4", [128, 4, NF], BF16)
            AUGT = sbt("AUGT", [128, 4 * NCH, 128], BF16)
            ps = es.enter_context(nc.psum_tensor("ps2", [128, 2, 512], F32))
            psA = es.enter_context(nc.psum_tensor("psA", [128, 4 * NCH, 128], BF16))
            cPE, cACT, cDVE = (Chan(sem(n)) for n in ("b_pe", "b_act", "b_dve"))
            cST = Chan(sem("b_st"), 16)
            block = es.enter_context(nc.Block())

            @block.sync
            def _(_sync):
                FAf = FA[:, :, :].rearrange("p h t -> p (h t)")
                for h in range(8):
                    ins = DVE.tensor_scalar(out=FA[:, h, :], in0=FA[:, h, :], scalar1=PRM[:, 8 + h:9 + h],
                                            scalar2=None, op0=ALU.add)
                e = cDVE.inc(ins)
                W(ACT, e)
                e = cACT.inc(ACT.activation(out=E1[:, :], in_=FAf, func=AF.Exp, scale=-1.0))
                W(ACT, e)
                e = cACT.inc(ACT.activation(out=LS[:, :], in_=E1[:, :], func=AF.Ln, bias=ONE, scale=1.0))
                W(PE, e)
                PE.matmul(ps[:, 0, 0:NF], lhsT=CF[:, 0:128], rhs=LS[:, :], start=True, stop=True)
                e = cPE.inc(PE.matmul(ps[:, 1, 0:NF], lhsT=CF[:, 128:256], rhs=LS[:, :], start=True, stop=True))
                W(DVE, e)
                TOTf = TOT[:, :, :].rearrange("p h t -> p (h t)")
                INCf = INC[:, :, :].rearrange("p h t -> p (h t)")
                e = cDVE.inc(DVE.tensor_copy(out=TOTf, in_=ps[:, 1, 0:NF]))
                W(DVE, e)
                for h in range(8):
                    ins = DVE.tensor_tensor_scan(out=INC[:, h, :], data0=CF[:, 128:128 + NT], data1=TOT[:, h, :],
                                                 initial=0.0, op0=ALU.mult, op1=ALU.add)
                e = cDVE.inc(ins)
                W(DVE, e)
                e = cDVE.inc(DVE.tensor_tensor(out=N1[:, :], in0=ps[:, 0, 0:NF], in1=INCf, op=ALU.add))
                W(DVE, e)
                e = cDVE.inc(DVE.tensor_tensor(out=N8[:, :], in0=N1[:, :], in1=TOTf, op=ALU.subtract))
                W(DVE, e)
                e = cDVE.inc(DVE.tensor_scalar(out=N8[:, :], in0=N8[:, :], scalar1=8.0, scalar2=None, op0=ALU.mult))
                W(DVE, e)
                e = cDVE.inc(DVE.tensor_copy(out=AUG4[:, 1, :], in_=N8[:, :]))
                W(DVE, e)
                e = cDVE.inc(DVE.tensor_tensor(out=R1[:, :], in0=N8[:, :], in1=AUG4[:, 1, :], op=ALU.subtract))
                W(DVE, e)
                e = cDVE.inc(DVE.tensor_copy(out=AUG4[:, 2, :], in_=R1[:, :]))
                W(DVE, e)
                e = cDVE.inc(DVE.tensor_tensor(out=R2[:, :], in0=R1[:, :], in1=AUG4[:, 2, :], op=ALU.subtract))
                W(DVE, e)
                DVE.tensor_copy(out=AUG4[:, 3, :], in_=R2[:, :])
                e = cDVE.inc(DVE.tensor_scalar(out=AUG4[:, 0, :], in0=AUG4[:, 1, :], scalar1=-1.0, scalar2=None,
                                               op0=ALU.mult))
                W(PE, e)
                for r in range(4):
                    for i in range(NCH):
                        ins = PE.transpose(out=psA[0:CHP, r * NCH + i, :], in_=AUG4[:, r, i * 128:i * 128 + CHP],
                                           identity=identb)
                e = cPE.inc(ins)
                W(DVE, e)
                e = cDVE.inc(DVE.tensor_copy(out=AUGT[0:CHP, :, :], in_=psA[0:CHP, :, :]))
                W(SP, e)
                for r in range(4):
                    for i in range(NCH):
                        dst = AUG[r:r + 1, i * 128 * 128:i * 128 * 128 + CHP * 128].rearrange(
                            "o (p t) -> (o p) t", t=128)
                        ev_st = cST.inc(SP.dma_start(out=dst, in_=AUGT[0:CHP, r * NCH + i, :]))
                W(SP, ev_st)

        def attention(kind):
            fox = kind == "fox"
            NH = 8 if fox else 4
            BW = 3 if fox else 2
            NPB = 4
            KR = 68 if fox else 64
            with ExitStack() as es:
                sbt = lambda n, s, d: es.enter_context(nc.sbuf_tensor(kind + "_" + n, s, d))
                sem = lambda n: es.enter_context(nc.semaphore(kind + "_" + n))
                QT = [sbt(f"QT{i}", [128, S], BF16) for i in range(2)]
                KT = [sbt(f"KT{i}", [128, S], BF16) for i in range(2)]
                V = [sbt(f"V{i}", [128, NT, 128], BF16) for i in range(2)]
                Z = [sbt(f"Z{i}", [128, S], BF16) for i in range(2)]
                P = [sbt(f"P{i}", [128, BW, 512], BF16) for i in range(NPB)]
                Rt = sbt("Rt", [128, 512], F32)
                Tt = [sbt(f"Tt{i}", [128, 512], F32) for i in range(2)]
                NYS = 2
                Yst = [sbt(f"Yst{i}", [128, 512], BF16) for i in range(NYS)]
                if not fox:
                    Dc = sbt("Dc", [128, 512], F32)
                    Xc = sbt("Xc", [128, 512], F32)
                    Yc = sbt("Yc", [128, 512], F32)
                    Sh = sbt("Sh", [128, 512], F32)
                    Of = sbt("Of", [128, 512], F32)
                    SQ = sbt("SQ", [128, 512], BF16)
                    LNV = sbt("LNV", [128, 512], F32)
                    RS = sbt("RS", [128, 512], F32)
                    Y1 = sbt("Y1", [128, 512], F32)
                ps = es.enter_context(nc.psum_tensor(kind + "_psat", [128, 8, 512], F32))
                cPE, cACT, cDVE, cPOOL = (Chan(sem(n)) for n in ("c_pe", "c_act", "c_dve", "c_pool"))
                cLD = [Chan(sem(f"c_ld{i}"), 16) for i in range(2)]
                cST = [Chan(sem(f"c_st{i}"), 16) for i in range(NYS)]
                block = es.enter_context(nc.Block())

                @block.sync
                def _(_sync):
                    init_ev = None
                    if fox:
                        for b in range(2):
                            DVE.memset(QT[b][64:68, :], 1.0)
                            DVE.memset(KT[b][64:68, :], 1.0)
                            ins = POOL.memset(V[b][:, :, 64:128], 1.0)
                        init_ev = [cDVE.inc(DVE.memset(Rt[:, :], 1.0)), cPOOL.inc(ins)]
                    head_free = [None, None]
                    ld_ev = {}

                    def load_head(h):
                        b = h % 2
                        W(SP, head_free[b], init_ev)
                        c = cLD[b]
                        if fox:
                            c.inc(SP.dma_start(out=QT[b][0:64, :], in_=QF[h * 64:(h + 1) * 64, :]))
                            c.inc(SP.dma_start(out=QT[b][64:65, :], in_=AUG[0:1, h * S:(h + 1) * S]))
                            c.inc(SP.dma_start(out=KT[b][0:64, :], in_=KF[h * 64:(h + 1) * 64, :]))
                            c.inc(SP.dma_start(out=KT[b][65:68, :], in_=AUG[1:4, h * S:(h + 1) * S]))
                            vsrc = VF.rearrange("(n p) c -> p n c", p=128)
                            nsp = 4 if NT >= 4 else 1
                            for q in range(nsp):
                                a0, a1 = q * NT // nsp, (q + 1) * NT // nsp
                                c.inc(SP.dma_start(out=V[b][:, a0:a1, 0:64], in_=vsrc[:, a0:a1, h * 64:(h + 1) * 64]))
                            ld_ev[h] = c.inc(SP.dma_start(out=Z[b][0:64, :], in_=ZT[h * 64:(h + 1) * 64, :]))
                        else:
                            c.inc(SP.dma_start(out=QT[b][:, :], in_=QD[h * 128:(h + 1) * 128, :]))
                            c.inc(SP.dma_start(out=KT[b][:, :], in_=KD[h * 128:(h + 1) * 128, :]))
                            vsrc = VD.rearrange("(n p) c -> p n c", p=128)
                            nsp = 4 if NT >= 4 else 1
                            for q in range(nsp):
                                a0, a1 = q * NT // nsp, (q + 1) * NT // nsp
                                c.inc(SP.dma_start(out=V[b][:, a0:a1, :], in_=vsrc[:, a0:a1, h * 128:(h + 1) * 128]))
                            ld_ev[h] = c.inc(SP.dma_start(out=Z[b][:, :],
                                                          in_=ZT[512 + h * 128:512 + (h + 1) * 128, :]))

                    units = []
                    batches = []
                    for h in range(NH):
                        for g in range(NG):
                            u = len(units)
                            nk = 4 * (g + 1)
                            units.append(dict(h=h, g=g, nk=nk))
                            if fox:
                                blks = [(kt, 0) for kt in range(nk)]
                            else:
                                blks = [(kt, m) for kt in range(nk) for m in range(2)]
                            for s0 in range(0, len(blks), BW):
                                batches.append(dict(u=u, blks=blks[s0:s0 + BW], first=(s0 == 0),
                                                    last=(s0 + BW >= len(blks))))
                    NB = len(batches)
                    qk_ev = {}
                    exp_ev = {}
                    mask_ev = {}
                    pv_ev = {}
                    acc_free = {}
                    deferred = {}
                    st_free = [None] * NYS
                    head_last_pe = {}
                    head_last_epi = {}
                    misc = {"ys": 0, "of_free": None, "sub_free": None, "tt_free": [None, None], "loaded": -1}

                    def rows(m):
                        return slice(0, KR) if (fox or m == 0) else slice(64, 128)

                    def acc_banks(u):
                        if fox:
                            return (6 + u % 2,)
                        return (4, 5, 6)

                    def emit_qk(n):
                        bt = batches[n]
                        un = units[bt["u"]]
                        h, g = un["h"], un["g"]
                        b = h % 2
                        W(PE, ld_ev[h], exp_ev.get(n - 2))
                        sbase = (n % 2) * BW
                        for j, (kt, m) in enumerate(bt["blks"]):
                            band = kt >= 4 * g
                            c0 = 128 * (kt - 4 * g) if band else 0
                            r = rows(m)
                            ins = PE.matmul(ps[:, sbase + j, c0:512], lhsT=KT[b][r, kt * 128:(kt + 1) * 128],
                                            rhs=QT[b][r, g * 512 + c0:(g + 1) * 512], start=True, stop=True)
                        for j, (kt, m) in enumerate(bt["blks"]):
                            if kt >= 4 * g:
                                c1 = 128 * (kt - 4 * g + 1)
                                ins = PE.matmul(ps[:, sbase + j, 0:c1], lhsT=identb, rhs=mask(kt - 4 * g)[:, 0:c1],
                                                start=False, stop=True, skip_group_check=True)
                        qk_ev[n] = cPE.inc(ins)

                    def emit_exp(n):
                        bt = batches[n]
                        nb = len(bt["blks"])
                        sbase = (n % 2) * BW
                        W(ACT, qk_ev[n], pv_ev.get(n - NPB))
                        ins = ACT.activation(out=P[n % NPB][:, 0:nb, :], in_=ps[:, sbase:sbase + nb, :], func=AF.Exp,
                                             scale=0.125)
                        exp_ev[n] = cACT.inc(ins)

                    def emit_mask(n):
                        bt = batches[n]
                        un = units[bt["u"]]
                        g = un["g"]
                        ins = None
                        for j, kt in enumerate(bt["kts"]):
                            if kt >= 4 * g:
                                W(POOL, exp_ev[n])
                                ins = POOL.tensor_tensor(out=P[n % NPB][:, j, :], in0=P[n % NPB][:, j, :],
                                                         in1=mask(kt - 4 * g), op=ALU.mult)
                        if ins is not None:
                            mask_ev[n] = cPOOL.inc(ins)

                    def emit_pv(n):
                        bt = batches[n]
                        u = bt["u"]
                        un = units[u]
                        h, nk = un["h"], un["nk"]
                        b = h % 2
                        banks = acc_banks(u)
                        W(PE, exp_ev[n], mask_ev.get(n))
                        if bt["first"]:
                            W(PE, acc_free.get(banks[0]))
                        g = un["g"]
                        if fox:
                            for j, (kt, m) in enumerate(bt["blks"]):
                                c0 = 128 * (kt - 4 * g) if kt >= 4 * g else 0
                                ins = PE.matmul(ps[:, banks[0], c0:512], lhsT=V[b][:, kt, :],
                                                rhs=P[n % NPB][:, j, c0:512], start=(kt == 0), stop=(kt == nk - 1))
                        else:
                            for half in range(2):
                                for j, (kt, m) in enumerate(bt["blks"]):
                                    c0 = 128 * (kt - 4 * g) if kt >= 4 * g else 0
                                    ins = PE.matmul(ps[64 * m:64 * m + 64, banks[half], c0:512],
                                                    lhsT=V[b][:, kt, 64 * half:64 * half + 64],
                                                    rhs=P[n % NPB][:, j, c0:512], start=(kt == 0),
                                                    stop=(kt == nk - 1), tile_position=(0, 64 * m))
                        if not fox:
                            for j, (kt, m) in enumerate(bt["blks"]):
                                c0 = 128 * (kt - 4 * g) if kt >= 4 * g else 0
                                ins = PE.matmul(ps[64 * m:64 * m + 64, banks[2], c0:512], lhsT=onesb[:, 0:64],
                                                rhs=P[n % NPB][:, j, c0:512], start=(kt == 0), stop=(kt == nk - 1),
                                                tile_position=(0, 64 * m))
                        pv_ev[n] = cPE.inc(ins)
                        head_last_pe[h] = pv_ev[n]

                    def ystore(h, g, e, nrow, row0):
                        ys = misc["ys"] % NYS
                        return ys

                    def epilogue_fox(n):
                        u = batches[n]["u"]
                        un = units[u]
                        h, g = un["h"], un["g"]
                        b = h % 2
                        a = acc_banks(u)[0]
                        tt = u % 2
                        W(DVE, pv_ev[n], misc.get("dl"))
                        e1 = cDVE.inc(DVE.reciprocal(out=Rt[64:128, :], in_=ps[64:128, a, :]))
                        W(DVE, e1, misc["tt_free"][tt])
                        e2 = cDVE.inc(DVE.tensor_tensor(out=Tt[tt][0:64, :], in0=ps[0:64, a, :], in1=Rt[64:128, :],
                                                        op=ALU.mult))
                        acc_free[a] = e2
                        misc["dl"] = e2
                        ys = misc["ys"] % NYS
                        misc["ys"] += 1
                        W(POOL, e2, st_free[ys])
                        e3 = cPOOL.inc(POOL.tensor_tensor(out=Yst[ys][0:64, :], in0=Tt[tt][0:64, :],
                                                          in1=Z[b][0:64, g * 512:(g + 1) * 512], op=ALU.mult))
                        misc["tt_free"][tt] = e3
                        head_last_epi[h] = e3
                        W(SP, e3)
                        st_free[ys] = cST[ys].inc(SP.dma_start(out=YT[h * 64:(h + 1) * 64, g * 512:(g + 1) * 512],
                                                               in_=Yst[ys][0:64, :]))

                    def epilogue_diff(n):
                        u = batches[n]["u"]
                        un = units[u]
                        h, g = un["h"], un["g"]
                        b = h % 2
                        W(ACT, pv_ev[n], misc.get("dc_free"))
                        eDc = cACT.inc(ACT.activation(out=Dc[:, :], in_=ps[:, 6, :], func=AF.Copy))
                        W(DVE, pv_ev[n], misc.get("dl"))
                        DVE.tensor_copy(out=Xc[:, :], in_=ps[:, 4, :])
                        eOc = cDVE.inc(DVE.tensor_copy(out=Yc[:, :], in_=ps[:, 5, :]))
                        acc_free[4] = [eOc, eDc]
                        W(DVE, eDc, eOc)
                        e1 = cDVE.inc(DVE.reciprocal(out=Rt[:, :], in_=Dc[:, :]))
                        misc["dc_free"] = e1
                        W(DVE, e1)
                        DVE.tensor_tensor(out=Xc[:, :], in0=Xc[:, :], in1=Rt[:, :], op=ALU.mult)
                        e2 = cDVE.inc(DVE.tensor_tensor(out=Yc[:, :], in0=Yc[:, :], in1=Rt[:, :], op=ALU.mult))
                        W(DVE, e2, misc["of_free"], misc.get("sq_done"))
                        DVE.tensor_scalar(out=Sh[0:64, :], in0=Xc[64:128, :], scalar1=SM[64:128, 0:1], scalar2=None,
                                          op0=ALU.mult)
                        e3 = cDVE.inc(DVE.tensor_copy(out=Sh[64:128, :], in_=Yc[0:64, :]))
                        W(DVE, e3)
                        DVE.tensor_tensor(out=Of[0:64, :], in0=Xc[0:64, :], in1=Sh[0:64, :], op=ALU.add)
                        eO = cDVE.inc(DVE.scalar_tensor_tensor(out=Of[64:128, :], in0=Yc[64:128, :],
                                                               scalar=SM[64:128, 0:1], in1=Sh[64:128, :],
                                                               op0=ALU.mult, op1=ALU.add))
                        misc["dl"] = eO
                        W(POOL, eO, misc.get("sq_read"), misc.get("pl"))
                        eS = cPOOL.inc(POOL.tensor_tensor(out=SQ[:, :], in0=Of[:, :], in1=Of[:, :], op=ALU.mult))
                        misc["sq_done"] = eS
                        misc["pl"] = eS

                        def pe_part():
                            W(PE, eS, misc["sub_free"])
                            ePS = cPE.inc(PE.matmul(ps[:, 7, :], lhsT=onesb, rhs=SQ[:, :], start=True, stop=True))
                            head_last_pe[h] = ePS
                            misc["sq_read"] = ePS

                            def act_part():
                                W(ACT, ePS, misc.get("al"))
                                eL = cACT.inc(ACT.activation(out=LNV[:, :], in_=ps[:, 7, :], func=AF.Ln,
                                                             scale=1.0 / 128.0, bias=EPS5))
                                misc["sub_free"] = eL
                                W(ACT, eL, misc["of_free"])
                                eR = cACT.inc(ACT.activation(out=RS[:, :], in_=LNV[:, :], func=AF.Exp, scale=-0.5))
                                misc["al"] = eR
                                W(DVE, eR, misc.get("dl"), misc.get("y1_read"))
                                eY1 = cDVE.inc(DVE.scalar_tensor_tensor(out=Y1[:, :], in0=Of[:, :], scalar=GSUBC,
                                                                        in1=RS[:, :], op0=ALU.mult, op1=ALU.mult))
                                misc["of_free"] = eY1
                                misc["dl"] = eY1
                                ys = misc["ys"] % NYS
                                misc["ys"] += 1
                                W(POOL, eY1, st_free[ys], misc.get("pl"))
                                eY = cPOOL.inc(POOL.tensor_tensor(out=Yst[ys][:, :], in0=Y1[:, :],
                                                                  in1=Z[b][:, g * 512:(g + 1) * 512], op=ALU.mult))
                                misc["y1_read"] = eY
                                misc["pl"] = eY
                                head_last_epi[h] = eY
                                W(SP, eY)
                                st_free[ys] = cST[ys].inc(SP.dma_start(
                                    out=YT[512 + h * 128:512 + (h + 1) * 128, g * 512:(g + 1) * 512],
                                    in_=Yst[ys][:, :]))

                            deferred.setdefault(n + 3, []).append(act_part)

                        deferred.setdefault(n + 2, []).append(pe_part)

                    def run_deferred(n):
                        while True:
                            ks = sorted([k for k in deferred if k <= n])
                            if not ks:
                                break
                            for fn in deferred.pop(ks[0]):
                                fn()

                    def ensure_loaded(h):
                        while misc["loaded"] < min(h, NH - 1):
                            hh = misc["loaded"] + 1
                            if hh >= 2:
                                head_free[hh % 2] = [head_last_pe.get(hh - 2), head_last_epi.get(hh - 2)]
                            load_head(hh)
                            misc["loaded"] = hh

                    ensure_loaded(1)
                    emit_qk(0)
                    for n in range(NB):
                        run_deferred(n)
                        if n + 1 < NB:
                            hn = units[batches[n + 1]["u"]]["h"]
                            emit_qk(n + 1)
                        emit_exp(n)
                        emit_pv(n)
                        if batches[n]["last"]:
                            (epilogue_fox if fox else epilogue_diff)(n)
                            un = units[batches[n]["u"]]
                            if un["g"] == NG - 1:
                                if not fox:
                                    run_deferred(NB + 10)
                                if un["h"] + 2 < NH:
                                    ensure_loaded(un["h"] + 2)
                    run_deferred(NB + 10)
                    W(SP, *st_free)

        attention("fox")
        attention("diff")

        with ExitStack() as es:
            sbt = lambda n, s, d: es.enter_context(nc.sbuf_tensor(n, s, d))
            sem = lambda n: es.enter_context(nc.semaphore(n))
            WB = sbt("WB", [128, 8, D], BF16)
            WO = sbt("WO", [128, 8, D], BF16)
            GP = sbt("GP", [128, D], F32)
            wst = [sbt(f"dwst{i}", [128, D], F32) for i in range(2)]
            Yin = [sbt(f"Yin{i}", [128, 8, 512], BF16) for i in range(2)]
            Gin = [sbt(f"Gin{i}", [128, 16, 512], BF16) for i in range(2)]
            MT = [sbt(f"MT{i}", [128, 8, 512], BF16) for i in range(2)]
            t1 = [sbt(f"dt1{i}", [128, 512], F32) for i in range(2)]
            t2 = [sbt(f"dt2{i}", [128, 512], F32) for i in range(2)]
            xin = [sbt(f"xin{i}", [128, D], F32) for i in range(2)]
            yn = [sbt(f"yn{i}", [128, D], F32) for i in range(2)]
            ob = [sbt(f"ob{i}", [128, D], F32) for i in range(2)]
            junk = sbt("djunk", [128, D], BF16)
            ss = sbt("dss", [128, NT], F32)
            lnv = sbt("dlnv", [128, NT], F32)
            rstd = sbt("drstd", [128, NT], F32)
            ps = es.enter_context(nc.psum_tensor("psd", [128, 8, 512], F32))
            cPE, cACT, cDVE, cPOOL = (Chan(sem(n)) for n in ("d_pe", "d_act", "d_dve", "d_pool"))
            cWL = [Chan(sem(f"d_wl{i}"), 16) for i in range(2)]
            cGL = [Chan(sem(f"d_gl{i}"), 16) for i in range(2)]
            cXL = [Chan(sem(f"d_xl{i}"), 16) for i in range(2)]
            cOS = [Chan(sem(f"d_os{i}"), 16) for i in range(2)]
            cL0 = Chan(sem("d_l0"), 16)
            block = es.enter_context(nc.Block())

            @block.sync
            def _(_sync):
                ev_gp = cL0.inc(SP.dma_start(out=GP[:, :], in_=prm_d[:, 274:274 + D]))
                cast_ev = [None, None]
                lastc = {}
                k = 0
                for (src, dstw) in ((wbr, WB), (wout, WO)):
                    for c in range(8):
                        sl = k % 2
                        W(SP, cast_ev[sl])
                        ev = cWL[sl].inc(SP.dma_start(out=wst[sl][:, :], in_=src[:, c, :]))
                        eng, ch = (DVE, cDVE) if k % 2 == 0 else (POOL, cPOOL)
                        W(eng, ev)
                        cast_ev[sl] = ch.inc(eng.tensor_copy(out=dstw[:, c, :], in_=wst[sl][:, :]))
                        lastc[k % 2] = cast_ev[sl]
                        k += 1
                W(PE, lastc[0], lastc[1])
                W(DVE, ev_gp)

                gl_ev = {}
                xld = {}
                g_free = [None, None]
                mt_free = [None, None]
                mt_ready = {}
                x_free = [None, None]
                ob_free = [None, None]
                tfree = [None, None]
                bank_free = {}
                st = {"pc": 0, "tc": 0}

                def load_group(G):
                    sl = G % 2
                    W(SP, g_free[sl])
                    gs = slice(G * 512, (G + 1) * 512)
                    cGL[sl].inc(SP.dma_start(out=Yin[sl][:, :, :], in_=YT[:, gs].rearrange("(c p) t -> p c t", p=128)))
                    gl_ev[G] = cGL[sl].inc(SP.dma_start(out=Gin[sl][:, :, :],
                                                        in_=GT[:, gs].rearrange("(c p) t -> p c t", p=128)))

                def merge_group(G):
                    sl = G % 2
                    W(PE, gl_ev[G])
                    last_dve = None
                    for j in range(8):
                        bA = (st["pc"] % 2) * 2
                        bB = bA + 1
                        st["pc"] += 1
                        W(PE, bank_free.get(bA), bank_free.get(bB))
                        for c in range(4):
                            ins = PE.matmul(ps[:, bA, :], lhsT=WB[:, c, j * 128:(j + 1) * 128], rhs=Yin[sl][:, c, :],
                                            start=(c == 0), stop=(c == 3))
                        for c in range(4):
                            ins = PE.matmul(ps[:, bB, :], lhsT=WB[:, 4 + c, j * 128:(j + 1) * 128],
                                            rhs=Yin[sl][:, 4 + c, :], start=(c == 0), stop=(c == 3))
                        mev = cPE.inc(ins)
                        ti = st["tc"] % 2
                        st["tc"] += 1
                        W(DVE, mev, tfree[ti])
                        ins = DVE.tensor_tensor(out=t1[ti][:, :], in0=ps[:, bA, :], in1=Gin[sl][:, j, :], op=ALU.mult)
                        bank_free[bA] = cDVE.inc(ins)
                        ins = DVE.tensor_tensor(out=t2[ti][:, :], in0=ps[:, bB, :], in1=Gin[sl][:, 8 + j, :],
                                                op=ALU.mult)
                        e2 = cDVE.inc(ins)
                        bank_free[bB] = e2
                        last_dve = e2
                        W(POOL, e2)
                        if j == 0:
                            W(POOL, mt_free[sl])
                        e3 = cPOOL.inc(POOL.tensor_tensor(out=MT[sl][:, j, :], in0=t1[ti][:, :], in1=t2[ti][:, :],
                                                          op=ALU.add))
                        tfree[ti] = e3
                    mt_ready[G] = e3
                    g_free[sl] = [last_dve, mev]

                def out_group(G):
                    sl = G % 2
                    W(PE, mt_ready[G])
                    for i in range(4):
                        T = G * 4 + i
                        xs = T % 2
                        if T == 0:
                            xld[0] = cXL[0].inc(SP.dma_start(out=xin[0][:, :], in_=x[0:128, :]))
                        if T + 1 < NT:
                            xn = (T + 1) % 2
                            W(SP, x_free[xn])
                            xld[T + 1] = cXL[xn].inc(SP.dma_start(out=xin[xn][:, :],
                                                                  in_=x[(T + 1) * 128:(T + 2) * 128, :]))
                        xev = xld[T]
                        b0 = 4 + (T % 2) * 2
                        W(PE, bank_free.get(b0))
                        for half in range(2):
                            for c in range(8):
                                ins = PE.matmul(ps[:, b0 + half, :], lhsT=MT[sl][:, c, i * 128:(i + 1) * 128],
                                                rhs=WO[:, c, half * 512:(half + 1) * 512], start=(c == 0),
                                                stop=(c == 7))
                        mev = cPE.inc(ins)
                        W(ACT, mev)
                        eq = cACT.inc(ACT.activation(out=junk[:, :], in_=ps[:, b0:b0 + 2, :].rearrange(
                            "p a b -> p (a b)"), func=AF.Square, accum_out=ss[:, T:T + 1]))
                        W(ACT, eq)
                        el = cACT.inc(ACT.activation(out=lnv[:, T:T + 1], in_=ss[:, T:T + 1], func=AF.Ln,
                                                     scale=1.0 / D, bias=EPS6))
                        W(ACT, el)
                        er = cACT.inc(ACT.activation(out=rstd[:, T:T + 1], in_=lnv[:, T:T + 1], func=AF.Exp,
                                                     scale=-0.5))
                        W(DVE, er, mev, x_free[xs])
                        ey = cDVE.inc(DVE.scalar_tensor_tensor(
                            out=yn[xs][:, :], in0=ps[:, b0:b0 + 2, :].rearrange("p a b -> p (a b)"),
                            scalar=rstd[:, T:T + 1], in1=GP[:, :], op0=ALU.mult, op1=ALU.mult))
                        bank_free[b0] = [ey, eq]
                        W(POOL, ey, xev, ob_free[xs])
                        eo = cPOOL.inc(POOL.tensor_tensor(out=ob[xs][:, :], in0=yn[xs][:, :], in1=xin[xs][:, :],
                                                          op=ALU.add))
                        x_free[xs] = eo
                        W(SP, eo)
                        ob_free[xs] = cOS[xs].inc(SP.dma_start(out=out_d[T * 128:(T + 1) * 128, :], in_=ob[xs][:, :]))
                        if i == 3:
                            mt_free[sl] = mev

                load_group(0)
                if NG > 1:
                    load_group(1)
                merge_group(0)
                for G in range(NG):
                    if G + 1 < NG:
                        merge_group(G + 1)
                    if G + 2 < NG:
                        load_group(G + 2)
                    out_group(G)
                W(SP, ob_free[0], ob_free[1])
    return nc


_CONST_CACHE = {}


def _consts(S):
    if S in _CONST_CACHE:
        return _CONST_CACHE[S]
    bf = ml_dtypes.bfloat16
    cb = np.zeros((128, 2304), np.float32)
    cb[:, 0:128] = np.eye(128, dtype=np.float32)
    cb[:, 128:256] = 1.0
    k = np.arange(128)[:, None]
    q = np.arange(512)[None, :]
    for j in range(4):
        cb[:, 256 + j * 512:256 + (j + 1) * 512] = np.where((128 * j + k) <= q, 0.0, -240000.0)
    cb = cb.astype(bf)
    cf = np.zeros((128, 384), np.float32)
    s_ = np.arange(128)[:, None]
    t_ = np.arange(128)[None, :]
    cf[:, 0:128] = (s_ <= t_).astype(np.float32)
    cf[:, 128:256] = 1.0
    cf[:, 256:384] = np.eye(128, dtype=np.float32)
    pos = np.arange(S, dtype=np.float32)
    inv_freq = (np.float32(10000.0) ** (-(np.arange(0, 64, 2, dtype=np.float32) / np.float32(64)))).astype(np.float32)
    ang = (pos[:, None] * inv_freq[None, :]).astype(np.float32)
    cos = np.cos(ang).astype(np.float32).T
    sin = np.sin(ang).astype(np.float32).T
    ropec = np.zeros((128, S), np.float32)
    ropes = np.zeros((128, S), np.float32)
    for p in range(128):
        i = p % 32
        ropec[p] = cos[i]
        ropes[p] = -sin[i] if p < 64 else sin[i]
    _CONST_CACHE[S] = (cb, cf, ropec, ropes)
    return _CONST_CACHE[S]


def _layout_weights(g_pre, w_in, b_forget, lq1, lk1, lq2, lk2, g_subln, w_branch, w_out, g_post):
    w = np.asarray(w_in[0], np.float32)
    qa, ka, va, fa, za = w[:, 0:512], w[:, 512:1024], w[:, 1024:1536], w[:, 1536:1544], w[:, 1544:2056]
    qb, kb, vb, zb, gates = w[:, 2056:2568], w[:, 2568:3080], w[:, 3080:3592], w[:, 3592:4104], w[:, 4104:6152]
    perm = np.arange(512).reshape(4, 2, 2, 32).transpose(0, 2, 1, 3).reshape(-1)
    w2 = np.concatenate([qa, ka, za, zb, gates, qb[:, perm], kb[:, perm], va, vb, fa], axis=1)
    assert w2.shape[1] == NW
    w2 = np.ascontiguousarray(w2.reshape(8, 128, NW).transpose(1, 0, 2))
    wbr = np.asarray(w_branch[0], np.float32).reshape(2 * 512, D)
    wbr = np.ascontiguousarray(wbr.reshape(8, 128, D).transpose(1, 0, 2))
    wo = np.ascontiguousarray(np.asarray(w_out[0], np.float32).reshape(8, 128, D).transpose(1, 0, 2))
    prm = np.zeros((128, NPRM), np.float32)
    prm[:, 0:8] = np.asarray(g_pre[0], np.float32).reshape(8, 128).T
    prm[:, 8:16] = np.asarray(b_forget[0], np.float32)[None, :]
    prm[:, 16:80] = np.asarray(lq1[0], np.float32)[None, :]
    prm[:, 80:144] = np.asarray(lk1[0], np.float32)[None, :]
    prm[:, 144:208] = np.asarray(lq2[0], np.float32)[None, :]
    prm[:, 208:272] = np.asarray(lk2[0], np.float32)[None, :]
    prm[:, 272] = np.asarray(g_subln[0], np.float32)
    prm[:, 274:274 + D] = np.asarray(g_post[0], np.float32)[None, :]
    return w2, wbr, wo, prm


_NC_CACHE = {}


def kernel(x, g_pre, w_in, b_forget, lambda_q1, lambda_k1, lambda_q2, lambda_k2, g_subln, w_branch, w_out,
           g_post):
    x = np.asarray(x, np.float32)
    B, S, _ = x.shape
    w2, wbr, wo, prm = _layout_weights(g_pre, w_in, b_forget, lambda_q1, lambda_k1, lambda_q2, lambda_k2,
                                       g_subln, w_branch, w_out, g_post)
    cb, cf, ropec, ropes = _consts(S)
    if S not in _NC_CACHE:
        _NC_CACHE[S] = build(S)
    nc = _NC_CACHE[S]
    in_maps = [dict(x=np.ascontiguousarray(x[b]), w2=w2, wbr=wbr, wout=wo, prm=prm, cb=cb, cf=cf,
                    ropec=ropec, ropes=ropes) for b in range(B)]
    res = run_bass_kernel_spmd(nc, in_maps, core_ids=list(range(B)))
    return np.stack([np.asarray(r["out"], np.float32) for r in res.results], axis=0)
```

```python
import math
from contextlib import ExitStack

import numpy as np
import ml_dtypes
import concourse.bass as bass
import concourse.mybir as mybir
from concourse.bass_utils import run_bass_kernel_spmd

F32 = mybir.dt.float32
BF16 = mybir.dt.bfloat16
ALU = mybir.AluOpType
AF = mybir.ActivationFunctionType

D = 1024
NFM = 40
COL_VA = NFM * 128
COL_VB = COL_VA + 512
COL_FA = COL_VB + 512
NW = COL_FA + 8
WCH = 769
NPRM = 274 + 1024
LAM_INIT = 0.8 - 0.6 * math.exp(-0.3 * 0.0)
NORM_EPS = 1e-6
SUBLN_EPS = 1e-5


class Ev:
    __slots__ = ("ch", "val")

    def __init__(self, ch, val):
        self.ch = ch
        self.val = val


class Chan:
    def __init__(self, sem, step=1):
        self.sem = sem
        self.step = step
        self.val = 0

    def inc(self, ins):
        self.val += self.step
        ins.then_inc(self.sem, self.step)
        return Ev(self, self.val)


class Emitter:
    def __init__(self, nc):
        self.nc = nc
        self.waited = {}

    def wait(self, eng, *evs):
        for ev in evs:
            if ev is None:
                continue
            if isinstance(ev, (list, tuple)):
                self.wait(eng, *ev)
                continue
            key = (id(eng), ev.ch)
            if self.waited.get(key, 0) >= ev.val:
                continue
            eng.wait_ge(ev.ch.sem, ev.val)
            self.waited[key] = ev.val


def build(S):
    NT = S // 128
    NG = S // 512
    NF = 8 * NT
    NCH = max(1, NF // 128)
    CHP = min(128, NF)

    nc = bass.Bass("TRN2", target_bir_lowering=False)
    x = nc.dram_tensor("x", [S, D], F32, kind="ExternalInput").ap()
    w2 = nc.dram_tensor("w2", [128, 8, NW], F32, kind="ExternalInput").ap()
    wbr = nc.dram_tensor("wbr", [128, 8, D], F32, kind="ExternalInput").ap()
    wout = nc.dram_tensor("wout", [128, 8, D], F32, kind="ExternalInput").ap()
    prm_d = nc.dram_tensor("prm", [128, NPRM], F32, kind="ExternalInput").ap()
    cb_d = nc.dram_tensor("cb", [128, 2304], BF16, kind="ExternalInput").ap()
    cf_d = nc.dram_tensor("cf", [128, 384], F32, kind="ExternalInput").ap()
    ropec_d = nc.dram_tensor("ropec", [128, S], F32, kind="ExternalInput").ap()
    ropes_d = nc.dram_tensor("ropes", [128, S], F32, kind="ExternalInput").ap()
    out_d = nc.dram_tensor("out", [S, D], F32, kind="ExternalOutput").ap()

    QF = nc.dram_tensor("QF", [512, S], BF16, kind="Internal").ap()
    KF = nc.dram_tensor("KF", [512, S], BF16, kind="Internal").ap()
    VF = nc.dram_tensor("VF", [S, 512], BF16, kind="Internal").ap()
    VD = nc.dram_tensor("VD", [S, 512], BF16, kind="Internal").ap()
    ZT = nc.dram_tensor("ZT", [1024, S], BF16, kind="Internal").ap()
    GT = nc.dram_tensor("GT", [2048, S], BF16, kind="Internal").ap()
    QD = nc.dram_tensor("QD", [512, S], BF16, kind="Internal").ap()
    KD = nc.dram_tensor("KD", [512, S], BF16, kind="Internal").ap()
    YT = nc.dram_tensor("YT", [1024, S], BF16, kind="Internal").ap()
    AUG = nc.dram_tensor("AUG", [4, 8 * S], BF16, kind="Internal").ap()

    em = Emitter(nc)
    W = em.wait
    PE, ACT, DVE, POOL, SP = nc.tensor, nc.scalar, nc.vector, nc.gpsimd, nc.sync

    with ExitStack() as g:
        g.enter_context(nc.allow_low_precision("bf16 matmul operands, fp32 accumulation"))
        g.enter_context(nc.allow_non_contiguous_dma("small strided V loads"))
        PRM = g.enter_context(nc.sbuf_tensor("PRM", [128, 274], F32))
        CB = g.enter_context(nc.sbuf_tensor("CB", [128, 2304], BF16))
        CF = g.enter_context(nc.sbuf_tensor("CF", [128, 384], F32))
        FA = g.enter_context(nc.sbuf_tensor("FA", [128, 8, NT], F32))
        SM = g.enter_context(nc.sbuf_tensor("SM", [128, 16], F32))
        identb = CB[:, 0:128]
        onesb = CB[:, 128:256]
        NEGLAM = SM[:, 0:1]
        GSUBC = SM[:, 1:2]
        EPS6 = SM[:, 2:3]
        EPS5 = SM[:, 3:4]
        ONE = SM[:, 4:5]

        def mask(j):
            return CB[:, 256 + j * 512:256 + (j + 1) * 512]

        with ExitStack() as es:
            sbt = lambda n, s, d: es.enter_context(nc.sbuf_tensor(n, s, d))
            sem = lambda n: es.enter_context(nc.semaphore(n))
            Wb = sbt("Wb", [128, 8, NW], BF16)
            wst = [sbt(f"wst{i}", [128, WCH], F32) for i in range(6)]
            xt = [sbt(f"xt{i}", [128, D], F32) for i in range(2)]
            hb = [sbt(f"hb{i}", [128, D], BF16) for i in range(4)]
            hT = [sbt(f"hT{i}", [128, 8, 512], BF16) for i in range(2)]
            rc = [sbt(f"rc{i}", [128, 512], F32) for i in range(2)]
            rs = [sbt(f"rs{i}", [128, 512], F32) for i in range(2)]
            NS = 6
            stage = [sbt(f"stg{i}", [128, 512], BF16) for i in range(NS)]
            tmp1s = [sbt(f"tmp1_{i}", [128, 512], F32) for i in range(2)]
            tmp2s = [sbt(f"tmp2_{i}", [128, 512], F32) for i in range(2)]
            junk = sbt("junk", [128, D], BF16)
            ss = sbt("ss", [128, NT], F32)
            lnv = sbt("lnv", [128, NT], F32)
            rstd = sbt("rstd", [128, NT], F32)
            lt = sbt("lt", [128, 64], F32)
            psm = es.enter_context(nc.psum_tensor("psm", [128, 4, 512], F32))
            psT = es.enter_context(nc.psum_tensor("psT", [128, 2, 8, 128], BF16))
            psf = es.enter_context(nc.psum_tensor("psf", [128, 512], F32))
            cPE, cACT, cDVE, cPOOL = (Chan(sem(n)) for n in ("a_pe", "a_act", "a_dve", "a_pool"))
            cLD0 = Chan(sem("a_ld0"), 16)
            cWL = [Chan(sem(f"a_wl{i}"), 16) for i in range(6)]
            cXL = [Chan(sem(f"a_xl{i}"), 16) for i in range(2)]
            cRL = [Chan(sem(f"a_rl{i}"), 16) for i in range(2)]
            cST = [Chan(sem(f"a_st{i}"), 16) for i in range(NS)]
            block = es.enter_context(nc.Block())

            @block.sync
            def _(_sync):
                SP.dma_start(out=PRM[:, :], in_=prm_d[:, 0:274]).then_inc(cLD0.sem, 16)
                SP.dma_start(out=CB[:, :], in_=cb_d[:, :]).then_inc(cLD0.sem, 16)
                ins = SP.dma_start(out=CF[:, :], in_=cf_d[:, :])
                cLD0.val = 32
                ev0 = cLD0.inc(ins)
                W(DVE, ev0)
                DVE.memset(SM[:, 2:3], NORM_EPS)
                DVE.memset(SM[:, 3:4], SUBLN_EPS)
                DVE.memset(SM[:, 4:5], 1.0)
                W(DVE, cDVE.inc(DVE.tensor_tensor(out=lt[:, :], in0=PRM[:, 16:80], in1=PRM[:, 80:144], op=ALU.mult)))
                ins = DVE.tensor_reduce(out=SM[:, 5:6], in_=lt[:, :], axis=mybir.AxisListType.X, op=ALU.add)
                e = cDVE.inc(ins)
                W(DVE, e)
                W(DVE, cDVE.inc(DVE.tensor_tensor(out=lt[:, :], in0=PRM[:, 144:208], in1=PRM[:, 208:272], op=ALU.mult)))
                ins = DVE.tensor_reduce(out=SM[:, 6:7], in_=lt[:, :], axis=mybir.AxisListType.X, op=ALU.add)
                e = cDVE.inc(ins)
                W(ACT, e)
                ins = ACT.activation(out=SM[:, 7:9], in_=SM[:, 5:7], func=AF.Exp)
                e = cACT.inc(ins)
                W(DVE, e)
                ins = DVE.tensor_tensor(out=SM[:, 9:10], in0=SM[:, 8:9], in1=SM[:, 7:8], op=ALU.subtract)
                e = cDVE.inc(ins)
                W(DVE, e)
                DVE.tensor_scalar(out=SM[:, 0:1], in0=SM[:, 9:10], scalar1=-LAM_INIT, scalar2=None, op0=ALU.add)
                ins = DVE.tensor_scalar(out=SM[:, 1:2], in0=PRM[:, 272:273], scalar1=1.0 - LAM_INIT, scalar2=None,
                                        op0=ALU.mult)
                ev_sm = cDVE.inc(ins)
                W(ACT, ev_sm)
                W(POOL, ev0)

                cast_ev = [None] * 6
                last_cast = {}
                chunk_ev = {}
                k = 0
                for q in range(NW // WCH):
                    for c in range(8):
                        sl = k % 6
                        W(SP, cast_ev[sl])
                        ev = cWL[sl].inc(SP.dma_start(out=wst[sl][:, :], in_=w2[:, c, q * WCH:(q + 1) * WCH]))
                        if k % 2 == 0:
                            W(DVE, ev)
                            ins = DVE.tensor_scalar(out=Wb[:, c, q * WCH:(q + 1) * WCH], in0=wst[sl][:, :],
                                                    scalar1=PRM[:, c:c + 1], scalar2=None, op0=ALU.mult)
                            cast_ev[sl] = cDVE.inc(ins)
                        else:
                            W(ACT, ev)
                            ins = ACT.activation(out=Wb[:, c, q * WCH:(q + 1) * WCH], in_=wst[sl][:, :],
                                                 func=AF.Copy, scale=PRM[:, c:c + 1])
                            cast_ev[sl] = cACT.inc(ins)
                        last_cast[k % 2] = cast_ev[sl]
                        chunk_ev[q] = [last_cast.get(0), last_cast.get(1)]
                        k += 1
                W(PE, ev0)

                def wready(col_hi):
                    W(PE, chunk_ev[(col_hi - 1) // WCH])

                xld_ev = {}
                sq_ev = {}
                rstd_ev = {}
                x_free = [None, None]
                tr_ev = {}
                trc_ev = {}
                h_evs = {}
                grp_mm_ev = {}
                rl_ev = {}
                rope_done = {}

                def tile_load(T):
                    sl = T % 2
                    W(SP, x_free[sl])
                    xld_ev[T] = cXL[sl].inc(SP.dma_start(out=xt[sl][:, :], in_=x[T * 128:(T + 1) * 128, :]))
                    W(ACT, xld_ev[T], sq_ev.get(T - 1))
                    ins = ACT.activation(out=junk[:, :], in_=xt[sl][:, :], func=AF.Square, accum_out=ss[:, T:T + 1])
                    sq_ev[T] = cACT.inc(ins)

                def tile_stats(T):
                    W(ACT, sq_ev[T])
                    ins = ACT.activation(out=lnv[:, T:T + 1], in_=ss[:, T:T + 1], func=AF.Ln, scale=1.0 / D,
                                         bias=EPS6)
                    e1 = cACT.inc(ins)
                    W(ACT, e1)
                    ins = ACT.activation(out=rstd[:, T:T + 1], in_=lnv[:, T:T + 1], func=AF.Exp, scale=-0.5)
                    rstd_ev[T] = cACT.inc(ins)

                def tile_hA(T):
                    sl = T % 2
                    hs = T % 4
                    W(DVE, rstd_ev[T], xld_ev[T], tr_ev.get(T - 4))
                    ins = DVE.tensor_scalar(out=hb[hs][:, :], in0=xt[sl][:, :], scalar1=rstd[:, T:T + 1],
                                            scalar2=None, op0=ALU.mult)
                    h_evs[T] = cDVE.inc(ins)
                    x_free[sl] = [h_evs[T], sq_ev[T]]

                def tile_hB(T):
                    G, i = divmod(T, 4)
                    sl = T % 2
                    hs = T % 4
                    W(PE, h_evs[T], trc_ev.get(T - 2))
                    for c in range(8):
                        ins = PE.transpose(out=psT[:, sl, c, :], in_=hb[hs][:, c * 128:(c + 1) * 128],
                                           identity=identb)
                    tr_ev[T] = cPE.inc(ins)
                    W(DVE, tr_ev[T], grp_mm_ev.get(G - 2))
                    ins = DVE.tensor_copy(out=hT[G % 2][:, :, i * 128:(i + 1) * 128], in_=psT[:, sl, :, :])
                    trc_ev[T] = cDVE.inc(ins)

                def rope_load(G):
                    sl = G % 2
                    W(SP, rope_done.get(G - 2))
                    cRL[sl].inc(SP.dma_start(out=rc[sl][:, :], in_=ropec_d[:, G * 512:(G + 1) * 512]))
                    rl_ev[G] = cRL[sl].inc(SP.dma_start(out=rs[sl][:, :], in_=ropes_d[:, G * 512:(G + 1) * 512]))

                state = {"pc": 0, "sc": 0}
                bank_free = [None] * 4
                stage_free = [None] * NS
                tmp_free = [None, None]
                rope_cnt = [0]
                fa_free = [None]

                def mm_fm(G, blk, bank):
                    wready((blk + 1) * 128)
                    W(PE, bank_free[bank])
                    for c in range(8):
                        ins = PE.matmul(psm[:, bank, :], lhsT=Wb[:, c, blk * 128:(blk + 1) * 128],
                                        rhs=hT[G % 2][:, c, :], start=(c == 0), stop=(c == 7))
                    return cPE.inc(ins)

                def mm_tm(G, i, col, bank):
                    wready(col + 512)
                    W(PE, bank_free[bank])
                    for c in range(8):
                        ins = PE.matmul(psm[:, bank, :], lhsT=hT[G % 2][:, c, i * 128:(i + 1) * 128],
                                        rhs=Wb[:, c, col:col + 512], start=(c == 0), stop=(c == 7))
                    return cPE.inc(ins)

                def store(slot, ev, dst):
                    W(SP, ev)
                    stage_free[slot] = cST[slot].inc(SP.dma_start(out=dst, in_=stage[slot][:, :]))

                def nbank():
                    b = state["pc"] % 4
                    state["pc"] += 1
                    return b

                def nslot():
                    s = state["sc"] % NS
                    state["sc"] += 1
                    return s

                def group_mm(G, mid=None):
                    W(PE, trc_ev[G * 4 + 3])
                    gs = slice(G * 512, (G + 1) * 512)
                    last_pe = None
                    fm = []
                    for b in range(4):
                        fm.append((b, "copy", QF[b * 128:(b + 1) * 128, gs]))
                    for b in range(4):
                        fm.append((4 + b, "copy", KF[b * 128:(b + 1) * 128, gs]))
                    for b in range(8):
                        fm.append((8 + b, "silu", ZT[b * 128:(b + 1) * 128, gs]))
                    for b in range(16):
                        fm.append((16 + b, "sigm", GT[b * 128:(b + 1) * 128, gs]))
                    for bi, (blk, kind, dst) in enumerate(fm):
                        if bi == 12 and mid is not None:
                            mid()
                        bank = nbank()
                        mev = mm_fm(G, blk, bank)
                        slot = nslot()
                        if kind == "copy":
                            W(DVE, mev, stage_free[slot])
                            ins = DVE.tensor_copy(out=stage[slot][:, :], in_=psm[:, bank, :])
                            eev = cDVE.inc(ins)
                        else:
                            W(ACT, mev, stage_free[slot])
                            ins = ACT.activation(out=stage[slot][:, :], in_=psm[:, bank, :],
                                                 func=AF.Silu if kind == "silu" else AF.Sigmoid)
                            eev = cACT.inc(ins)
                        bank_free[bank] = eev
                        store(slot, eev, dst)
                    for b in range(8):
                        dram = QD if b < 4 else KD
                        r0 = (b % 4) * 128
                        bankA = nbank()
                        mevA = mm_fm(G, 32 + b, bankA)
                        slot = nslot()
                        ti = rope_cnt[0] % 2
                        rope_cnt[0] += 1
                        tmp1, tmp2 = tmp1s[ti], tmp2s[ti]
                        W(DVE, mevA, rl_ev[G], tmp_free[ti])
                        DVE.tensor_tensor(out=tmp1[:, :], in0=psm[:, bankA, :], in1=rc[G % 2][:, :], op=ALU.mult)
                        DVE.tensor_tensor(out=tmp2[0:64, :], in0=psm[64:128, bankA, :], in1=rs[G % 2][0:64, :],
                                          op=ALU.mult)
                        ins = DVE.tensor_tensor(out=tmp2[64:128, :], in0=psm[0:64, bankA, :],
                                                in1=rs[G % 2][64:128, :], op=ALU.mult)
                        e2 = cDVE.inc(ins)
                        bank_free[bankA] = e2
                        W(POOL, e2, stage_free[slot])
                        ins = POOL.tensor_tensor(out=stage[slot][:, :], in0=tmp1[:, :], in1=tmp2[:, :], op=ALU.add)
                        e3 = cPOOL.inc(ins)
                        tmp_free[ti] = e3
                        W(SP, e3)
                        for (p0, d0) in ((0, 0), (32, 64), (64, 32), (96, 96)):
                            stage_free[slot] = cST[slot].inc(SP.dma_start(
                                out=dram[r0 + d0:r0 + d0 + 32, gs], in_=stage[slot][p0:p0 + 32, :]))
                    rope_done[G] = e2
                    W(PE, fa_free[0])
                    for i in range(4):
                        T = G * 4 + i
                        for (col, dstT) in ((COL_VA, VF), (COL_VB, VD)):
                            bank = nbank()
                            mev = mm_tm(G, i, col, bank)
                            slot = nslot()
                            W(ACT, mev, stage_free[slot])
                            ins = ACT.activation(out=stage[slot][:, :], in_=psm[:, bank, :], func=AF.Copy)
                            eev = cACT.inc(ins)
                            bank_free[bank] = eev
                            store(slot, eev, dstT[T * 128:(T + 1) * 128, :])
                        wready(NW)
                        for c in range(8):
                            ins = PE.matmul(psf[:, i * 8:(i + 1) * 8], lhsT=hT[G % 2][:, c, i * 128:(i + 1) * 128],
                                            rhs=Wb[:, c, COL_FA:COL_FA + 8], start=(c == 0), stop=(c == 7))
                        last_pe = cPE.inc(ins)
                    grp_mm_ev[G] = last_pe
                    W(DVE, last_pe)
                    for i in range(4):
                        ins = DVE.tensor_copy(out=FA[:, :, G * 4 + i], in_=psf[:, i * 8:(i + 1) * 8])
                    fa_free[0] = cDVE.inc(ins)

                def prepA(G):
                    rope_load(G)
                    for i in range(4):
                        T = G * 4 + i
                        tile_load(T)
                        tile_stats(T)
                        tile_hA(T)

                def prepB(G):
                    for i in range(4):
                        tile_hB(G * 4 + i)

                prepA(0)
                prepB(0)
                for G in range(NG):
                    if G + 1 < NG:
                        prepA(G + 1)
                        group_mm(G, mid=lambda G=G: prepB(G + 1))
                    else:
                        group_mm(G)
                W(SP, *stage_free)
                W(SP, fa_free[0])

        with ExitStack() as es:
            sbt = lambda n, s, d: es.enter_context(nc.sbuf_tensor(n, s, d))
            sem = lambda n: es.enter_context(nc.semaphore(n))
            E1 = sbt("E1", [128, NF], F32)
            LS = sbt("LS", [128, NF], F32)
            TOT = sbt("TOT", [128, 8, NT], F32)
            INC = sbt("INC", [128, 8, NT], F32)
            N1 = sbt("N1", [128, NF], F32)
            N8 = sbt("N8", [128, NF], F32)
            R1 = sbt("R1", [128, NF], F32)
            R2 = sbt("R2", [128, NF], F32)
            AUG4 = sbt("AUG4", [128, 4, NF], BF16)
            AUGT = sbt("AUGT", [128, 4 * NCH, 128], BF16)
            ps = es.enter_context(nc.psum_tensor("ps2", [128, 2, 512], F32))
            psA = es.enter_context(nc.psum_tensor("psA", [128, 4 * NCH, 128], BF16))
            cPE, cACT, cDVE = (Chan(sem(n)) for n in ("b_pe", "b_act", "b_dve"))
            cST = Chan(sem("b_st"), 16)
            block = es.enter_context(nc.Block())

            @block.sync
            def _(_sync):
                FAf = FA[:, :, :].rearrange("p h t -> p (h t)")
                for h in range(8):
                    ins = DVE.tensor_scalar(out=FA[:, h, :], in0=FA[:, h, :], scalar1=PRM[:, 8 + h:9 + h],
                                            scalar2=None, op0=ALU.add)
                e = cDVE.inc(ins)
                W(ACT, e)
                e = cACT.inc(ACT.activation(out=E1[:, :], in_=FAf, func=AF.Exp, scale=-1.0))
                W(ACT, e)
                e = cACT.inc(ACT.activation(out=LS[:, :], in_=E1[:, :], func=AF.Ln, bias=ONE, scale=1.0))
                W(PE, e)
                PE.matmul(ps[:, 0, 0:NF], lhsT=CF[:, 0:128], rhs=LS[:, :], start=True, stop=True)
                e = cPE.inc(PE.matmul(ps[:, 1, 0:NF], lhsT=CF[:, 128:256], rhs=LS[:, :], start=True, stop=True))
                W(DVE, e)
                TOTf = TOT[:, :, :].rearrange("p h t -> p (h t)")
                INCf = INC[:, :, :].rearrange("p h t -> p (h t)")
                e = cDVE.inc(DVE.tensor_copy(out=TOTf, in_=ps[:, 1, 0:NF]))
                W(DVE, e)
                for h in range(8):
                    ins = DVE.tensor_tensor_scan(out=INC[:, h, :], data0=CF[:, 128:128 + NT], data1=TOT[:, h, :],
                                                 initial=0.0, op0=ALU.mult, op1=ALU.add)
                e = cDVE.inc(ins)
                W(DVE, e)
                e = cDVE.inc(DVE.tensor_tensor(out=N1[:, :], in0=ps[:, 0, 0:NF], in1=INCf, op=ALU.add))
                W(DVE, e)
                e = cDVE.inc(DVE.tensor_tensor(out=N8[:, :], in0=N1[:, :], in1=TOTf, op=ALU.subtract))
                W(DVE, e)
                e = cDVE.inc(DVE.tensor_scalar(out=N8[:, :], in0=N8[:, :], scalar1=8.0, scalar2=None, op0=ALU.mult))
                W(DVE, e)
                e = cDVE.inc(DVE.tensor_copy(out=AUG4[:, 1, :], in_=N8[:, :]))
                W(DVE, e)
                e = cDVE.inc(DVE.tensor_tensor(out=R1[:, :], in0=N8[:, :], in1=AUG4[:, 1, :], op=ALU.subtract))
                W(DVE, e)
                e = cDVE.inc(DVE.tensor_copy(out=AUG4[:, 2, :], in_=R1[:, :]))
                W(DVE, e)
                e = cDVE.inc(DVE.tensor_tensor(out=R2[:, :], in0=R1[:, :], in1=AUG4[:, 2, :], op=ALU.subtract))
                W(DVE, e)
                DVE.tensor_copy(out=AUG4[:, 3, :], in_=R2[:, :])
                e = cDVE.inc(DVE.tensor_scalar(out=AUG4[:, 0, :], in0=AUG4[:, 1, :], scalar1=-1.0, scalar2=None,
                                               op0=ALU.mult))
                W(PE, e)
                for r in range(4):
                    for i in range(NCH):
                        ins = PE.transpose(out=psA[0:CHP, r * NCH + i, :], in_=AUG4[:, r, i * 128:i * 128 + CHP],
                                           identity=identb)
                e = cPE.inc(ins)
                W(DVE, e)
                e = cDVE.inc(DVE.tensor_copy(out=AUGT[0:CHP, :, :], in_=psA[0:CHP, :, :]))
                W(SP, e)
                for r in range(4):
                    for i in range(NCH):
                        dst = AUG[r:r + 1, i * 128 * 128:i * 128 * 128 + CHP * 128].rearrange(
                            "o (p t) -> (o p) t", t=128)
                        ev_st = cST.inc(SP.dma_start(out=dst, in_=AUGT[0:CHP, r * NCH + i, :]))
                W(SP, ev_st)

        def attention(kind):
            fox = kind == "fox"
            NH = 8 if fox else 4
            BW = 3 if fox else 2
            NPB = 4
            KR = 68 if fox else 64
            with ExitStack() as es:
                sbt = lambda n, s, d: es.enter_context(nc.sbuf_tensor(kind + "_" + n, s, d))
                sem = lambda n: es.enter_context(nc.semaphore(kind + "_" + n))
                QT = [sbt(f"QT{i}", [128, S], BF16) for i in range(2)]
                KT = [sbt(f"KT{i}", [128, S], BF16) for i in range(2)]
                V = [sbt(f"V{i}", [128, NT, 128], BF16) for i in range(2)]
                Z = [sbt(f"Z{i}", [128, S], BF16) for i in range(2)]
                P = [sbt(f"P{i}", [128, BW, 512], BF16) for i in range(NPB)]
                Rt = sbt("Rt", [128, 512], F32)
                Tt = [sbt(f"Tt{i}", [128, 512], F32) for i in range(2)]
                NYS = 2
                Yst = [sbt(f"Yst{i}", [128, 512], BF16) for i in range(NYS)]
                if not fox:
                    Dc = sbt("Dc", [128, 512], F32)
                    Xc = sbt("Xc", [128, 512], F32)
                    Yc = sbt("Yc", [128, 512], F32)
                    Sh = sbt("Sh", [128, 512], F32)
                    Of = sbt("Of", [128, 512], F32)
                    SQ = sbt("SQ", [128, 512], BF16)
                    LNV = sbt("LNV", [128, 512], F32)
                    RS = sbt("RS", [128, 512], F32)
                    Y1 = sbt("Y1", [128, 512], F32)
                ps = es.enter_context(nc.psum_tensor(kind + "_psat", [128, 8, 512], F32))
                cPE, cACT, cDVE, cPOOL = (Chan(sem(n)) for n in ("c_pe", "c_act", "c_dve", "c_pool"))
                cLD = [Chan(sem(f"c_ld{i}"), 16) for i in range(2)]
                cST = [Chan(sem(f"c_st{i}"), 16) for i in range(NYS)]
                block = es.enter_context(nc.Block())

                @block.sync
                def _(_sync):
                    init_ev = None
                    if fox:
                        for b in range(2):
                            DVE.memset(QT[b][64:68, :], 1.0)
                            DVE.memset(KT[b][64:68, :], 1.0)
                            ins = POOL.memset(V[b][:, :, 64:128], 1.0)
                        init_ev = [cDVE.inc(DVE.memset(Rt[:, :], 1.0)), cPOOL.inc(ins)]
                    head_free = [None, None]
                    ld_ev = {}

                    def load_head(h):
                        b = h % 2
                        W(SP, head_free[b], init_ev)
                        c = cLD[b]
                        if fox:
                            c.inc(SP.dma_start(out=QT[b][0:64, :], in_=QF[h * 64:(h + 1) * 64, :]))
                            c.inc(SP.dma_start(out=QT[b][64:65, :], in_=AUG[0:1, h * S:(h + 1) * S]))
                            c.inc(SP.dma_start(out=KT[b][0:64, :], in_=KF[h * 64:(h + 1) * 64, :]))
                            c.inc(SP.dma_start(out=KT[b][65:68, :], in_=AUG[1:4, h * S:(h + 1) * S]))
                            vsrc = VF.rearrange("(n p) c -> p n c", p=128)
                            nsp = 4 if NT >= 4 else 1
                            for q in range(nsp):
                                a0, a1 = q * NT // nsp, (q + 1) * NT // nsp
                                c.inc(SP.dma_start(out=V[b][:, a0:a1, 0:64], in_=vsrc[:, a0:a1, h * 64:(h + 1) * 64]))
                            ld_ev[h] = c.inc(SP.dma_start(out=Z[b][0:64, :], in_=ZT[h * 64:(h + 1) * 64, :]))
                        else:
                            c.inc(SP.dma_start(out=QT[b][:, :], in_=QD[h * 128:(h + 1) * 128, :]))
                            c.inc(SP.dma_start(out=KT[b][:, :], in_=KD[h * 128:(h + 1) * 128, :]))
                            vsrc = VD.rearrange("(n p) c -> p n c", p=128)
                            nsp = 4 if NT >= 4 else 1
                            for q in range(nsp):
                                a0, a1 = q * NT // nsp, (q + 1) * NT // nsp
                                c.inc(SP.dma_start(out=V[b][:, a0:a1, :], in_=vsrc[:, a0:a1, h * 128:(h + 1) * 128]))
                            ld_ev[h] = c.inc(SP.dma_start(out=Z[b][:, :],
                                                          in_=ZT[512 + h * 128:512 + (h + 1) * 128, :]))

                    units = []
                    batches = []
                    for h in range(NH):
                        for g in range(NG):
                            u = len(units)
                            nk = 4 * (g + 1)
                            units.append(dict(h=h, g=g, nk=nk))
                            if fox:
                                blks = [(kt, 0) for kt in range(nk)]
                            else:
                                blks = [(kt, m) for kt in range(nk) for m in range(2)]
                            for s0 in range(0, len(blks), BW):
                                batches.append(dict(u=u, blks=blks[s0:s0 + BW], first=(s0 == 0),
                                                    last=(s0 + BW >= len(blks))))
                    NB = len(batches)
                    qk_ev = {}
                    exp_ev = {}
                    mask_ev = {}
                    pv_ev = {}
                    acc_free = {}
                    deferred = {}
                    st_free = [None] * NYS
                    head_last_pe = {}
                    head_last_epi = {}
                    misc = {"ys": 0, "of_free": None, "sub_free": None, "tt_free": [None, None], "loaded": -1}

                    def rows(m):
                        return slice(0, KR) if (fox or m == 0) else slice(64, 128)

                    def acc_banks(u):
                        if fox:
                            return (6 + u % 2,)
                        return (4, 5, 6)

                    def emit_qk(n):
                        bt = batches[n]
                        un = units[bt["u"]]
                        h, g = un["h"], un["g"]
                        b = h % 2
                        W(PE, ld_ev[h], exp_ev.get(n - 2))
                        sbase = (n % 2) * BW
                        for j, (kt, m) in enumerate(bt["blks"]):
                            band = kt >= 4 * g
                            c0 = 128 * (kt - 4 * g) if band else 0
                            r = rows(m)
                            ins = PE.matmul(ps[:, sbase + j, c0:512], lhsT=KT[b][r, kt * 128:(kt + 1) * 128],
                                            rhs=QT[b][r, g * 512 + c0:(g + 1) * 512], start=True, stop=True)
                        for j, (kt, m) in enumerate(bt["blks"]):
                            if kt >= 4 * g:
                                c1 = 128 * (kt - 4 * g + 1)
                                ins = PE.matmul(ps[:, sbase + j, 0:c1], lhsT=identb, rhs=mask(kt - 4 * g)[:, 0:c1],
                                                start=False, stop=True, skip_group_check=True)
                        qk_ev[n] = cPE.inc(ins)

                    def emit_exp(n):
                        bt = batches[n]
                        nb = len(bt["blks"])
                        sbase = (n % 2) * BW
                        W(ACT, qk_ev[n], pv_ev.get(n - NPB))
                        g = units[bt["u"]]["g"]
                        c0 = 0
                        if not fox:
                            kt0 = bt["blks"][0][0]
                            c0 = 128 * (kt0 - 4 * g) if kt0 >= 4 * g else 0
                        ins = ACT.activation(out=P[n % NPB][:, 0:nb, c0:512], in_=ps[:, sbase:sbase + nb, c0:512],
                                             func=AF.Exp, scale=0.125)
                        exp_ev[n] = cACT.inc(ins)

                    def emit_mask(n):
                        bt = batches[n]
                        un = units[bt["u"]]
                        g = un["g"]
                        ins = None
                        for j, kt in enumerate(bt["kts"]):
                            if kt >= 4 * g:
                                W(POOL, exp_ev[n])
                                ins = POOL.tensor_tensor(out=P[n % NPB][:, j, :], in0=P[n % NPB][:, j, :],
                                                         in1=mask(kt - 4 * g), op=ALU.mult)
                        if ins is not None:
                            mask_ev[n] = cPOOL.inc(ins)

                    def emit_pv(n):
                        bt = batches[n]
                        u = bt["u"]
                        un = units[u]
                        h, nk = un["h"], un["nk"]
                        b = h % 2
                        banks = acc_banks(u)
                        W(PE, exp_ev[n], mask_ev.get(n))
                        if bt["first"]:
                            W(PE, acc_free.get(banks[0]))
                        g = un["g"]
                        if fox:
                            for j, (kt, m) in enumerate(bt["blks"]):
                                c0 = 128 * (kt - 4 * g) if kt >= 4 * g else 0
                                ins = PE.matmul(ps[:, banks[0], c0:512], lhsT=V[b][:, kt, :],
                                                rhs=P[n % NPB][:, j, c0:512], start=(kt == 0), stop=(kt == nk - 1))
                        else:
                            for half in range(2):
                                for j, (kt, m) in enumerate(bt["blks"]):
                                    c0 = 128 * (kt - 4 * g) if kt >= 4 * g else 0
                                    ins = PE.matmul(ps[64 * m:64 * m + 64, banks[half], c0:512],
                                                    lhsT=V[b][:, kt, 64 * half:64 * half + 64],
                                                    rhs=P[n % NPB][:, j, c0:512], start=(kt == 0),
                                                    stop=(kt == nk - 1), tile_position=(0, 64 * m))
                        if not fox:
                            for j, (kt, m) in enumerate(bt["blks"]):
                                c0 = 128 * (kt - 4 * g) if kt >= 4 * g else 0
                                ins = PE.matmul(ps[64 * m:64 * m + 64, banks[2], c0:512], lhsT=onesb[:, 0:64],
                                                rhs=P[n % NPB][:, j, c0:512], start=(kt == 0), stop=(kt == nk - 1),
                                                tile_position=(0, 64 * m))
                        pv_ev[n] = cPE.inc(ins)
                        head_last_pe[h] = pv_ev[n]

                    def ystore(h, g, e, nrow, row0):
                        ys = misc["ys"] % NYS
                        return ys

                    def epilogue_fox(n):
                        u = batches[n]["u"]
                        un = units[u]
                        h, g = un["h"], un["g"]
                        b = h % 2
                        a = acc_banks(u)[0]
                        tt = u % 2
                        W(DVE, pv_ev[n], misc.get("dl"))
                        e1 = cDVE.inc(DVE.reciprocal(out=Rt[64:128, :], in_=ps[64:128, a, :]))
                        W(DVE, e1, misc["tt_free"][tt])
                        e2 = cDVE.inc(DVE.tensor_tensor(out=Tt[tt][0:64, :], in0=ps[0:64, a, :], in1=Rt[64:128, :],
                                                        op=ALU.mult))
                        acc_free[a] = e2
                        misc["dl"] = e2
                        ys = misc["ys"] % NYS
                        misc["ys"] += 1
                        W(POOL, e2, st_free[ys])
                        e3 = cPOOL.inc(POOL.tensor_tensor(out=Yst[ys][0:64, :], in0=Tt[tt][0:64, :],
                                                          in1=Z[b][0:64, g * 512:(g + 1) * 512], op=ALU.mult))
                        misc["tt_free"][tt] = e3
                        head_last_epi[h] = e3
                        W(SP, e3)
                        st_free[ys] = cST[ys].inc(SP.dma_start(out=YT[h * 64:(h + 1) * 64, g * 512:(g + 1) * 512],
                                                               in_=Yst[ys][0:64, :]))

                    def epilogue_diff(n):
                        u = batches[n]["u"]
                        un = units[u]
                        h, g = un["h"], un["g"]
                        b = h % 2
                        W(ACT, pv_ev[n], misc.get("dc_free"))
                        eDc = cACT.inc(ACT.activation(out=Dc[:, :], in_=ps[:, 6, :], func=AF.Copy))
                        W(DVE, pv_ev[n], misc.get("dl"))
                        DVE.tensor_copy(out=Xc[:, :], in_=ps[:, 4, :])
                        eOc = cDVE.inc(DVE.tensor_copy(out=Yc[:, :], in_=ps[:, 5, :]))
                        acc_free[4] = [eOc, eDc]
                        W(DVE, eDc, eOc)
                        e1 = cDVE.inc(DVE.reciprocal(out=Rt[:, :], in_=Dc[:, :]))
                        misc["dc_free"] = e1
                        W(DVE, e1)
                        DVE.tensor_tensor(out=Xc[:, :], in0=Xc[:, :], in1=Rt[:, :], op=ALU.mult)
                        e2 = cDVE.inc(DVE.tensor_tensor(out=Yc[:, :], in0=Yc[:, :], in1=Rt[:, :], op=ALU.mult))
                        W(DVE, e2, misc["of_free"], misc.get("sq_done"))
                        DVE.tensor_scalar(out=Sh[0:64, :], in0=Xc[64:128, :], scalar1=SM[64:128, 0:1], scalar2=None,
                                          op0=ALU.mult)
                        e3 = cDVE.inc(DVE.tensor_copy(out=Sh[64:128, :], in_=Yc[0:64, :]))
                        W(DVE, e3)
                        DVE.tensor_tensor(out=Of[0:64, :], in0=Xc[0:64, :], in1=Sh[0:64, :], op=ALU.add)
                        eO = cDVE.inc(DVE.scalar_tensor_tensor(out=Of[64:128, :], in0=Yc[64:128, :],
                                                               scalar=SM[64:128, 0:1], in1=Sh[64:128, :],
                                                               op0=ALU.mult, op1=ALU.add))
                        misc["dl"] = eO
                        W(POOL, eO, misc.get("sq_read"), misc.get("pl"))
                        eS = cPOOL.inc(POOL.tensor_tensor(out=SQ[:, :], in0=Of[:, :], in1=Of[:, :], op=ALU.mult))
                        misc["sq_done"] = eS
                        misc["pl"] = eS

                        def pe_part():
                            W(PE, eS, misc["sub_free"])
                            ePS = cPE.inc(PE.matmul(ps[:, 7, :], lhsT=onesb, rhs=SQ[:, :], start=True, stop=True))
                            head_last_pe[h] = ePS
                            misc["sq_read"] = ePS

                            def act_part():
                                W(ACT, ePS, misc.get("al"))
                                eL = cACT.inc(ACT.activation(out=LNV[:, :], in_=ps[:, 7, :], func=AF.Ln,
                                                             scale=1.0 / 128.0, bias=EPS5))
                                misc["sub_free"] = eL
                                W(ACT, eL, misc["of_free"])
                                eR = cACT.inc(ACT.activation(out=RS[:, :], in_=LNV[:, :], func=AF.Exp, scale=-0.5))
                                misc["al"] = eR
                                W(DVE, eR, misc.get("dl"), misc.get("y1_read"))
                                eY1 = cDVE.inc(DVE.scalar_tensor_tensor(out=Y1[:, :], in0=Of[:, :], scalar=GSUBC,
                                                                        in1=RS[:, :], op0=ALU.mult, op1=ALU.mult))
                                misc["of_free"] = eY1
                                misc["dl"] = eY1
                                ys = misc["ys"] % NYS
                                misc["ys"] += 1
                                W(POOL, eY1, st_free[ys], misc.get("pl"))
                                eY = cPOOL.inc(POOL.tensor_tensor(out=Yst[ys][:, :], in0=Y1[:, :],
                                                                  in1=Z[b][:, g * 512:(g + 1) * 512], op=ALU.mult))
                                misc["y1_read"] = eY
                                misc["pl"] = eY
                                head_last_epi[h] = eY
                                W(SP, eY)
                                st_free[ys] = cST[ys].inc(SP.dma_start(
                                    out=YT[512 + h * 128:512 + (h + 1) * 128, g * 512:(g + 1) * 512],
                                    in_=Yst[ys][:, :]))

                            deferred.setdefault(n + 11, []).append(act_part)

                        deferred.setdefault(n + 9, []).append(pe_part)

                    def run_deferred(n):
                        while True:
                            ks = sorted([k for k in deferred if k <= n])
                            if not ks:
                                break
                            for fn in deferred.pop(ks[0]):
                                fn()

                    def ensure_loaded(h):
                        while misc["loaded"] < min(h, NH - 1):
                            hh = misc["loaded"] + 1
                            if hh >= 2:
                                head_free[hh % 2] = [head_last_pe.get(hh - 2), head_last_epi.get(hh - 2)]
                            load_head(hh)
                            misc["loaded"] = hh

                    ensure_loaded(1)
                    emit_qk(0)
                    if NB > 1:
                        emit_qk(1)
                    for n in range(NB):
                        run_deferred(n)
                        emit_exp(n)
                        if n + 2 < NB:
                            emit_qk(n + 2)
                        emit_pv(n)
                        if batches[n]["last"]:
                            if not fox:
                                run_deferred(NB + 10)
                            (epilogue_fox if fox else epilogue_diff)(n)
                            un = units[batches[n]["u"]]
                            if un["g"] == NG - 1:
                                if not fox:
                                    run_deferred(NB + 10)
                                if un["h"] + 2 < NH:
                                    ensure_loaded(un["h"] + 2)
                    run_deferred(NB + 10)
                    W(SP, *st_free)

        attention("fox")
        attention("diff")

        with ExitStack() as es:
            sbt = lambda n, s, d: es.enter_context(nc.sbuf_tensor(n, s, d))
            sem = lambda n: es.enter_context(nc.semaphore(n))
            WB = sbt("WB", [128, 8, D], BF16)
            WO = sbt("WO", [128, 8, D], BF16)
            GP = sbt("GP", [128, D], F32)
            wst = [sbt(f"dwst{i}", [128, D], F32) for i in range(2)]
            Yin = [sbt(f"Yin{i}", [128, 8, 512], BF16) for i in range(2)]
            Gin = [sbt(f"Gin{i}", [128, 16, 512], BF16) for i in range(2)]
            MT = [sbt(f"MT{i}", [128, 8, 512], BF16) for i in range(2)]
            t1 = [sbt(f"dt1{i}", [128, 512], F32) for i in range(2)]
            t2 = [sbt(f"dt2{i}", [128, 512], F32) for i in range(2)]
            xin = [sbt(f"xin{i}", [128, D], F32) for i in range(2)]
            yn = [sbt(f"yn{i}", [128, D], F32) for i in range(2)]
            ob = [sbt(f"ob{i}", [128, D], F32) for i in range(2)]
            junk = sbt("djunk", [128, D], BF16)
            ss = sbt("dss", [128, NT], F32)
            lnv = sbt("dlnv", [128, NT], F32)
            rstd = sbt("drstd", [128, NT], F32)
            ps = es.enter_context(nc.psum_tensor("psd", [128, 8, 512], F32))
            cPE, cACT, cDVE, cPOOL = (Chan(sem(n)) for n in ("d_pe", "d_act", "d_dve", "d_pool"))
            cWL = [Chan(sem(f"d_wl{i}"), 16) for i in range(2)]
            cGL = [Chan(sem(f"d_gl{i}"), 16) for i in range(2)]
            cXL = [Chan(sem(f"d_xl{i}"), 16) for i in range(2)]
            cOS = [Chan(sem(f"d_os{i}"), 16) for i in range(2)]
            cL0 = Chan(sem("d_l0"), 16)
            block = es.enter_context(nc.Block())

            @block.sync
            def _(_sync):
                ev_gp = cL0.inc(SP.dma_start(out=GP[:, :], in_=prm_d[:, 274:274 + D]))
                cast_ev = [None, None]
                lastc = {}
                k = 0
                for (src, dstw) in ((wbr, WB), (wout, WO)):
                    for c in range(8):
                        sl = k % 2
                        W(SP, cast_ev[sl])
                        ev = cWL[sl].inc(SP.dma_start(out=wst[sl][:, :], in_=src[:, c, :]))
                        eng, ch = (DVE, cDVE) if k % 2 == 0 else (POOL, cPOOL)
                        W(eng, ev)
                        cast_ev[sl] = ch.inc(eng.tensor_copy(out=dstw[:, c, :], in_=wst[sl][:, :]))
                        lastc[k % 2] = cast_ev[sl]
                        k += 1
                W(PE, lastc[0], lastc[1])
                W(DVE, ev_gp)

                gl_ev = {}
                xld = {}
                g_free = [None, None]
                mt_free = [None, None]
                mt_ready = {}
                x_free = [None, None]
                ob_free = [None, None]
                tfree = [None, None]
                bank_free = {}
                st = {"pc": 0, "tc": 0}

                def load_group(G):
                    sl = G % 2
                    W(SP, g_free[sl])
                    gs = slice(G * 512, (G + 1) * 512)
                    cGL[sl].inc(SP.dma_start(out=Yin[sl][:, :, :], in_=YT[:, gs].rearrange("(c p) t -> p c t", p=128)))
                    gl_ev[G] = cGL[sl].inc(SP.dma_start(out=Gin[sl][:, :, :],
                                                        in_=GT[:, gs].rearrange("(c p) t -> p c t", p=128)))

                def merge_group(G):
                    sl = G % 2
                    W(PE, gl_ev[G])
                    last_dve = None
                    for j in range(8):
                        bA = (st["pc"] % 2) * 2
                        bB = bA + 1
                        st["pc"] += 1
                        W(PE, bank_free.get(bA), bank_free.get(bB))
                        for c in range(4):
                            ins = PE.matmul(ps[:, bA, :], lhsT=WB[:, c, j * 128:(j + 1) * 128], rhs=Yin[sl][:, c, :],
                                            start=(c == 0), stop=(c == 3))
                        for c in range(4):
                            ins = PE.matmul(ps[:, bB, :], lhsT=WB[:, 4 + c, j * 128:(j + 1) * 128],
                                            rhs=Yin[sl][:, 4 + c, :], start=(c == 0), stop=(c == 3))
                        mev = cPE.inc(ins)
                        ti = st["tc"] % 2
                        st["tc"] += 1
                        W(DVE, mev, tfree[ti])
                        ins = DVE.tensor_tensor(out=t1[ti][:, :], in0=ps[:, bA, :], in1=Gin[sl][:, j, :], op=ALU.mult)
                        bank_free[bA] = cDVE.inc(ins)
                        ins = DVE.tensor_tensor(out=t2[ti][:, :], in0=ps[:, bB, :], in1=Gin[sl][:, 8 + j, :],
                                                op=ALU.mult)
                        e2 = cDVE.inc(ins)
                        bank_free[bB] = e2
                        last_dve = e2
                        W(POOL, e2)
                        if j == 0:
                            W(POOL, mt_free[sl])
                        e3 = cPOOL.inc(POOL.tensor_tensor(out=MT[sl][:, j, :], in0=t1[ti][:, :], in1=t2[ti][:, :],
                                                          op=ALU.add))
                        tfree[ti] = e3
                    mt_ready[G] = e3
                    g_free[sl] = [last_dve, mev]

                def out_group(G):
                    sl = G % 2
                    W(PE, mt_ready[G])
                    for i in range(4):
                        T = G * 4 + i
                        xs = T % 2
                        if T == 0:
                            xld[0] = cXL[0].inc(SP.dma_start(out=xin[0][:, :], in_=x[0:128, :]))
                        if T + 1 < NT:
                            xn = (T + 1) % 2
                            W(SP, x_free[xn])
                            xld[T + 1] = cXL[xn].inc(SP.dma_start(out=xin[xn][:, :],
                                                                  in_=x[(T + 1) * 128:(T + 2) * 128, :]))
                        xev = xld[T]
                        b0 = 4 + (T % 2) * 2
                        W(PE, bank_free.get(b0))
                        for half in range(2):
                            for c in range(8):
                                ins = PE.matmul(ps[:, b0 + half, :], lhsT=MT[sl][:, c, i * 128:(i + 1) * 128],
                                                rhs=WO[:, c, half * 512:(half + 1) * 512], start=(c == 0),
                                                stop=(c == 7))
                        mev = cPE.inc(ins)
                        W(ACT, mev)
                        eq = cACT.inc(ACT.activation(out=junk[:, :], in_=ps[:, b0:b0 + 2, :].rearrange(
                            "p a b -> p (a b)"), func=AF.Square, accum_out=ss[:, T:T + 1]))
                        W(ACT, eq)
                        el = cACT.inc(ACT.activation(out=lnv[:, T:T + 1], in_=ss[:, T:T + 1], func=AF.Ln,
                                                     scale=1.0 / D, bias=EPS6))
                        W(ACT, el)
                        er = cACT.inc(ACT.activation(out=rstd[:, T:T + 1], in_=lnv[:, T:T + 1], func=AF.Exp,
                                                     scale=-0.5))
                        W(DVE, er, mev, x_free[xs])
                        ey = cDVE.inc(DVE.scalar_tensor_tensor(
                            out=yn[xs][:, :], in0=ps[:, b0:b0 + 2, :].rearrange("p a b -> p (a b)"),
                            scalar=rstd[:, T:T + 1], in1=GP[:, :], op0=ALU.mult, op1=ALU.mult))
                        bank_free[b0] = [ey, eq]
                        W(POOL, ey, xev, ob_free[xs])
                        eo = cPOOL.inc(POOL.tensor_tensor(out=ob[xs][:, :], in0=yn[xs][:, :], in1=xin[xs][:, :],
                                                          op=ALU.add))
                        x_free[xs] = eo
                        W(SP, eo)
                        ob_free[xs] = cOS[xs].inc(SP.dma_start(out=out_d[T * 128:(T + 1) * 128, :], in_=ob[xs][:, :]))
                        if i == 3:
                            mt_free[sl] = mev

                load_group(0)
                if NG > 1:
                    load_group(1)
                merge_group(0)
                for G in range(NG):
                    if G + 1 < NG:
                        merge_group(G + 1)
                    if G + 2 < NG:
                        load_group(G + 2)
                    out_group(G)
                W(SP, ob_free[0], ob_free[1])
    return nc


_CONST_CACHE = {}


def _consts(S):
    if S in _CONST_CACHE:
        return _CONST_CACHE[S]
    bf = ml_dtypes.bfloat16
    cb = np.zeros((128, 2304), np.float32)
    cb[:, 0:128] = np.eye(128, dtype=np.float32)
    cb[:, 128:256] = 1.0
    k = np.arange(128)[:, None]
    q = np.arange(512)[None, :]
    for j in range(4):
        cb[:, 256 + j * 512:256 + (j + 1) * 512] = np.where((128 * j + k) <= q, 0.0, -240000.0)
    cb = cb.astype(bf)
    cf = np.zeros((128, 384), np.float32)
    s_ = np.arange(128)[:, None]
    t_ = np.arange(128)[None, :]
    cf[:, 0:128] = (s_ <= t_).astype(np.float32)
    cf[:, 128:256] = 1.0
    cf[:, 256:384] = np.eye(128, dtype=np.float32)
    pos = np.arange(S, dtype=np.float32)
    inv_freq = (np.float32(10000.0) ** (-(np.arange(0, 64, 2, dtype=np.float32) / np.float32(64)))).astype(np.float32)
    ang = (pos[:, None] * inv_freq[None, :]).astype(np.float32)
    cos = np.cos(ang).astype(np.float32).T
    sin = np.sin(ang).astype(np.float32).T
    ropec = np.zeros((128, S), np.float32)
    ropes = np.zeros((128, S), np.float32)
    for p in range(128):
        i = p % 32
        ropec[p] = cos[i]
        ropes[p] = -sin[i] if p < 64 else sin[i]
    _CONST_CACHE[S] = (cb, cf, ropec, ropes)
    return _CONST_CACHE[S]


def _layout_weights(g_pre, w_in, b_forget, lq1, lk1, lq2, lk2, g_subln, w_branch, w_out, g_post):
    w = np.asarray(w_in[0], np.float32)
    qa, ka, va, fa, za = w[:, 0:512], w[:, 512:1024], w[:, 1024:1536], w[:, 1536:1544], w[:, 1544:2056]
    qb, kb, vb, zb, gates = w[:, 2056:2568], w[:, 2568:3080], w[:, 3080:3592], w[:, 3592:4104], w[:, 4104:6152]
    perm = np.arange(512).reshape(4, 2, 2, 32).transpose(0, 2, 1, 3).reshape(-1)
    w2 = np.concatenate([qa, ka, za, zb, gates, qb[:, perm], kb[:, perm], va, vb, fa], axis=1)
    assert w2.shape[1] == NW
    w2 = np.ascontiguousarray(w2.reshape(8, 128, NW).transpose(1, 0, 2))
    wbr = np.asarray(w_branch[0], np.float32).reshape(2 * 512, D)
    wbr = np.ascontiguousarray(wbr.reshape(8, 128, D).transpose(1, 0, 2))
    wo = np.ascontiguousarray(np.asarray(w_out[0], np.float32).reshape(8, 128, D).transpose(1, 0, 2))
    prm = np.zeros((128, NPRM), np.float32)
    prm[:, 0:8] = np.asarray(g_pre[0], np.float32).reshape(8, 128).T
    prm[:, 8:16] = np.asarray(b_forget[0], np.float32)[None, :]
    prm[:, 16:80] = np.asarray(lq1[0], np.float32)[None, :]
    prm[:, 80:144] = np.asarray(lk1[0], np.float32)[None, :]
    prm[:, 144:208] = np.asarray(lq2[0], np.float32)[None, :]
    prm[:, 208:272] = np.asarray(lk2[0], np.float32)[None, :]
    prm[:, 272] = np.asarray(g_subln[0], np.float32)
    prm[:, 274:274 + D] = np.asarray(g_post[0], np.float32)[None, :]
    return w2, wbr, wo, prm


_NC_CACHE = {}


def kernel(x, g_pre, w_in, b_forget, lambda_q1, lambda_k1, lambda_q2, lambda_k2, g_subln, w_branch, w_out,
           g_post):
    x = np.asarray(x, np.float32)
    B, S, _ = x.shape
    w2, wbr, wo, prm = _layout_weights(g_pre, w_in, b_forget, lambda_q1, lambda_k1, lambda_q2, lambda_k2,
                                       g_subln, w_branch, w_out, g_post)
    cb, cf, ropec, ropes = _consts(S)
    if S not in _NC_CACHE:
        _NC_CACHE[S] = build(S)
    nc = _NC_CACHE[S]
    in_maps = [dict(x=np.ascontiguousarray(x[b]), w2=w2, wbr=wbr, wout=wo, prm=prm, cb=cb, cf=cf,
                    ropec=ropec, ropes=ropes) for b in range(B)]
    res = run_bass_kernel_spmd(nc, in_maps, core_ids=list(range(B)))
    return np.stack([np.asarray(r["out"], np.float32) for r in res.results], axis=0)
```
